# Optimizing a Trainium2 kernel written in Bass

```python
import jax, jax.numpy as jnp
from jax import lax
import numpy as np

D_MODEL = 1024
BATCH = 8
SEQ = 4096
DEPTH = 2

HEAD_DIM = 64
HG_HEADS = 4
HG_WIDTH = HG_HEADS * HEAD_DIM
HG_CHUNK = 64
ATT_HEADS = 8
ATT_KV_HEADS = 2
ATT_WIDTH = ATT_HEADS * HEAD_DIM
ATT_KV_WIDTH = ATT_KV_HEADS * HEAD_DIM
WINDOW = 128
ROPE_THETA = 10000.0
SG_GROUPS = 4
SG_WIDTH = SG_GROUPS * HEAD_DIM
SG_CHUNK = 128

MIX_WIDTH = HG_WIDTH + ATT_WIDTH + SG_WIDTH
IN_SPLITS = (HG_WIDTH, HG_WIDTH, HG_WIDTH, HG_WIDTH,
             ATT_WIDTH, ATT_KV_WIDTH, ATT_KV_WIDTH,
             SG_WIDTH, SG_WIDTH)
IN_WIDTH = sum(IN_SPLITS)
D_FF = 2816
PLE_DIM = 256
N_NORMS = 8
EPS = 1e-6
MASK_VALUE = -1e30
LB_FLOOR = 1e-30

kernel_name = "hybrid_hgrn2_swa_sink_sgu_macaron"


def rmsnorm(x, g):
    xf = x.astype(jnp.float32)
    y = xf * lax.rsqrt(jnp.mean(xf * xf, axis=-1, keepdims=True) + EPS)
    return (y * g.astype(jnp.float32)).astype(x.dtype)


def layernorm(x, g):
    xf = x.astype(jnp.float32)
    mu = jnp.mean(xf, axis=-1, keepdims=True)
    xc = xf - mu
    y = xc * lax.rsqrt(jnp.mean(xc * xc, axis=-1, keepdims=True) + EPS)
    return (y * g.astype(jnp.float32)).astype(x.dtype)


def swiglu(x, w_gu, w_down):
    gate, up = jnp.split(x @ w_gu, 2, axis=-1)
    return (jax.nn.silu(gate) * up) @ w_down


def rope(x, pos):
    half = x.shape[-1] // 2
    inv = ROPE_THETA ** (-jnp.arange(half, dtype=jnp.float32) / half)
    ang = pos.astype(jnp.float32)[..., None] * inv
    cos = jnp.cos(ang)[:, :, None, :]
    sin = jnp.sin(ang)[:, :, None, :]
    xf = x.astype(jnp.float32)
    x1, x2 = xf[..., :half], xf[..., half:]
    return jnp.concatenate([x1 * cos - x2 * sin, x2 * cos + x1 * sin], axis=-1).astype(x.dtype)


def hgrn2(q, f_raw, i, lb):
    B, S, _ = q.shape
    C = HG_CHUNK
    n = S // C
    q = q.astype(jnp.float32)
    v = i.astype(jnp.float32)
    lb = lb.astype(jnp.float32)
    logf = jnp.logaddexp(jnp.log(jnp.maximum(lb, LB_FLOOR)),
                         jnp.log1p(-lb) + jax.nn.log_sigmoid(f_raw.astype(jnp.float32)))
    k = -jnp.expm1(logf)

    def heads(t):
        return t.reshape(B, n, C, HG_HEADS, HEAD_DIM).transpose(1, 0, 3, 2, 4)

    causal = jnp.tril(jnp.ones((C, C), dtype=bool))[:, :, None]

    def step(state, inp):
        qc, kc, vc, lc = inp
        b = jnp.cumsum(lc, axis=2)
        o_inter = jnp.einsum('bhcd,bhde->bhce', qc * jnp.exp(b), state)
        diff = b[:, :, :, None, :] - b[:, :, None, :, :]
        dec = jnp.where(causal, jnp.exp(jnp.where(causal, diff, 0.0)), 0.0)
        att = jnp.einsum('bhid,bhjd,bhijd->bhij', qc, kc, dec)
        o_intra = jnp.einsum('bhij,bhje->bhie', att, vc)
        b_last = b[:, :, -1:, :]
        new_state = (jnp.exp(b_last[:, :, 0, :])[..., None] * state
                     + jnp.einsum('bhcd,bhce->bhde', kc * jnp.exp(b_last - b), vc))
        return new_state, o_inter + o_intra

    s0 = jnp.zeros((B, HG_HEADS, HEAD_DIM, HEAD_DIM), jnp.float32)
    _, o = lax.scan(step, s0, (heads(q), heads(k), heads(v), heads(logf)))
    return o.transpose(1, 0, 3, 2, 4).reshape(B, S, HG_HEADS, HEAD_DIM)


def swa_sink_attention(q, k, v, sinks, pos):
    B, S, _ = q.shape
    G = ATT_HEADS // ATT_KV_HEADS
    nb = S // WINDOW
    q = rope(q.reshape(B, S, ATT_HEADS, HEAD_DIM), pos)
    k = rope(k.reshape(B, S, ATT_KV_HEADS, HEAD_DIM), pos)
    v = v.reshape(B, S, ATT_KV_HEADS, HEAD_DIM)
    qb = q.reshape(B, nb, WINDOW, ATT_KV_HEADS, G, HEAD_DIM)

    def with_prev(t):
        t = t.reshape(B, nb, WINDOW, ATT_KV_HEADS, HEAD_DIM)
        prev = jnp.concatenate([jnp.zeros_like(t[:, :1]), t[:, :-1]], axis=1)
        return jnp.concatenate([prev, t], axis=2)

    kk, vv = with_prev(k), with_prev(v)
    s = jnp.einsum('bnqkgd,bnskd->bnkgqs', qb, kk).astype(jnp.float32) * (HEAD_DIM ** -0.5)
    qi = jnp.arange(WINDOW)[:, None]
    sj = jnp.arange(2 * WINDOW)[None, :]
    rel = qi + WINDOW - sj
    band = (rel >= 0) & (rel < WINDOW)
    exists = (jnp.arange(nb)[:, None, None] * WINDOW + sj[None] - WINDOW) >= 0
    mask = (band[None] & exists)[None, :, None, None]
    s = jnp.where(mask, s, MASK_VALUE)
    sink = sinks.astype(jnp.float32).reshape(1, 1, ATT_KV_HEADS, G, 1, 1)
    m = jnp.maximum(jnp.max(s, axis=-1, keepdims=True), sink)
    pr = jnp.where(mask, jnp.exp(s - m), 0.0)
    denom = jnp.sum(pr, axis=-1, keepdims=True) + jnp.exp(sink - m)
    o = jnp.einsum('bnkgqs,bnskd->bnqkgd', (pr / denom).astype(v.dtype), vv)
    return o.reshape(B, S, ATT_WIDTH)


def spatial_gating(u, v, ln_g, w_s, b_s):
    B, S, _ = u.shape
    nc = S // SG_CHUNK
    u = jax.nn.gelu(u)
    v = layernorm(jax.nn.gelu(v), ln_g)
    vg = v.reshape(B, nc, SG_CHUNK, SG_GROUPS, HEAD_DIM)
    w = w_s * jnp.tril(jnp.ones((SG_CHUNK, SG_CHUNK), w_s.dtype))
    mix = jnp.einsum('gts,bnsgd->bntgd', w, vg) + b_s.T[:, :, None]
    return (u.reshape(B, nc, SG_CHUNK, SG_GROUPS, HEAD_DIM) * mix).reshape(B, S, SG_WIDTH)


def hybrid_layer(h, p_i, pos, lb, norm_g, w_in, w_out, ffn1_gu, ffn1_down, ffn2_gu, ffn2_down,
                 hg_norm_g, sinks, sg_ln_g, sg_w, sg_b, ple_proj, ple_gate):
    B, S, _ = h.shape
    h = h + 0.5 * rmsnorm(swiglu(rmsnorm(h, norm_g[0]), ffn1_gu, ffn1_down), norm_g[1])
    z = rmsnorm(h, norm_g[2]) @ w_in
    idx = [int(c) for c in np.cumsum(IN_SPLITS)[:-1]]
    hq, hf, hi, hg, aq, ak, av, su, sv = jnp.split(z, idx, axis=-1)
    o_a = hgrn2(hq, hf, hi, lb)
    o_a = rmsnorm(o_a, hg_norm_g) * jax.nn.sigmoid(hg.astype(jnp.float32).reshape(B, S, HG_HEADS, HEAD_DIM))
    o_a = o_a.reshape(B, S, HG_WIDTH).astype(h.dtype)
    o_b = swa_sink_attention(aq, ak, av, sinks, pos).astype(h.dtype)
    o_c = spatial_gating(su, sv, sg_ln_g, sg_w, sg_b).astype(h.dtype)
    mixed = jnp.concatenate([o_a, o_b, o_c], axis=-1) @ w_out
    h = h + rmsnorm(mixed, norm_g[3])
    h = h + 0.5 * rmsnorm(swiglu(rmsnorm(h, norm_g[4]), ffn2_gu, ffn2_down), norm_g[5])
    gate = jax.nn.sigmoid(rmsnorm(h, norm_g[6]) @ ple_gate)
    h = h + rmsnorm((p_i @ ple_proj) * gate, norm_g[7])
    return h


def setup_inputs(seed: int = 0) -> dict:
    key = jax.random.key(seed)
    ks = jax.random.split(key, 20)

    def nrm(k, shape, scale):
        return jax.random.normal(k, shape, jnp.float32) * scale

    x = nrm(ks[0], (BATCH, SEQ, D_MODEL), 1.0)
    p = nrm(ks[1], (DEPTH, BATCH, SEQ, PLE_DIM), 1.0)
    positions = (jnp.arange(SEQ, dtype=jnp.int32)[None, :]
                 + jax.random.randint(ks[2], (BATCH, 1), 0, 1024, dtype=jnp.int32))
    return {
        "x": x,
        "p": p,
        "positions": positions,
        "norm_gains": 1.0 + nrm(ks[3], (DEPTH, N_NORMS, D_MODEL), 0.05),
        "w_in": nrm(ks[4], (DEPTH, D_MODEL, IN_WIDTH), D_MODEL ** -0.5),
        "w_out": nrm(ks[5], (DEPTH, MIX_WIDTH, D_MODEL), MIX_WIDTH ** -0.5),
        "ffn1_gate_up": nrm(ks[6], (DEPTH, D_MODEL, 2 * D_FF), D_MODEL ** -0.5),
        "ffn1_down": nrm(ks[7], (DEPTH, D_FF, D_MODEL), D_FF ** -0.5),
        "ffn2_gate_up": nrm(ks[8], (DEPTH, D_MODEL, 2 * D_FF), D_MODEL ** -0.5),
        "ffn2_down": nrm(ks[9], (DEPTH, D_FF, D_MODEL), D_FF ** -0.5),
        "hgrn_lb_logits": nrm(ks[10], (DEPTH, HG_WIDTH), 0.5),
        "hgrn_norm_gain": 1.0 + nrm(ks[11], (DEPTH, HEAD_DIM), 0.05),
        "attn_sinks": nrm(ks[12], (DEPTH, ATT_HEADS), 0.5),
        "sg_ln_gain": 1.0 + nrm(ks[13], (DEPTH, SG_WIDTH), 0.05),
        "sg_spatial_w": nrm(ks[14], (DEPTH, SG_GROUPS, SG_CHUNK, SG_CHUNK), SG_CHUNK ** -0.5),
        "sg_spatial_b": 1.0 + nrm(ks[15], (DEPTH, SG_GROUPS, SG_CHUNK), 0.1),
        "ple_proj": nrm(ks[16], (DEPTH, PLE_DIM, D_MODEL), PLE_DIM ** -0.5),
        "ple_gate": nrm(ks[17], (DEPTH, D_MODEL, D_MODEL), D_MODEL ** -0.5),
    }


def reference(x, p, positions, norm_gains, w_in, w_out, ffn1_gate_up, ffn1_down, ffn2_gate_up,
              ffn2_down, hgrn_lb_logits, hgrn_norm_gain, attn_sinks, sg_ln_gain, sg_spatial_w,
              sg_spatial_b, ple_proj, ple_gate):
    probs = jax.nn.softmax(hgrn_lb_logits.astype(jnp.float32), axis=0)
    lower_bounds = jnp.cumsum(probs, axis=0) - probs[0]
    h = x
    for l in range(DEPTH):
        h = hybrid_layer(h, p[l], positions, lower_bounds[l], norm_gains[l], w_in[l], w_out[l],
                         ffn1_gate_up[l], ffn1_down[l], ffn2_gate_up[l], ffn2_down[l],
                         hgrn_norm_gain[l], attn_sinks[l], sg_ln_gain[l], sg_spatial_w[l],
                         sg_spatial_b[l], ple_proj[l], ple_gate[l])
    return h
```

```python
import numpy as np
import os
MIXSTOP = float(os.environ.get('MIXSTOP', '99'))
from contextlib import ExitStack
import concourse.bass as bass
import concourse.mybir as mybir
from concourse.bass_utils import run_bass_kernel_spmd

F32 = mybir.dt.float32
BF16 = mybir.dt.bfloat16
I32 = mybir.dt.int32
AF = mybir.ActivationFunctionType
ALU = mybir.AluOpType
AX = mybir.AxisListType

EPOCH = int(os.environ.get('EPOCH', '2000'))
DMA_EPOCH = 100
NCORES = 8
SEQ = 4096
D = 1024
DFF = 2816
NF = 22
INW = 2304
EPS = 1e-6
G = 512
NSUB = 4
GELU_C = 2.0 * 0.7978845608028654


class Res:
    __slots__ = ("name", "w", "r")

    def __init__(self, name):
        self.name = name
        self.w = None
        self.r = {}


class EngW:
    def __init__(self, fw, name, is_pe=False):
        self.fw = fw
        self.name = name
        self.is_pe = is_pe
        self.sems = []
        self.count = 0
        self.seen = {}
        self.pending = []
        self.q = []

    def next_token(self):
        e = self.count // EPOCH
        while len(self.sems) <= e:
            self.sems.append(self.fw.new_sem(f"{self.name}_e{len(self.sems)}"))
        tok = (self.sems[e], (self.count % EPOCH) + 1)
        self.count += 1
        return tok


class FW:
    def __init__(self, nc, stack):
        self.nc = nc
        self.stack = stack
        self.pe = EngW(self, "pe", is_pe=True)
        self.act = EngW(self, "act")
        self.dve = EngW(self, "dve")
        self.pool = EngW(self, "pool")
        self.sp = EngW(self, "sp")
        self.dma_sems = {}

    def new_sem(self, name):
        return self.stack.enter_context(self.nc.semaphore(name))

    def sb(self, name, shape, dt):
        return self.stack.enter_context(self.nc.sbuf_tensor(name, list(shape), dt))

    def ps(self, name, shape, dt):
        return self.stack.enter_context(self.nc.psum_tensor(name, list(shape), dt))

    def _deps(self, E, reads, writes, skip_pending=False):
        deps = {}

        def need(tok):
            if tok is None:
                return
            if tok == "PENDING":
                if skip_pending:
                    return
                raise RuntimeError("dependency on a pending PE op")
            sem, val = tok
            if deps.get(id(sem), (None, 0))[1] < val:
                deps[id(sem)] = (sem, val)

        for r in reads:
            need(r.w)
        for r in writes:
            need(r.w)
            for tok in r.r.values():
                need(tok)
        own = set(id(s) for s in E.sems) if E.is_pe else ()
        for k, (sem, val) in deps.items():
            if k in own:
                continue
            if E.seen.get(k, 0) < val:
                E.q.append(("w", sem, val))
                E.seen[k] = val

    def _commit(self, key, tok, reads, writes):
        for r in reads:
            r.r[key] = tok
        for r in writes:
            r.w = tok
            r.r = {}

    def op(self, E, name, *args, reads=(), writes=(), inc=True, **kw):
        self._deps(E, reads, writes, skip_pending=E.is_pe)
        if inc:
            tok = E.next_token()
            E.q.append(("i", name, args, kw, tok[0], 1))
            if E.pending:
                for (rs, ws) in E.pending:
                    self._commit(E.name, tok, rs, ws)
                E.pending = []
            self._commit(E.name, tok, reads, writes)
        else:
            assert E.is_pe
            E.q.append(("i", name, args, kw, None, 0))
            E.pending.append((list(reads), list(writes)))
            for r in reads:
                r.r[E.name] = "PENDING"
            for r in writes:
                r.w = "PENDING"
                r.r = {}

    def dma(self, Q, out, in_, reads=(), writes=(), key="dma", **kw):
        self._deps(Q, reads, writes)
        ent = self.dma_sems.get(key)
        if ent is None or ent[1] >= 16 * DMA_EPOCH:
            self.n_dma_sem = getattr(self, "n_dma_sem", 0) + 1
            ent = [self.new_sem(f"dma_{key}_{self.n_dma_sem}"), 0]
            self.dma_sems[key] = ent
            if key == "st":
                self.st_sems = getattr(self, "st_sems", []) + [ent]
        ent[1] += 16
        kw = dict(kw)
        kw["out"] = out
        kw["in_"] = in_
        Q.q.append(("i", "dma_start", (), kw, ent[0], 16))
        tok = (ent[0], ent[1])
        self._commit("dma_" + key, tok, reads, writes)
        return tok

    def replay(self, block):
        def run(E):
            def body(eng):
                for ent in E.q:
                    if ent[0] == "w":
                        eng.wait_ge(ent[1], ent[2])
                    else:
                        _, name, args, kw, sem, incv = ent
                        inst = getattr(eng, name)(*args, **kw)
                        if sem is not None:
                            inst.then_inc(sem, incv)
            return body
        block.tensor(run(self.pe))
        block.scalar(run(self.act))
        block.vector(run(self.dve))
        block.gpsimd(run(self.pool))
        block.sync(run(self.sp))


class Rot:
    def __init__(self, fw, name, shape, dt, n):
        self.t = [fw.sb(f"{name}{i}", shape, dt) for i in range(n)]
        self.r = [Res(f"{name}{i}") for i in range(n)]
        self.i = 0

    def next(self):
        k = self.i % len(self.t)
        self.i += 1
        return self.t[k], self.r[k]


C_IDENT, C_LT, C_MHG, C_MCUR, C_MPREV, C_TRIL, C_SEL, C_RMASK, C_INV = 0, 128, 256, 384, 512, 640, 768, 772, 774
C_TOT = 806


def host_consts():
    c = np.zeros((128, C_TOT), np.float32)
    i = np.arange(128)
    c[:, C_IDENT:C_IDENT + 128] = np.eye(128)
    J, I = np.meshgrid(i, i, indexing="ij")
    same = (J // 64) == (I // 64)
    mid = 64 * (I // 64) + 31
    c[:, C_LT:C_LT + 128] = same * ((J <= I).astype(np.float32) - (J <= mid).astype(np.float32))
    c[:, C_MHG:C_MHG + 128] = same & (J <= I)
    c[:, C_MCUR:C_MCUR + 128] = (J <= I)
    c[:, C_MPREV:C_MPREV + 128] = (J > I)
    c[:, C_TRIL:C_TRIL + 128] = (I <= J)
    for ch in range(2):
        inch = (i // 64) == ch
        m = 64 * ch + 31
        c[:, C_SEL + 2 * ch] = inch & (i <= m)
        c[:, C_SEL + 2 * ch + 1] = inch & (i > m)
        c[:, C_RMASK + ch] = inch
    inv = (10000.0 ** (-np.arange(32, dtype=np.float32) / 32)).astype(np.float32)
    c[:, C_INV:C_INV + 32] = inv[None, :]
    return c


def build(n_st=8, n_layers=2, n_phases=4):
    nc = bass.Bass("TRN2", target_bir_lowering=False)

    def din(name, shape, dt=F32):
        return nc.dram_tensor(name, list(shape), dt, kind="ExternalInput").ap()

    x_d = din("x", [SEQ, D])
    p_d = din("p", [2, SEQ, 256])
    pos_d = din("positions", [SEQ], I32)
    ng_d = din("norm_gains", [2, 8, D])
    W_d = {
        "win": din("w_in", [2, D, INW]), "wout": din("w_out", [2, D, D]),
        "gu1": din("ffn1_gate_up", [2, D, 2 * DFF]), "d1": din("ffn1_down", [2, DFF, D]),
        "gu2": din("ffn2_gate_up", [2, D, 2 * DFF]), "d2": din("ffn2_down", [2, DFF, D]),
        "ple": din("ple_proj", [2, 256, D]), "gate": din("ple_gate", [2, D, D]),
    }
    lbl_d = din("hgrn_lb_logits", [2, 256])
    hgn_d = din("hgrn_norm_gain", [2, 64])
    snk_d = din("attn_sinks", [2, 8])
    lng_d = din("sg_ln_gain", [2, 256])
    sgw_d = din("sg_spatial_w", [2, 4, 128, 128])
    sgb_d = din("sg_spatial_b", [2, 4, 128])
    cst_d = din("consts", [128, C_TOT])
    out_d = nc.dram_tensor("out", [SEQ, D], F32, kind="ExternalOutput").ap()

    S = {}
    RS = {}
    for k, ap in W_d.items():
        for l in range(2):
            shp = list(ap.shape[1:])
            S[k, l] = nc.dram_tensor(f"s_{k}{l}", shp, BF16, kind="Internal").ap()
            RS[k, l] = Res(f"s_{k}{l}")

    with ExitStack() as st_:
        fw = FW(nc, st_)
        pe, act, dve, pool, sp = fw.pe, fw.act, fw.dve, fw.pool, fw.sp
        op, dma = fw.op, fw.dma

        order = []
        for l in range(2):
            order += [("gu1", l), ("d1", l), ("win", l), ("wout", l), ("gu2", l), ("d2", l), ("gate", l), ("ple", l)]
        for (k, l) in order:
            if l >= n_layers:
                continue
            src = W_d[k][l]
            a, b = src.shape
            dma(pool, S[k, l].rearrange("a b -> (a b)").rearrange("(r c) -> r c", c=1024),
                src.rearrange("a b -> (a b)").rearrange("(r c) -> r c", c=1024),
                writes=[RS[k, l]], key="cast")

        def T(name, shape, dt):
            return fw.sb(name, shape, dt), Res(name)

        cst, Rcst = T("cst", [128, C_TOT], F32)
        identb, Rident = T("identb", [128, 128], BF16)
        LTb, RLT = T("LTb", [128, 128], BF16)
        selb, Rsel = T("selb", [128, 4], BF16)
        mcurb, Rmcur = T("mcurb", [128, 128], BF16)
        mprevb, Rmprev = T("mprevb", [128, 128], BF16)
        neghalf, Rneg = T("neghalf", [128, 8], F32)
        gT, RgT = T("gT", [128, 2, 8, 8], F32)
        lbm, Rlbm = T("lbm", [128, 2, 256], F32)
        oml, Roml = T("oml", [128, 2, 256], F32)
        hgn, Rhgn = T("hgn", [128, 2, 64], F32)
        esk, Resk = T("esk", [128, 2, 8], F32)
        lng, Rlng = T("lng", [128, 2, 256], F32)
        sgb, Rsgb = T("sgb", [128, 2, 4], F32)
        sgwT, RsgwT = T("sgwT", [128, 8, 128], BF16)
        posi, Rposi = T("posi", [128, 32], I32)
        posf, Rposf = T("posf", [128, 32], F32)

        h, _ = T("h", [128, NSUB, D], F32)
        Rh = [Res(f"h{s}") for s in range(NSUB)]
        xT, _ = T("xT", [128, 8, G], BF16)
        RxT = [Res(f"xT{s}") for s in range(NSUB)]
        mT, RmT = xT, RxT
        aT, _ = T("aT", [128, NF, G], BF16)
        RaT = [Res(f"aT{j}") for j in range(NF)]
        wgu = Rot(fw, "wgu", [128, 2, 8, 256], BF16, 2)
        wd = Rot(fw, "wd", [128, 2, D], BF16, 2)
        win_t, Rwin = T("win_t", [128, 8, INW], BF16)
        wout_v = aT[:, 0:16, :].rearrange("p (c a) t -> p c (a t)", a=2)
        wgate_v = win_t[:, :, 0:1024]
        wple_v = win_t[:, 0:2, 1024:2048]
        gbc = Rot(fw, "gbc", [128, D], F32, 1)
        psb = Rot(fw, "psb", [128, 256], F32, 2)
        junk, _ = T("junk", [128, D], BF16)

        psum = fw.ps("psum", [128, 8, 512], F32)
        PB = [Res(f"bank{i}") for i in range(8)]
        bank_ctr = [0]

        def nb():
            b = bank_ctr[0] % 8
            bank_ctr[0] += 1
            return b

        def nb2():
            if bank_ctr[0] % 2:
                bank_ctr[0] += 1
            b = bank_ctr[0] % 8
            bank_ctr[0] += 2
            return b

        ss_r = Rot(fw, "ss", [128, 8], F32, 4)
        ms_r = Rot(fw, "ms", [128, 8], F32, 4)
        rstd_r = Rot(fw, "rstd", [128, 8], F32, 4)
        xs_r = Rot(fw, "xs", [128, D], BF16, 2)
        silu_r = Rot(fw, "silu", [128, G], F32, 2)
        tmp_r = Rot(fw, "tmpf", [128, D], F32, 1)
        gate_r = Rot(fw, "gatef", [128, D], F32, 1)

        hq_sb, Rhq = T("hq_sb", [128, 256], F32)
        sgf, Rsgf = T("sgf", [128, 256], F32)
        sgate, Rsgate = T("sgate", [128, 256], F32)
        v_bf, Rv = T("v_bf", [128, 256], BF16)
        qk_sb, Rqk = T("qk_sb", [128, 10, 64], F32)
        ge_x, Rgex = T("ge_x", [128, 512], F32)
        ge_t, Rget = T("ge_t", [128, 512], F32)
        ge_s, Rges = ge_t, Rget
        ff, Rff = T("ff", [128, 256], F32)
        logf, Rlogf = ff, Rff
        kk, Rkk = T("kk", [128, 256], F32)
        lhi, Rlhi = T("lhi", [128, 256], BF16)
        llo, Rllo = T("llo", [128, 256], BF16)
        eb, Reb = T("eb", [128, 256], F32)
        enb, Renb = T("enb", [128, 256], F32)
        E_sb, RE = T("E_sb", [64, 4, 4], F32)
        qt, Rqt = T("qt", [128, 256], BF16)
        ktb, Rktb = T("ktb", [128, 256], BF16)
        kt0, Rkt0 = T("kt0", [128, 256], BF16)
        kt1, Rkt1 = T("kt1", [128, 256], BF16)
        qkT, RqkT = T("qkT", [64, 8, 128], BF16)
        qT0, RqT0 = T("qT0", [64, 4, 128], BF16)
        qT1, RqT1 = T("qT1", [64, 4, 128], BF16)
        attT, RattT = T("attT", [128, 4, 128], BF16)
        UE, RUE = T("UE", [64, 2, 4, 64], F32)
        Smf, RSmf = T("Smf", [64, 4, 64], F32)
        Smb, _ = T("Smb", [64, 2, 4, 64], BF16)
        RSmb = [Res("Smb0"), Res("Smb1")]
        St1, RSt1 = T("St1", [64, 4, 64], F32)
        o_sb, Ro = T("o_sb", [128, 256], F32)
        o_sq, Rosq = T("o_sq", [128, 256], F32)
        rtmp, Rrtmp = T("rtmp", [128, 10, 64], F32)
        rsw, Rrsw = T("rsw", [128, 10, 64], F32)
        qk_r, Rqkr = T("qk_r", [128, 10, 64], BF16)
        qT_sb, RqTs = T("qT_sb", [64, 8, 128], BF16)
        PTc, RPTc = T("PTc", [128, 2, 512], BF16)
        PTp, RPTp = T("PTp", [128, 2, 512], BF16)
        den, Rden = T("den", [128, 8], F32)
        rden, Rrden = T("rden", [128, 8], F32)
        st6, Rst6 = T("st6", [128, 6], F32)
        mv, Rmv = T("mv", [128, 2], F32)
        vn, Rvn = T("vn", [128, 256], F32)
        vnb, Rvnb = T("vnb", [128, 256], BF16)
        mixed, _ = T("mixed", [128, D], BF16)
        Rmix = [Res("mix_a"), Res("mix_b"), Res("mix_c")]
        p_bf, Rpbf = T("p_bf", [128, 256], BF16)
        pT, RpT = T("pT", [128, 2, 128], BF16)
        cos2, Rcos2 = T("cos2", [128, NSUB, 64], F32)
        sin2, Rsin2 = T("sin2", [128, NSUB, 2, 32], F32)
        rp_t, Rrpt = rtmp[:, 0:4, :], Rrtmp
        rp_m, Rrpm = rtmp[:, 4:8, :], Rrtmp
        rp_f, Rrpf = rsw[:, 0:4, :], Rrsw
        rp_i, Rrpi = rsw[:, 4:8, :].bitcast(I32), Rrsw

        Sf = [T(f"Sf{l}", [64, 4, 64], F32) for l in range(2)]
        kTb = [fw.sb(f"kTb{l}", [64, 2, 5, 128], BF16) for l in range(2)]
        RkTb = [[Res(f"kTb{l}_{i}") for i in range(5)] for l in range(2)]
        Vau = [fw.sb(f"Vau{l}", [128, 5, 2, 65], BF16) for l in range(2)]
        RVau = [[Res(f"Vau{l}_{i}") for i in range(5)] for l in range(2)]

        dma(sp, cst[:], cst_d, writes=[Rcst], key="ld")
        op(dve, "tensor_copy", identb[:], cst[:, C_IDENT:C_IDENT + 128], reads=[Rcst], writes=[Rident])
        op(dve, "tensor_copy", LTb[:], cst[:, C_LT:C_LT + 128], reads=[Rcst], writes=[RLT])
        op(dve, "tensor_copy", selb[:], cst[:, C_SEL:C_SEL + 4], reads=[Rcst], writes=[Rsel])
        op(dve, "tensor_copy", mcurb[:], cst[:, C_MCUR:C_MCUR + 128], reads=[Rcst], writes=[Rmcur])
        op(dve, "tensor_copy", mprevb[:], cst[:, C_MPREV:C_MPREV + 128], reads=[Rcst], writes=[Rmprev])
        op(pool, "memset", neghalf[:], -0.5, writes=[Rneg])
        for l in range(2):
            for n in range(8):
                dma(sp, gT[:, l, n, :], ng_d[l, n].rearrange("(c p) -> p c", p=128), writes=[RgT], key="ld",
                    allow_slow_non_contiguous=True)
        lgt, Rlgt = gate_r.t[0][:, 0:512].rearrange("p (a b) -> p a b", a=2), gate_r.r[0]
        dma(sp, lgt, lbl_d.partition_broadcast(128), writes=[Rlgt], key="ld")
        dma(sp, hgn[:], hgn_d.partition_broadcast(128), writes=[Rhgn], key="ld")
        dma(sp, esk[:], snk_d.partition_broadcast(128), writes=[Resk], key="ld")
        dma(sp, lng[:], lng_d.partition_broadcast(128), writes=[Rlng], key="ld")
        dma(sp, sgb[:], sgb_d.rearrange("l g t -> t l g"), writes=[Rsgb], key="ld", allow_slow_non_contiguous=True)
        dma(sp, posi[:], pos_d.rearrange("(n p) -> p n", p=128), writes=[Rposi], key="ld", allow_slow_non_contiguous=True)
        op(dve, "tensor_copy", posf[:], posi[:], reads=[Rposi], writes=[Rposf])
        op(act, "activation", esk[:], esk[:], AF.Exp, reads=[Resk], writes=[Resk])
        d01, Rd01 = ge_x[:, 0:256], Rgex
        p0, Rp0 = ge_x[:, 256:512], Rgex
        p1, Rp1 = ge_t[:, 0:256], Rget
        op(dve, "tensor_tensor", d01, lgt[:, 0, :], lgt[:, 1, :], ALU.subtract, reads=[Rlgt], writes=[Rd01])
        op(act, "activation", p0, d01, AF.Sigmoid, reads=[Rd01], writes=[Rp0])
        op(act, "activation", p1, d01, AF.Sigmoid, scale=-1.0, reads=[Rd01], writes=[Rp1])
        op(dve, "tensor_tensor", lbm[:, 0, :], p0, p0, ALU.subtract, reads=[Rp0], writes=[Rlbm])
        op(dve, "tensor_tensor", lbm[:, 1, :], p0, p1, ALU.add, reads=[Rp0, Rp1, Rlbm], writes=[Rlbm])
        op(dve, "tensor_tensor", lbm[:, 1, :], lbm[:, 1, :], p0, ALU.subtract, reads=[Rp0, Rlbm], writes=[Rlbm])
        op(dve, "tensor_scalar", oml[:], lbm[:], -1.0, 1.0, ALU.mult, ALU.add, reads=[Rlbm], writes=[Roml])
        op(dve, "tensor_scalar", lbm[:], lbm[:], 1e-30, None, ALU.max, reads=[Rlbm, Roml], writes=[Rlbm])
        sgw_f, Rsgwf = tmp_r.t[0][:].rearrange("p (a b) -> p a b", a=8), tmp_r.r[0]
        sgw_b, Rsgwb = xs_r.t[0][:].rearrange("p (a b) -> p a b", a=8), xs_r.r[0]
        dma(sp, sgw_f, sgw_d.rearrange("l g t s -> t (l g) s"), writes=[Rsgwf], key="ld")
        op(dve, "tensor_tensor", sgw_b, sgw_f, cst[:, C_TRIL:C_TRIL + 128].unsqueeze(1).to_broadcast([128, 8, 128]),
           ALU.mult, reads=[Rsgwf, Rcst], writes=[Rsgwb])
        b = nb()
        bT = psum[:, b, :].bitcast(BF16)
        for i in range(8):
            op(pe, "transpose", bT[:, i * 128:(i + 1) * 128], sgw_b[:, i, :], identb[:], reads=[Rsgwb, Rident],
               writes=[PB[b]], inc=(i == 7))
        op(dve, "tensor_copy", sgwT[:].rearrange("p a b -> p (a b)"), bT, reads=[PB[b]], writes=[RsgwT])
        for l in range(2):
            op(pool, "memset", Sf[l][0][:], 0.0, writes=[Sf[l][1]])
            op(pool, "memset", Vau[l][:], 1.0, writes=RVau[l])
            op(pool, "memset", kTb[l][:], 0.0, writes=RkTb[l])
        op(pool, "memset", qT0[:], 0.0, writes=[RqT0])
        op(pool, "memset", qT1[:], 0.0, writes=[RqT1])

        def rstd_from(ss_ap, k, n, eps):
            ms_t, Rms = ms_r.next()
            rs_t, Rrs = rstd_r.next()
            return ms_t, Rms, rs_t, Rrs

        def emit_rstd(ss_t, Rss, k, n, eps):
            ms_t, Rms = ms_r.next()
            rs_t, Rrs = rstd_r.next()
            op(pool, "tensor_scalar", ms_t[:, 0:k], ss_t[:, 0:k], 1.0 / n, eps, ALU.mult, ALU.add, reads=[Rss], writes=[Rms])
            op(pool, "tensor_tensor", rs_t[:, 0:k], ms_t[:, 0:k], neghalf[:, 0:k], ALU.pow, reads=[Rms, Rneg], writes=[Rrs])
            return rs_t, Rrs

        def prenorm_T(l, n):
            for s in range(NSUB):
                ss_t, Rss = ss_r.next()
                op(act, "activation", junk[:], h[:, s, :], AF.Square, accum_out=ss_t[:, 0:1], reads=[Rh[s]], writes=[Rss])
                rs_t, Rrs = emit_rstd(ss_t, Rss, 1, D, EPS)
                xs_t, Rxs = xs_r.next()
                op(dve, "tensor_scalar", xs_t[:], h[:, s, :], rs_t[:, 0:1], None, ALU.mult, reads=[Rh[s], Rrs], writes=[Rxs])
                b = nb()
                bT = psum[:, b, :].bitcast(BF16)
                for kc in range(8):
                    op(pe, "transpose", bT[:, kc * 128:(kc + 1) * 128], xs_t[:, kc * 128:(kc + 1) * 128], identb[:],
                       reads=[Rxs, Rident], writes=[PB[b]], inc=(kc == 7))
                op(dve, "tensor_tensor", xT[:, :, s * 128:(s + 1) * 128], bT.rearrange("p (c t) -> p c t", c=8),
                   gT[:, l, n, :].unsqueeze(2).to_broadcast([128, 8, 128]), ALU.mult,
                   reads=[PB[b], RgT], writes=[RxT[s]])

        def load_gbc(l, n):
            t, r = gbc.next()
            dma(sp, t[:], ng_d[l, n].partition_broadcast(128), writes=[r], key="gbc")
            return t, r

        def post_norm(y_ap, y_reads, g_t, Rg, factor, s):
            ss_t, Rss = ss_r.next()
            op(act, "activation", junk[:].rearrange("p (a b) -> p a b", a=2), y_ap, AF.Square, accum_out=ss_t[:, 0:1],
               reads=y_reads, writes=[Rss])
            rs_t, Rrs = emit_rstd(ss_t, Rss, 1, D, EPS)
            tmp_t, Rtmp = tmp_r.next()
            op(dve, "scalar_tensor_tensor", tmp_t[:].rearrange("p (a b) -> p a b", a=2), y_ap, rs_t[:, 0:1],
               g_t[:].rearrange("p (a b) -> p a b", a=2), ALU.mult, ALU.mult, reads=list(y_reads) + [Rrs, Rg], writes=[Rtmp])
            op(dve, "scalar_tensor_tensor", h[:, s, :], tmp_t[:], float(factor), h[:, s, :], ALU.mult, ALU.add,
               reads=[Rtmp, Rh[s]], writes=[Rh[s]])

        def ffn(l, which):
            n_pre, n_post = (0, 1) if which == 1 else (4, 5)
            sgu, Rsgu = S[f"gu{which}", l], RS[f"gu{which}", l]
            sdn, Rsdn = S[f"d{which}", l], RS[f"d{which}", l]
            g_t, Rg = load_gbc(l, n_post)
            prenorm_T(l, n_pre)

            def load_gu(g):
                t, r = wgu.next()
                c0 = g * 256
                dma(sp, t[:, 0, :, :], sgu[:, c0:c0 + 256].rearrange("(c p) n -> p c n", p=128), reads=[Rsgu], writes=[r], key="wgu")
                dma(sp, t[:, 1, :, :], sgu[:, DFF + c0:DFF + c0 + 256].rearrange("(c p) n -> p c n", p=128), reads=[Rsgu], writes=[r], key="wgu")
                return t, r

            nxt = load_gu(0)
            for g in range(11):
                w_t, Rw = nxt
                if g + 1 < 11:
                    nxt = load_gu(g + 1)
                for jj in range(2):
                    j = 2 * g + jj
                    bA = nb()
                    bB = nb()
                    for (gu, bk) in ((0, bA), (1, bB)):
                        for kc in range(8):
                            op(pe, "matmul", psum[:, bk, :], w_t[:, gu, kc, jj * 128:(jj + 1) * 128], xT[:, kc, :],
                               start=(kc == 0), stop=(kc == 7), reads=[Rw] + RxT, writes=[PB[bk]], inc=(kc == 7))
                    sg_t, Rsg = silu_r.next()
                    op(act, "activation", sg_t[:], psum[:, bA, :], AF.Silu, reads=[PB[bA]], writes=[Rsg])
                    op(dve, "tensor_tensor", aT[:, j, :], sg_t[:], psum[:, bB, :], ALU.mult, reads=[Rsg, PB[bB]], writes=[RaT[j]])

            def load_d(jg):
                t, r = wd.next()
                dma(sp, t[:], sdn[jg * 256:(jg + 1) * 256, :].rearrange("(j p) n -> p j n", p=128), reads=[Rsdn], writes=[r], key="wd")
                return t, r

            nxt = load_d(0)
            for jg in range(11):
                w_t, Rw = nxt
                if jg + 1 < 11:
                    nxt = load_d(jg + 1)
                for jj in range(2):
                    j = 2 * jg + jj
                    for s in range(NSUB):
                        for hh in range(2):
                            last = (jj == 1 and s == NSUB - 1 and hh == 1)
                            op(pe, "matmul", psum[:, 2 * s + hh, :], aT[:, j, s * 128:(s + 1) * 128], w_t[:, jj, hh * 512:(hh + 1) * 512],
                               start=(j == 0), stop=(j == NF - 1), reads=[RaT[j], Rw], writes=[PB[2 * s + hh]], inc=last)
            bank_ctr[0] = 0
            for s in range(NSUB):
                post_norm(psum[:, 2 * s:2 * s + 2, :], [PB[2 * s], PB[2 * s + 1]], g_t, Rg, 0.5, s)

        def rope_tables(st):
            for s in range(NSUB):
                n = st * NSUB + s
                op(dve, "tensor_scalar", rp_t[:, s, 0:32], cst[:, C_INV:C_INV + 32], posf[:, n:n + 1], 1.0 / (2 * np.pi),
                   ALU.mult, ALU.mult, reads=[Rcst, Rposf], writes=[Rrpt])
            op(dve, "tensor_scalar", rp_t[:, :, 32:64], rp_t[:, :, 0:32], 0.25, None, ALU.add, reads=[Rrpt], writes=[Rrpt])
            op(dve, "tensor_copy", rp_i, rp_t, reads=[Rrpt], writes=[Rrpi])
            op(dve, "tensor_copy", rp_f, rp_i, reads=[Rrpi], writes=[Rrpf])
            op(dve, "tensor_tensor", rp_t, rp_t, rp_f, ALU.subtract, reads=[Rrpt, Rrpf], writes=[Rrpt])
            op(dve, "tensor_single_scalar", rp_m, rp_t, 0.5, ALU.is_gt, reads=[Rrpt], writes=[Rrpm])
            op(dve, "tensor_tensor", rp_t, rp_t, rp_m, ALU.subtract, reads=[Rrpt, Rrpm], writes=[Rrpt])
            op(dve, "tensor_single_scalar", rp_m, rp_t, -0.5, ALU.is_lt, reads=[Rrpt], writes=[Rrpm])
            op(dve, "tensor_tensor", rp_t, rp_t, rp_m, ALU.add, reads=[Rrpt, Rrpm], writes=[Rrpt])
            op(act, "activation", rp_f, rp_t, AF.Sin, scale=6.28318, reads=[Rrpt], writes=[Rrpf])
            op(dve, "tensor_copy", cos2[:, :, 0:32], rp_f[:, :, 32:64], reads=[Rrpf], writes=[Rcos2])
            op(dve, "tensor_copy", cos2[:, :, 32:64], rp_f[:, :, 32:64], reads=[Rrpf, Rcos2], writes=[Rcos2])
            op(dve, "tensor_scalar", sin2[:, :, 0, :], rp_f[:, :, 0:32], -1.0, None, ALU.mult, reads=[Rrpf], writes=[Rsin2])
            op(dve, "tensor_copy", sin2[:, :, 1, :], rp_f[:, :, 0:32], reads=[Rrpf, Rsin2], writes=[Rsin2])

        def mixer(l, st):
            g_t, Rg = load_gbc(l, 3)
            prenorm_T(l, 2)
            Sf_t, RSf = Sf[l]
            for s in range(NSUB):
                nblk = st * NSUB + s
                sl_cur, sl_prev = s + 1, s
                cbs = [(0, 512), (512, 512), (1024, 512), (1536, 256), (1792, 512)]
                zb = []
                for (c0, w) in cbs:
                    b = nb()
                    zb.append(b)
                    for kc in range(8):
                        op(pe, "matmul", psum[:, b, 0:w], xT[:, kc, s * 128:(s + 1) * 128], win_t[:, kc, c0:c0 + w],
                           start=(kc == 0), stop=(kc == 7), reads=[RxT[s], Rwin], writes=[PB[b]], inc=(kc == 7))
                op(act, "activation", hq_sb[:], psum[:, zb[0], 0:256], AF.Copy, reads=[PB[zb[0]]], writes=[Rhq])
                op(act, "activation", v_bf[:], psum[:, zb[1], 0:256], AF.Copy, reads=[PB[zb[1]]], writes=[Rv])
                op(act, "activation", qk_sb[:, 0:8, :].rearrange("p a b -> p (a b)"), psum[:, zb[2], :], AF.Copy,
                   reads=[PB[zb[2]]], writes=[Rqk])
                op(act, "activation", qk_sb[:, 8:10, :].rearrange("p a b -> p (a b)"), psum[:, zb[3], 0:128], AF.Copy,
                   reads=[PB[zb[3]], Rqk], writes=[Rqk])
                op(act, "activation", Vau[l][:, sl_cur, :, 0:64], psum[:, zb[3], 128:256].rearrange("p (k d) -> p k d", k=2),
                   AF.Copy, reads=[PB[zb[3]]], writes=[RVau[l][sl_cur]])
                op(act, "activation", ge_x[:], psum[:, zb[4], :], AF.Copy, reads=[PB[zb[4]]], writes=[Rgex])
                op(act, "activation", sgf[:], psum[:, zb[0], 256:512], AF.Sigmoid, reads=[PB[zb[0]]], writes=[Rsgf])
                op(act, "activation", sgate[:], psum[:, zb[1], 256:512], AF.Sigmoid, reads=[PB[zb[1]]], writes=[Rsgate])
                op(dve, "tensor_tensor", ge_t[:], ge_x[:], ge_x[:], ALU.mult, reads=[Rgex], writes=[Rget])
                op(dve, "tensor_scalar", ge_t[:], ge_t[:], 0.044715, 1.0, ALU.mult, ALU.add, reads=[Rget], writes=[Rget])
                op(dve, "tensor_tensor", ge_t[:], ge_t[:], ge_x[:], ALU.mult, reads=[Rget, Rgex], writes=[Rget])
                op(act, "activation", ge_t[:], ge_t[:], AF.Sigmoid, scale=GELU_C, reads=[Rget], writes=[Rget])
                op(dve, "tensor_tensor", ge_x[:], ge_x[:], ge_t[:], ALU.mult, reads=[Rgex, Rget], writes=[Rgex])
                op(pool, "tensor_tensor", sgate[:].rearrange("p (a b) -> p a b", a=4), sgate[:].rearrange("p (a b) -> p a b", a=4),
                   hgn[:, l, :].unsqueeze(1).to_broadcast([128, 4, 64]), ALU.mult, reads=[Rsgate, Rhgn], writes=[Rsgate])

                if MIXSTOP <= 1:
                    continue
                op(dve, "tensor_tensor", ff[:], sgf[:], oml[:, l, :], ALU.mult, reads=[Rsgf, Roml], writes=[Rff])
                op(dve, "tensor_tensor", ff[:], ff[:], lbm[:, l, :], ALU.add, reads=[Rff, Rlbm], writes=[Rff])
                op(act, "activation", ff[:], ff[:], AF.Ln, reads=[Rff], writes=[Rff])
                op(dve, "tensor_scalar", kk[:], sgf[:], -1.0, 1.0, ALU.mult, ALU.add, reads=[Rsgf], writes=[Rkk])
                op(dve, "tensor_tensor", kk[:], kk[:], oml[:, l, :], ALU.mult, reads=[Rkk, Roml], writes=[Rkk])
                op(dve, "tensor_copy", lhi[:], logf[:], reads=[Rlogf], writes=[Rlhi])
                op(dve, "tensor_tensor", llo[:], logf[:], lhi[:], ALU.subtract, reads=[Rlogf, Rlhi], writes=[Rllo])
                bb = nb()
                op(pe, "matmul", psum[:, bb, 0:256], LTb[:], lhi[:], start=True, stop=False, reads=[RLT, Rlhi], writes=[PB[bb]], inc=False)
                op(pe, "matmul", psum[:, bb, 0:256], LTb[:], llo[:], start=False, stop=True, reads=[RLT, Rllo], writes=[PB[bb]], inc=True)
                be = nb()
                for hh in range(4):
                    op(pe, "matmul", psum[0:64, be, hh * 4:(hh + 1) * 4], lhi[:, hh * 64:(hh + 1) * 64], selb[:], start=True, stop=False,
                       reads=[Rlhi, Rsel], writes=[PB[be]], inc=False)
                    op(pe, "matmul", psum[0:64, be, hh * 4:(hh + 1) * 4], llo[:, hh * 64:(hh + 1) * 64], selb[:], start=False, stop=True,
                       reads=[Rllo, Rsel], writes=[PB[be]], inc=(hh == 3))
                op(act, "activation", eb[:], psum[:, bb, 0:256], AF.Exp, reads=[PB[bb]], writes=[Reb])
                op(act, "activation", enb[:], psum[:, bb, 0:256], AF.Exp, scale=-1.0, reads=[PB[bb]], writes=[Renb])
                op(act, "activation", E_sb[:].rearrange("p a b -> p (a b)"), psum[0:64, be, 0:16], AF.Exp, reads=[PB[be]], writes=[RE])
                op(dve, "tensor_tensor", qt[:], hq_sb[:], eb[:], ALU.mult, reads=[Rhq, Reb], writes=[Rqt])
                op(dve, "tensor_tensor", ktb[:], kk[:], enb[:], ALU.mult, reads=[Rkk, Renb], writes=[Rktb])
                op(dve, "tensor_scalar", kt0[:], ktb[:], cst[:, C_RMASK:C_RMASK + 1], None, ALU.mult, reads=[Rktb, Rcst], writes=[Rkt0])
                op(dve, "tensor_scalar", kt1[:], ktb[:], cst[:, C_RMASK + 1:C_RMASK + 2], None, ALU.mult, reads=[Rktb, Rcst], writes=[Rkt1])
                if MIXSTOP <= 2:
                    continue
                bt = nb()
                bT = psum[:, bt, :].bitcast(BF16)
                for hh in range(4):
                    op(pe, "transpose", bT[0:64, hh * 128:(hh + 1) * 128], qt[:, hh * 64:(hh + 1) * 64], identb[:],
                       reads=[Rqt, Rident], writes=[PB[bt]], inc=False)
                for hh in range(4):
                    op(pe, "transpose", bT[0:64, (4 + hh) * 128:(5 + hh) * 128], ktb[:, hh * 64:(hh + 1) * 64], identb[:],
                       reads=[Rktb, Rident], writes=[PB[bt]], inc=(hh == 3))
                if MIXSTOP <= 2.3:
                    continue
                op(act, "activation", qkT[:].rearrange("p a b -> p (a b)"), bT[0:64, :], AF.Copy, reads=[PB[bt]], writes=[RqkT])
                if MIXSTOP <= 2.35:
                    continue
                bT3 = bT[0:64, 0:512].rearrange("p (a b) -> p a b", a=4)
                op(pool, "tensor_copy", qT0[:, :, 0:64], qkT[:, 0:4, 0:64], reads=[RqkT], writes=[RqT0])
                if MIXSTOP <= 2.4:
                    continue
                op(pool, "tensor_copy", qT1[:, :, 64:128], qkT[:, 0:4, 64:128], reads=[RqkT], writes=[RqT1])
                if MIXSTOP <= 2.5:
                    continue
                ba = nb()
                for hh in range(4):
                    op(pe, "matmul", psum[:, ba, hh * 128:(hh + 1) * 128], qkT[:, 4 + hh, :], qkT[:, hh, :], start=True, stop=True,
                       reads=[RqkT], writes=[PB[ba]], inc=(hh == 3))
                if MIXSTOP <= 2.7:
                    continue
                op(dve, "tensor_tensor", attT[:], psum[:, ba, :].rearrange("p (a b) -> p a b", a=4),
                   cst[:, C_MHG:C_MHG + 128].unsqueeze(1).to_broadcast([128, 4, 128]), ALU.mult,
                   reads=[PB[ba], Rcst], writes=[RattT])
                if MIXSTOP <= 3:
                    continue
                bo = nb()
                for hh in range(4):
                    op(pe, "matmul", psum[:, bo, hh * 64:(hh + 1) * 64], attT[:, hh, :], v_bf[:, hh * 64:(hh + 1) * 64], start=True, stop=True,
                       reads=[RattT, Rv], writes=[PB[bo]], inc=(hh == 3))
                bu = nb()
                for c in range(2):
                    ktc, Rktc = (kt0, Rkt0) if c == 0 else (kt1, Rkt1)
                    for hh in range(4):
                        o0 = (c * 4 + hh) * 64
                        op(pe, "matmul", psum[0:64, bu, o0:o0 + 64], ktc[:, hh * 64:(hh + 1) * 64], v_bf[:, hh * 64:(hh + 1) * 64],
                           start=True, stop=True, reads=[Rktc, Rv], writes=[PB[bu]], inc=(c == 1 and hh == 3))
                for c in range(2):
                    op(dve, "tensor_tensor", UE[:, c, :, :], psum[0:64, bu, c * 256:(c + 1) * 256].rearrange("p (a b) -> p a b", a=4),
                       E_sb[:, :, 2 * c + 1:2 * c + 2].to_broadcast([64, 4, 64]), ALU.mult, reads=[PB[bu], RE], writes=[RUE])
                for c in range(2):
                    op(dve, "tensor_tensor", Smf[:], Sf_t[:], E_sb[:, :, 2 * c:2 * c + 1].to_broadcast([64, 4, 64]), ALU.mult,
                       reads=[RSf, RE], writes=[RSmf])
                    op(act, "activation", Smb[:, c, :, :], Smf[:], AF.Copy, reads=[RSmf], writes=[RSmb[c]])
                    op(dve, "tensor_tensor", St1[:], Smf[:], E_sb[:, :, 2 * c + 1:2 * c + 2].to_broadcast([64, 4, 64]), ALU.mult,
                       reads=[RSmf, RE], writes=[RSt1])
                    op(dve, "tensor_tensor", Sf_t[:], St1[:], UE[:, c, :, :], ALU.add, reads=[RSt1, RUE], writes=[RSf])
                bi = nb()
                for hh in range(4):
                    for c in range(2):
                        qTc, RqTc = (qT0, RqT0) if c == 0 else (qT1, RqT1)
                        op(pe, "matmul", psum[:, bi, hh * 64:(hh + 1) * 64], qTc[:, hh, :], Smb[:, c, hh, :], start=(c == 0), stop=(c == 1),
                           reads=[RqTc, RSmb[c]], writes=[PB[bi]], inc=(hh == 3 and c == 1))
                op(act, "activation", o_sb[:], psum[:, bo, 0:256], AF.Copy, reads=[PB[bo]], writes=[Ro])
                op(dve, "tensor_tensor", o_sb[:], o_sb[:], psum[:, bi, 0:256], ALU.add, reads=[Ro, PB[bi]], writes=[Ro])
                op(dve, "tensor_tensor", o_sq[:], o_sb[:], o_sb[:], ALU.mult, reads=[Ro], writes=[Rosq])
                ss_t, Rss = ss_r.next()
                op(dve, "tensor_reduce", ss_t[:, 0:4], o_sq[:].rearrange("p (a b) -> p a b", a=4), AX.X, ALU.add, reads=[Rosq], writes=[Rss])
                rs_t, Rrs = emit_rstd(ss_t, Rss, 4, 64, EPS)
                op(dve, "tensor_tensor", o_sq[:].rearrange("p (a b) -> p a b", a=4), o_sb[:].rearrange("p (a b) -> p a b", a=4),
                   rs_t[:, 0:4].unsqueeze(2).to_broadcast([128, 4, 64]), ALU.mult, reads=[Ro, Rrs, Rosq], writes=[Rosq])
                op(dve, "tensor_tensor", mixed[:, 0:256], o_sq[:], sgate[:], ALU.mult, reads=[Rosq, Rsgate], writes=[Rmix[0]])

                if MIXSTOP <= 4:
                    continue
                qk3 = qk_sb[:].rearrange("p h (t d) -> p h t d", t=2)
                op(dve, "tensor_tensor", rsw[:].rearrange("p h (t d) -> p h t d", t=2)[:, :, 0, :], qk3[:, :, 1, :],
                   sin2[:, s, 0, :].unsqueeze(1).to_broadcast([128, 10, 32]), ALU.mult, reads=[Rqk, Rsin2], writes=[Rrsw])
                op(dve, "tensor_tensor", rsw[:].rearrange("p h (t d) -> p h t d", t=2)[:, :, 1, :], qk3[:, :, 0, :],
                   sin2[:, s, 1, :].unsqueeze(1).to_broadcast([128, 10, 32]), ALU.mult, reads=[Rqk, Rsin2, Rrsw], writes=[Rrsw])
                op(dve, "tensor_tensor", rtmp[:], qk_sb[:], cos2[:, s, :].unsqueeze(1).to_broadcast([128, 10, 64]), ALU.mult,
                   reads=[Rqk, Rcos2], writes=[Rrtmp])
                op(dve, "tensor_tensor", qk_r[:], rtmp[:], rsw[:], ALU.add, reads=[Rrtmp, Rrsw], writes=[Rqkr])
                bq = nb()
                bqT = psum[:, bq, :].bitcast(BF16)
                for hh in range(8):
                    op(pe, "transpose", bqT[0:64, hh * 128:(hh + 1) * 128], qk_r[:, hh, :], identb[:], reads=[Rqkr, Rident],
                       writes=[PB[bq]], inc=(hh == 7))
                bk = nb()
                bkT = psum[:, bk, :].bitcast(BF16)
                for kv in range(2):
                    op(pe, "transpose", bkT[0:64, kv * 128:(kv + 1) * 128], qk_r[:, 8 + kv, :], identb[:], reads=[Rqkr, Rident],
                       writes=[PB[bk]], inc=(kv == 1))
                op(act, "activation", qT_sb[:].rearrange("p a b -> p (a b)"), bqT[0:64, :], AF.Copy, reads=[PB[bq]], writes=[RqTs])
                op(act, "activation", kTb[l][:, :, sl_cur, :], bkT[0:64, 0:256].rearrange("p (a b) -> p a b", a=2), AF.Copy,
                   reads=[PB[bk]], writes=[RkTb[l][sl_cur]])
                slots = [(sl_cur, PTc, RPTc, mcurb, Rmcur)]
                if nblk > 0:
                    slots.append((sl_prev, PTp, RPTp, mprevb, Rmprev))
                for kv in range(2):
                    for (sl, PT, RPT, mk, Rmk) in slots:
                        bs = nb()
                        op(pe, "matmul", psum[:, bs, :], kTb[l][:, kv, sl, :], qT_sb[:, 4 * kv:4 * kv + 4, :], start=True, stop=True,
                           reads=[RkTb[l][sl], RqTs], writes=[PB[bs]], inc=True)
                        op(act, "activation", PT[:, kv, :], psum[:, bs, :], AF.Exp, scale=0.125, reads=[PB[bs]], writes=[RPT])
                        op(dve, "tensor_tensor", PT[:, kv, :].rearrange("p (a b) -> p a b", a=4), PT[:, kv, :].rearrange("p (a b) -> p a b", a=4),
                           mk[:].unsqueeze(1).to_broadcast([128, 4, 128]), ALU.mult, reads=[RPT, Rmk], writes=[RPT])
                for kv in range(2):
                    bp = nb()
                    for g in range(4):
                        for si, (sl, PT, RPT, mk, Rmk) in enumerate(slots):
                            op(pe, "matmul", psum[:, bp, g * 65:(g + 1) * 65], PT[:, kv, g * 128:(g + 1) * 128], Vau[l][:, sl, kv, :],
                               start=(si == 0), stop=(si == len(slots) - 1), reads=[RPT, RVau[l][sl]], writes=[PB[bp]],
                               inc=(g == 3 and si == len(slots) - 1))
                    pv3 = psum[:, bp, 0:260].rearrange("p (a b) -> p a b", a=4)
                    op(dve, "tensor_tensor", den[:, 4 * kv:4 * kv + 4], pv3[:, :, 64], esk[:, l, 4 * kv:4 * kv + 4], ALU.add,
                       reads=[PB[bp], Resk], writes=[Rden])
                    op(dve, "reciprocal", rden[:, 4 * kv:4 * kv + 4], den[:, 4 * kv:4 * kv + 4], reads=[Rden], writes=[Rrden])
                    op(dve, "tensor_tensor", mixed[:, 256 + 256 * kv:512 + 256 * kv].rearrange("p (a b) -> p a b", a=4), pv3[:, :, 0:64],
                       rden[:, 4 * kv:4 * kv + 4].unsqueeze(2).to_broadcast([128, 4, 64]), ALU.mult,
                       reads=[PB[bp], Rrden], writes=[Rmix[1]])

                if MIXSTOP <= 5:
                    continue
                op(dve, "bn_stats", st6[:], ge_x[:, 256:512], reads=[Rgex], writes=[Rst6])
                op(dve, "bn_aggr", mv[:], st6[:], reads=[Rst6], writes=[Rmv])
                ms_t, Rms = ms_r.next()
                rs2_t, Rrs2 = rstd_r.next()
                op(pool, "tensor_scalar", ms_t[:, 0:1], mv[:, 1:2], 1.0, EPS, ALU.mult, ALU.add, reads=[Rmv], writes=[Rms])
                op(pool, "tensor_tensor", rs2_t[:, 0:1], ms_t[:, 0:1], neghalf[:, 0:1], ALU.pow, reads=[Rms, Rneg], writes=[Rrs2])
                op(dve, "tensor_scalar", vn[:], ge_x[:, 256:512], mv[:, 0:1], rs2_t[:, 0:1], ALU.subtract, ALU.mult,
                   reads=[Rgex, Rmv, Rrs2], writes=[Rvn])
                op(dve, "tensor_tensor", vnb[:], vn[:], lng[:, l, :], ALU.mult, reads=[Rvn, Rlng], writes=[Rvnb])
                bm = nb()
                for g in range(4):
                    op(pe, "matmul", psum[:, bm, g * 64:(g + 1) * 64], sgwT[:, l * 4 + g, :], vnb[:, g * 64:(g + 1) * 64], start=True, stop=True,
                       reads=[RsgwT, Rvnb], writes=[PB[bm]], inc=(g == 3))
                op(dve, "tensor_tensor", vn[:].rearrange("p (a b) -> p a b", a=4), psum[:, bm, 0:256].rearrange("p (a b) -> p a b", a=4),
                   sgb[:, l, :].unsqueeze(2).to_broadcast([128, 4, 64]), ALU.add, reads=[PB[bm], Rsgb, Rvn], writes=[Rvn])
                op(dve, "tensor_tensor", mixed[:, 768:1024], vn[:], ge_x[:, 0:256], ALU.mult, reads=[Rvn, Rgex], writes=[Rmix[2]])

                if MIXSTOP <= 6:
                    continue
                bx = nb()
                bxT = psum[:, bx, :].bitcast(BF16)
                for kc in range(8):
                    op(pe, "transpose", bxT[:, kc * 128:(kc + 1) * 128], mixed[:, kc * 128:(kc + 1) * 128], identb[:],
                       reads=Rmix + [Rident], writes=[PB[bx]], inc=(kc == 7))
                op(act, "activation", mT[:, :, s * 128:(s + 1) * 128], bxT.rearrange("p (c t) -> p c t", c=8), AF.Copy,
                   reads=[PB[bx]], writes=[RmT[s]])
                b2 = nb2()
                for hh in range(2):
                    for kc in range(8):
                        op(pe, "matmul", psum[:, b2 + hh, :], mT[:, kc, s * 128:(s + 1) * 128], wout_v[:, kc, hh * 512:(hh + 1) * 512],
                           start=(kc == 0), stop=(kc == 7), reads=[RmT[s]] + RaT, writes=[PB[b2 + hh]], inc=(kc == 7))
                post_norm(psum[:, b2:b2 + 2, :], [PB[b2], PB[b2 + 1]], g_t, Rg, 1.0, s)
            op(pool, "tensor_copy", kTb[l][:, :, 0, :], kTb[l][:, :, 4, :], reads=[RkTb[l][4]], writes=[RkTb[l][0]])
            op(pool, "tensor_copy", Vau[l][:, 0, :, :], Vau[l][:, 4, :, :], reads=[RVau[l][4]], writes=[RVau[l][0]])

        def ple(l, st):
            g_t, Rg = load_gbc(l, 7)
            prenorm_T(l, 6)
            for s in range(NSUB):
                b2 = nb2()
                for hh in range(2):
                    for kc in range(8):
                        op(pe, "matmul", psum[:, b2 + hh, :], xT[:, kc, s * 128:(s + 1) * 128], wgate_v[:, kc, hh * 512:(hh + 1) * 512],
                           start=(kc == 0), stop=(kc == 7), reads=[RxT[s], Rwin], writes=[PB[b2 + hh]], inc=(kc == 7))
                ga_t, Rga = gate_r.next()
                op(act, "activation", ga_t[:].rearrange("p (a b) -> p a b", a=2), psum[:, b2:b2 + 2, :], AF.Sigmoid,
                   reads=[PB[b2], PB[b2 + 1]], writes=[Rga])
                ps_t, Rps = psb.next()
                dma(sp, ps_t[:], p_d[l, st * G + s * 128:st * G + (s + 1) * 128, :], writes=[Rps], key="ldx")
                op(dve, "tensor_copy", p_bf[:], ps_t[:], reads=[Rps], writes=[Rpbf])
                bt = nb()
                bT = psum[:, bt, :].bitcast(BF16)
                for kc in range(2):
                    op(pe, "transpose", bT[:, kc * 128:(kc + 1) * 128], p_bf[:, kc * 128:(kc + 1) * 128], identb[:],
                       reads=[Rpbf, Rident], writes=[PB[bt]], inc=(kc == 1))
                op(act, "activation", pT[:].rearrange("p a b -> p (a b)"), bT[:, 0:256], AF.Copy, reads=[PB[bt]], writes=[RpT])
                b3 = nb2()
                for hh in range(2):
                    for kc in range(2):
                        op(pe, "matmul", psum[:, b3 + hh, :], pT[:, kc, :], wple_v[:, kc, hh * 512:(hh + 1) * 512],
                           start=(kc == 0), stop=(kc == 1), reads=[RpT, Rwin], writes=[PB[b3 + hh]], inc=(kc == 1))
                op(dve, "tensor_tensor", ga_t[:].rearrange("p (a b) -> p a b", a=2), psum[:, b3:b3 + 2, :],
                   ga_t[:].rearrange("p (a b) -> p a b", a=2), ALU.mult, reads=[PB[b3], PB[b3 + 1], Rga], writes=[Rga])
                post_norm(ga_t[:].rearrange("p (a b) -> p a b", a=2), [Rga], g_t, Rg, 1.0, s)

        phase_ctr = 0
        for st in range(n_st):
            t0 = st * G
            dma(sp, h[:], x_d[t0:t0 + G, :].rearrange("(s p) d -> p s d", p=128), writes=Rh, key="ldx")
            rope_tables(st)
            for l in range(n_layers):
                np_ = n_phases if l == n_layers - 1 else 4
                dma(sp, win_t[:], S["win", l].rearrange("(c p) n -> p c n", p=128), reads=[RS["win", l]], writes=[Rwin], key="wres")
                if np_ >= 1:
                    ffn(l, 1)
                if np_ >= 2:
                    dma(sp, wout_v, S["wout", l].rearrange("(c p) n -> p c n", p=128), reads=[RS["wout", l]], writes=RaT, key="wres")
                    mixer(l, st)
                if np_ >= 3:
                    dma(sp, wgate_v, S["gate", l].rearrange("(c p) n -> p c n", p=128), reads=[RS["gate", l]], writes=[Rwin], key="wres")
                    dma(sp, wple_v, S["ple", l].rearrange("(c p) n -> p c n", p=128), reads=[RS["ple", l]], writes=[Rwin], key="wres")
                    ffn(l, 2)
                if np_ >= 4:
                    ple(l, st)
            dma(sp, out_d[t0:t0 + G, :].rearrange("(s p) d -> p s d", p=128), h[:], reads=Rh, key="st")

        for ent in fw.st_sems:
            sp.q.append(("w", ent[0], ent[1]))
        with nc.Block() as block:
            fw.replay(block)
    return nc


_CACHE = {}


def kernel(x, p, positions, norm_gains, w_in, w_out, ffn1_gate_up, ffn1_down, ffn2_gate_up, ffn2_down,
           hgrn_lb_logits, hgrn_norm_gain, attn_sinks, sg_ln_gain, sg_spatial_w, sg_spatial_b, ple_proj, ple_gate,
           _n_st=8, _n_layers=2, _n_phases=4, _cores=NCORES):
    f32 = lambda a: np.ascontiguousarray(np.asarray(a), dtype=np.float32)
    x = f32(x)
    p = f32(p)
    positions = np.ascontiguousarray(np.asarray(positions), dtype=np.int32)
    shared = {
        "norm_gains": f32(norm_gains), "w_in": f32(w_in), "w_out": f32(w_out),
        "ffn1_gate_up": f32(ffn1_gate_up), "ffn1_down": f32(ffn1_down),
        "ffn2_gate_up": f32(ffn2_gate_up), "ffn2_down": f32(ffn2_down),
        "hgrn_lb_logits": f32(hgrn_lb_logits), "hgrn_norm_gain": f32(hgrn_norm_gain),
        "attn_sinks": f32(attn_sinks), "sg_ln_gain": f32(sg_ln_gain),
        "sg_spatial_w": f32(sg_spatial_w), "sg_spatial_b": f32(sg_spatial_b),
        "ple_proj": f32(ple_proj), "ple_gate": f32(ple_gate),
        "consts": host_consts(),
    }
    key = (_n_st, _n_layers, _n_phases)
    nc = build(*key)
    in_maps = []
    for c in range(_cores):
        m = dict(shared)
        m["x"] = np.ascontiguousarray(x[c])
        m["p"] = np.ascontiguousarray(p[:, c])
        m["positions"] = np.ascontiguousarray(positions[c])
        in_maps.append(m)
    res = run_bass_kernel_spmd(nc, in_maps, core_ids=list(range(_cores)))
    out = np.stack([np.asarray(r["out"], dtype=np.float32) for r in res.results], axis=0)
    return out
```

```python
import numpy as np
import os
MIXSTOP = float(os.environ.get('MIXSTOP', '99'))
from contextlib import ExitStack
import concourse.bass as bass
import concourse.mybir as mybir
from concourse.bass_utils import run_bass_kernel_spmd

F32 = mybir.dt.float32
BF16 = mybir.dt.bfloat16
I32 = mybir.dt.int32
AF = mybir.ActivationFunctionType
ALU = mybir.AluOpType
AX = mybir.AxisListType

EPOCH = int(os.environ.get('EPOCH', '2000'))
DMA_EPOCH = 100
NCORES = 8
SEQ = 4096
D = 1024
DFF = 2816
NF = 22
INW = 2304
EPS = 1e-6
G = 512
NSUB = 4
GELU_C = 2.0 * 0.7978845608028654


class Res:
    __slots__ = ("name", "w", "r")

    def __init__(self, name):
        self.name = name
        self.w = None
        self.r = {}


class EngW:
    def __init__(self, fw, name, is_pe=False):
        self.fw = fw
        self.name = name
        self.is_pe = is_pe
        self.sems = []
        self.count = 0
        self.seen = {}
        self.pending = []
        self.q = []

    def next_token(self):
        e = self.count // EPOCH
        while len(self.sems) <= e:
            self.sems.append(self.fw.new_sem(f"{self.name}_e{len(self.sems)}"))
        tok = (self.sems[e], (self.count % EPOCH) + 1)
        self.count += 1
        return tok


class FW:
    def __init__(self, nc, stack):
        self.nc = nc
        self.stack = stack
        self.pe = EngW(self, "pe", is_pe=True)
        self.act = EngW(self, "act")
        self.dve = EngW(self, "dve")
        self.pool = EngW(self, "pool")
        self.sp = EngW(self, "sp")
        self.dma_sems = {}

    def new_sem(self, name):
        return self.stack.enter_context(self.nc.semaphore(name))

    def sb(self, name, shape, dt):
        return self.stack.enter_context(self.nc.sbuf_tensor(name, list(shape), dt))

    def ps(self, name, shape, dt):
        return self.stack.enter_context(self.nc.psum_tensor(name, list(shape), dt))

    def _deps(self, E, reads, writes, skip_pending=False):
        deps = {}

        def need(tok):
            if tok is None:
                return
            if tok == "PENDING":
                if skip_pending:
                    return
                raise RuntimeError("dependency on a pending PE op")
            sem, val = tok
            if deps.get(id(sem), (None, 0))[1] < val:
                deps[id(sem)] = (sem, val)

        for r in reads:
            need(r.w)
        for r in writes:
            need(r.w)
            for tok in r.r.values():
                need(tok)
        own = set(id(s) for s in E.sems) if E.is_pe else ()
        for k, (sem, val) in deps.items():
            if k in own:
                continue
            if E.seen.get(k, 0) < val:
                E.q.append(("w", sem, val))
                E.seen[k] = val

    def _commit(self, key, tok, reads, writes):
        for r in reads:
            r.r[key] = tok
        for r in writes:
            r.w = tok
            r.r = {}

    def op(self, E, name, *args, reads=(), writes=(), inc=True, **kw):
        self._deps(E, reads, writes, skip_pending=E.is_pe)
        if inc:
            tok = E.next_token()
            E.q.append(("i", name, args, kw, tok[0], 1))
            if E.pending:
                for (rs, ws) in E.pending:
                    self._commit(E.name, tok, rs, ws)
                E.pending = []
            self._commit(E.name, tok, reads, writes)
        else:
            assert E.is_pe
            E.q.append(("i", name, args, kw, None, 0))
            E.pending.append((list(reads), list(writes)))
            for r in reads:
                r.r[E.name] = "PENDING"
            for r in writes:
                r.w = "PENDING"
                r.r = {}

    def dma(self, Q, out, in_, reads=(), writes=(), key="dma", **kw):
        self._deps(Q, reads, writes)
        ent = self.dma_sems.get(key)
        if ent is None or ent[1] >= 16 * DMA_EPOCH:
            self.n_dma_sem = getattr(self, "n_dma_sem", 0) + 1
            ent = [self.new_sem(f"dma_{key}_{self.n_dma_sem}"), 0]
            self.dma_sems[key] = ent
            if key == "st":
                self.st_sems = getattr(self, "st_sems", []) + [ent]
        ent[1] += 16
        kw = dict(kw)
        kw["out"] = out
        kw["in_"] = in_
        Q.q.append(("i", "dma_start", (), kw, ent[0], 16))
        tok = (ent[0], ent[1])
        self._commit("dma_" + key, tok, reads, writes)
        return tok

    def replay(self, block):
        def run(E):
            def body(eng):
                for ent in E.q:
                    if ent[0] == "w":
                        eng.wait_ge(ent[1], ent[2])
                    else:
                        _, name, args, kw, sem, incv = ent
                        inst = getattr(eng, name)(*args, **kw)
                        if sem is not None:
                            inst.then_inc(sem, incv)
            return body
        block.tensor(run(self.pe))
        block.scalar(run(self.act))
        block.vector(run(self.dve))
        block.gpsimd(run(self.pool))
        block.sync(run(self.sp))


class Rot:
    def __init__(self, fw, name, shape, dt, n):
        self.t = [fw.sb(f"{name}{i}", shape, dt) for i in range(n)]
        self.r = [Res(f"{name}{i}") for i in range(n)]
        self.i = 0

    def next(self):
        k = self.i % len(self.t)
        self.i += 1
        return self.t[k], self.r[k]


C_IDENT, C_LT, C_MHG, C_MCUR, C_MPREV, C_TRIL, C_SEL, C_RMASK, C_INV = 0, 128, 256, 384, 512, 640, 768, 772, 774
C_TOT = 806


def host_consts():
    c = np.zeros((128, C_TOT), np.float32)
    i = np.arange(128)
    c[:, C_IDENT:C_IDENT + 128] = np.eye(128)
    J, I = np.meshgrid(i, i, indexing="ij")
    same = (J // 64) == (I // 64)
    mid = 64 * (I // 64) + 31
    c[:, C_LT:C_LT + 128] = same * ((J <= I).astype(np.float32) - (J <= mid).astype(np.float32))
    c[:, C_MHG:C_MHG + 128] = same & (J <= I)
    c[:, C_MCUR:C_MCUR + 128] = (J <= I)
    c[:, C_MPREV:C_MPREV + 128] = (J > I)
    c[:, C_TRIL:C_TRIL + 128] = (I <= J)
    for ch in range(2):
        inch = (i // 64) == ch
        m = 64 * ch + 31
        c[:, C_SEL + 2 * ch] = inch & (i <= m)
        c[:, C_SEL + 2 * ch + 1] = inch & (i > m)
        c[:, C_RMASK + ch] = inch
    inv = (10000.0 ** (-np.arange(32, dtype=np.float32) / 32)).astype(np.float32)
    c[:, C_INV:C_INV + 32] = inv[None, :]
    return c


def build(n_st=8, n_layers=2, n_phases=4):
    nc = bass.Bass("TRN2", target_bir_lowering=False)

    def din(name, shape, dt=F32):
        return nc.dram_tensor(name, list(shape), dt, kind="ExternalInput").ap()

    x_d = din("x", [SEQ, D])
    p_d = din("p", [2, SEQ, 256])
    pos_d = din("positions", [SEQ], I32)
    ng_d = din("norm_gains", [2, 8, D])
    W_d = {
        "win": din("w_in", [2, D, INW]), "wout": din("w_out", [2, D, D]),
        "gu1": din("ffn1_gate_up", [2, D, 2 * DFF]), "d1": din("ffn1_down", [2, DFF, D]),
        "gu2": din("ffn2_gate_up", [2, D, 2 * DFF]), "d2": din("ffn2_down", [2, DFF, D]),
        "ple": din("ple_proj", [2, 256, D]), "gate": din("ple_gate", [2, D, D]),
    }
    lbl_d = din("hgrn_lb_logits", [2, 256])
    hgn_d = din("hgrn_norm_gain", [2, 64])
    snk_d = din("attn_sinks", [2, 8])
    lng_d = din("sg_ln_gain", [2, 256])
    sgw_d = din("sg_spatial_w", [2, 4, 128, 128])
    sgb_d = din("sg_spatial_b", [2, 4, 128])
    cst_d = din("consts", [128, C_TOT])
    out_d = nc.dram_tensor("out", [SEQ, D], F32, kind="ExternalOutput").ap()

    S = {}
    RS = {}
    for k, ap in W_d.items():
        for l in range(2):
            shp = list(ap.shape[1:])
            S[k, l] = nc.dram_tensor(f"s_{k}{l}", shp, BF16, kind="Internal").ap()
            RS[k, l] = Res(f"s_{k}{l}")

    with ExitStack() as st_:
        fw = FW(nc, st_)
        pe, act, dve, pool, sp = fw.pe, fw.act, fw.dve, fw.pool, fw.sp
        op, dma = fw.op, fw.dma

        order = []
        for l in range(2):
            order += [("gu1", l), ("d1", l), ("win", l), ("wout", l), ("gu2", l), ("d2", l), ("gate", l), ("ple", l)]
        cast_todo = [(k, l) for (k, l) in order if l < n_layers]

        def emit_casts(n):
            for _ in range(n):
                if not cast_todo:
                    return
                k, l = cast_todo.pop(0)
                src = W_d[k][l]
                dma(pool, S[k, l].rearrange("a b -> (a b)").rearrange("(r c) -> r c", c=1024),
                    src.rearrange("a b -> (a b)").rearrange("(r c) -> r c", c=1024),
                    writes=[RS[k, l]], key="cast")

        emit_casts(100)

        def T(name, shape, dt):
            return fw.sb(name, shape, dt), Res(name)

        cst, Rcst = T("cst", [128, C_TOT], F32)
        identb, Rident = T("identb", [128, 128], BF16)
        LTb, RLT = T("LTb", [128, 128], BF16)
        selb, Rsel = T("selb", [128, 4], BF16)
        mcurb, Rmcur = T("mcurb", [128, 128], BF16)
        mprevb, Rmprev = T("mprevb", [128, 128], BF16)
        neghalf, Rneg = T("neghalf", [128, 8], F32)
        gT, RgT = T("gT", [128, 2, 8, 8], F32)
        lbm, Rlbm = T("lbm", [128, 2, 256], F32)
        oml, Roml = T("oml", [128, 2, 256], F32)
        hgn, Rhgn = T("hgn", [128, 2, 64], F32)
        esk, Resk = T("esk", [128, 2, 8], F32)
        lng, Rlng = T("lng", [128, 2, 256], F32)
        sgb, Rsgb = T("sgb", [128, 2, 4], F32)
        sgwT, RsgwT = T("sgwT", [128, 8, 128], BF16)
        posi, Rposi = T("posi", [128, 32], I32)
        posf, Rposf = T("posf", [128, 32], F32)

        h, _ = T("h", [128, NSUB, D], F32)
        Rh = [Res(f"h{s}") for s in range(NSUB)]
        xT, _ = T("xT", [128, 8, G], BF16)
        RxT = [Res(f"xT{s}") for s in range(NSUB)]
        mT, RmT = xT, RxT
        aT, _ = T("aT", [128, NF, G], BF16)
        RaT = [Res(f"aT{j}") for j in range(NF)]
        wgu = Rot(fw, "wgu", [128, 2, 8, 256], BF16, 2)
        wd = Rot(fw, "wd", [128, 2, D], BF16, 2)
        win_t, Rwin = T("win_t", [128, 8, INW], BF16)
        wout_v = aT[:, 0:16, :].rearrange("p (c a) t -> p c (a t)", a=2)
        wgate_v = win_t[:, :, 0:1024]
        wple_v = win_t[:, 0:2, 1024:2048]
        gbc = Rot(fw, "gbc", [128, D], F32, 1)
        psb = Rot(fw, "psb", [128, 256], F32, 1)
        junk, _ = T("junk", [128, D], BF16)

        psum = fw.ps("psum", [128, 8, 512], F32)
        PB = [Res(f"bank{i}") for i in range(8)]
        bank_ctr = [0]

        def nb():
            b = bank_ctr[0] % 8
            bank_ctr[0] += 1
            return b

        def nb2():
            if bank_ctr[0] % 2:
                bank_ctr[0] += 1
            b = bank_ctr[0] % 8
            bank_ctr[0] += 2
            return b

        ss_r = Rot(fw, "ss", [128, 8], F32, 4)
        ms_r = Rot(fw, "ms", [128, 8], F32, 4)
        rstd_r = Rot(fw, "rstd", [128, 8], F32, 4)
        xs_r = Rot(fw, "xs", [128, D], BF16, 2)
        silu_r = Rot(fw, "silu", [128, G], F32, 2)
        tmp_r = Rot(fw, "tmpf", [128, D], F32, 1)
        gate_r = Rot(fw, "gatef", [128, D], F32, 1)

        fsets = []
        for i_ in range(2):
            fsets.append({"hq": T(f"hq_sb{i_}", [128, 256], F32), "sgf": T(f"sgf{i_}", [128, 256], F32),
                          "sgate": T(f"sgate{i_}", [128, 256], F32), "v": T(f"v_bf{i_}", [128, 256], BF16),
                          "qk": T(f"qk_sb{i_}", [128, 10, 64], F32), "ge": T(f"ge_x{i_}", [128, 512], F32)})
        ge_x, Rgex = fsets[0]["ge"]
        ge_t, Rget = T("ge_t", [128, 512], F32)
        ge_s, Rges = ge_t, Rget
        ff, Rff = T("ff", [128, 256], F32)
        logf, Rlogf = ff, Rff
        kk, Rkk = T("kk", [128, 256], F32)
        lhi, Rlhi = T("lhi", [128, 256], BF16)
        llo, Rllo = T("llo", [128, 256], BF16)
        eb, Reb = T("eb", [128, 256], F32)
        enb, Renb = T("enb", [128, 256], F32)
        E_sb, RE = T("E_sb", [64, 4, 4], F32)
        qt, Rqt = T("qt", [128, 256], BF16)
        ktb, Rktb = T("ktb", [128, 256], BF16)
        kt0, Rkt0 = T("kt0", [128, 256], BF16)
        kt1, Rkt1 = T("kt1", [128, 256], BF16)
        qkT, RqkT = T("qkT", [64, 8, 128], BF16)
        qT0, RqT0 = T("qT0", [64, 4, 128], BF16)
        qT1, RqT1 = T("qT1", [64, 4, 128], BF16)
        attT, RattT = T("attT", [128, 4, 128], BF16)
        UE, RUE = T("UE", [64, 2, 4, 64], F32)
        Smf, RSmf = T("Smf", [64, 4, 64], F32)
        Smb, _ = T("Smb", [64, 2, 4, 64], BF16)
        RSmb = [Res("Smb0"), Res("Smb1")]
        St1, RSt1 = T("St1", [64, 4, 64], F32)
        o_sb, Ro = T("o_sb", [128, 256], F32)
        o_sq, Rosq = kk, Rkk
        rtmp, Rrtmp = T("rtmp", [128, 10, 64], F32)
        rsw, Rrsw = T("rsw", [128, 10, 64], F32)
        qk_r, Rqkr = T("qk_r", [128, 10, 64], BF16)
        qT_sb, RqTs = T("qT_sb", [64, 8, 128], BF16)
        PTc, _ = T("PTc", [128, 2, 512], BF16)
        PTp, _ = T("PTp", [128, 2, 512], BF16)
        RPTc = [Res("PTc0"), Res("PTc1")]
        RPTp = [Res("PTp0"), Res("PTp1")]
        den, Rden = T("den", [128, 8], F32)
        rden, Rrden = T("rden", [128, 8], F32)
        st6, Rst6 = T("st6", [128, 6], F32)
        mv, Rmv = T("mv", [128, 2], F32)
        vn, Rvn = rtmp[:, 0:4, :].rearrange("p a b -> p (a b)"), Rrtmp
        vnb, Rvnb = T("vnb", [128, 256], BF16)
        mixed_one = (fw.sb("mixed", [128, D], BF16), [Res("mix_a"), Res("mix_b"), Res("mix_c")])
        mixeds = [mixed_one, mixed_one]
        p_bf, Rpbf = T("p_bf", [128, 256], BF16)
        pT, RpT = T("pT", [128, 2, 128], BF16)
        cos2, Rcos2 = T("cos2", [128, NSUB, 64], F32)
        sin2, Rsin2 = T("sin2", [128, NSUB, 2, 32], F32)
        rp_t, Rrpt = rtmp[:, 0:4, :], Rrtmp
        rp_m, Rrpm = rtmp[:, 4:8, :], Rrtmp
        rp_f, Rrpf = rsw[:, 0:4, :], Rrsw
        rp_i, Rrpi = rsw[:, 4:8, :].bitcast(I32), Rrsw

        Sf = [T(f"Sf{l}", [64, 4, 64], F32) for l in range(2)]
        kTb = [fw.sb(f"kTb{l}", [64, 2, 5, 128], BF16) for l in range(2)]
        RkTb = [[Res(f"kTb{l}_{i}") for i in range(5)] for l in range(2)]
        Vau = [fw.sb(f"Vau{l}", [128, 5, 2, 65], BF16) for l in range(2)]
        RVau = [[Res(f"Vau{l}_{i}") for i in range(5)] for l in range(2)]

        dma(sp, cst[:], cst_d, writes=[Rcst], key="ld")
        op(dve, "tensor_copy", identb[:], cst[:, C_IDENT:C_IDENT + 128], reads=[Rcst], writes=[Rident])
        op(dve, "tensor_copy", LTb[:], cst[:, C_LT:C_LT + 128], reads=[Rcst], writes=[RLT])
        op(dve, "tensor_copy", selb[:], cst[:, C_SEL:C_SEL + 4], reads=[Rcst], writes=[Rsel])
        op(dve, "tensor_copy", mcurb[:], cst[:, C_MCUR:C_MCUR + 128], reads=[Rcst], writes=[Rmcur])
        op(dve, "tensor_copy", mprevb[:], cst[:, C_MPREV:C_MPREV + 128], reads=[Rcst], writes=[Rmprev])
        op(pool, "memset", neghalf[:], -0.5, writes=[Rneg])
        for l in range(2):
            for n in range(8):
                dma(sp, gT[:, l, n, :], ng_d[l, n].rearrange("(c p) -> p c", p=128), writes=[RgT], key="ld",
                    allow_slow_non_contiguous=True)
        lgt, Rlgt = gate_r.t[0][:, 0:512].rearrange("p (a b) -> p a b", a=2), gate_r.r[0]
        dma(sp, lgt, lbl_d.partition_broadcast(128), writes=[Rlgt], key="ld")
        dma(sp, hgn[:], hgn_d.partition_broadcast(128), writes=[Rhgn], key="ld")
        dma(sp, esk[:], snk_d.partition_broadcast(128), writes=[Resk], key="ld")
        dma(sp, lng[:], lng_d.partition_broadcast(128), writes=[Rlng], key="ld")
        dma(sp, sgb[:], sgb_d.rearrange("l g t -> t l g"), writes=[Rsgb], key="ld", allow_slow_non_contiguous=True)
        dma(sp, posi[:], pos_d.rearrange("(n p) -> p n", p=128), writes=[Rposi], key="ld", allow_slow_non_contiguous=True)
        op(dve, "tensor_copy", posf[:], posi[:], reads=[Rposi], writes=[Rposf])
        op(act, "activation", esk[:], esk[:], AF.Exp, reads=[Resk], writes=[Resk])
        d01, Rd01 = ge_x[:, 0:256], Rgex
        p0, Rp0 = ge_x[:, 256:512], Rgex
        p1, Rp1 = ge_t[:, 0:256], Rget
        op(dve, "tensor_tensor", d01, lgt[:, 0, :], lgt[:, 1, :], ALU.subtract, reads=[Rlgt], writes=[Rd01])
        op(act, "activation", p0, d01, AF.Sigmoid, reads=[Rd01], writes=[Rp0])
        op(act, "activation", p1, d01, AF.Sigmoid, scale=-1.0, reads=[Rd01], writes=[Rp1])
        op(dve, "tensor_tensor", lbm[:, 0, :], p0, p0, ALU.subtract, reads=[Rp0], writes=[Rlbm])
        op(dve, "tensor_tensor", lbm[:, 1, :], p0, p1, ALU.add, reads=[Rp0, Rp1, Rlbm], writes=[Rlbm])
        op(dve, "tensor_tensor", lbm[:, 1, :], lbm[:, 1, :], p0, ALU.subtract, reads=[Rp0, Rlbm], writes=[Rlbm])
        op(dve, "tensor_scalar", oml[:], lbm[:], -1.0, 1.0, ALU.mult, ALU.add, reads=[Rlbm], writes=[Roml])
        op(dve, "tensor_scalar", lbm[:], lbm[:], 1e-30, None, ALU.max, reads=[Rlbm, Roml], writes=[Rlbm])
        sgw_f, Rsgwf = tmp_r.t[0][:].rearrange("p (a b) -> p a b", a=8), tmp_r.r[0]
        sgw_b, Rsgwb = xs_r.t[0][:].rearrange("p (a b) -> p a b", a=8), xs_r.r[0]
        dma(sp, sgw_f, sgw_d.rearrange("l g t s -> t (l g) s"), writes=[Rsgwf], key="ld")
        op(dve, "tensor_tensor", sgw_b, sgw_f, cst[:, C_TRIL:C_TRIL + 128].unsqueeze(1).to_broadcast([128, 8, 128]),
           ALU.mult, reads=[Rsgwf, Rcst], writes=[Rsgwb])
        b = nb()
        bT = psum[:, b, :].bitcast(BF16)
        for i in range(8):
            op(pe, "transpose", bT[:, i * 128:(i + 1) * 128], sgw_b[:, i, :], identb[:], reads=[Rsgwb, Rident],
               writes=[PB[b]], inc=(i == 7))
        op(dve, "tensor_copy", sgwT[:].rearrange("p a b -> p (a b)"), bT, reads=[PB[b]], writes=[RsgwT])
        for l in range(2):
            op(pool, "memset", Sf[l][0][:], 0.0, writes=[Sf[l][1]])
            op(pool, "memset", Vau[l][:], 1.0, writes=RVau[l])
            op(pool, "memset", kTb[l][:], 0.0, writes=RkTb[l])
        op(pool, "memset", qT0[:], 0.0, writes=[RqT0])
        op(pool, "memset", qT1[:], 0.0, writes=[RqT1])

        def rstd_from(ss_ap, k, n, eps):
            ms_t, Rms = ms_r.next()
            rs_t, Rrs = rstd_r.next()
            return ms_t, Rms, rs_t, Rrs

        def emit_rstd(ss_t, Rss, k, n, eps):
            ms_t, Rms = ms_r.next()
            rs_t, Rrs = rstd_r.next()
            op(pool, "tensor_scalar", ms_t[:, 0:k], ss_t[:, 0:k], 1.0 / n, eps, ALU.mult, ALU.add, reads=[Rss], writes=[Rms])
            op(pool, "tensor_tensor", rs_t[:, 0:k], ms_t[:, 0:k], neghalf[:, 0:k], ALU.pow, reads=[Rms, Rneg], writes=[Rrs])
            return rs_t, Rrs

        def prenorm_T(l, n):
            for s in range(NSUB):
                ss_t, Rss = ss_r.next()
                op(act, "activation", junk[:], h[:, s, :], AF.Square, accum_out=ss_t[:, 0:1], reads=[Rh[s]], writes=[Rss])
                rs_t, Rrs = emit_rstd(ss_t, Rss, 1, D, EPS)
                xs_t, Rxs = xs_r.next()
                op(dve, "tensor_scalar", xs_t[:], h[:, s, :], rs_t[:, 0:1], None, ALU.mult, reads=[Rh[s], Rrs], writes=[Rxs])
                b = nb()
                bT = psum[:, b, :].bitcast(BF16)
                for kc in range(8):
                    op(pe, "transpose", bT[:, kc * 128:(kc + 1) * 128], xs_t[:, kc * 128:(kc + 1) * 128], identb[:],
                       reads=[Rxs, Rident], writes=[PB[b]], inc=(kc == 7))
                op(dve, "tensor_tensor", xT[:, :, s * 128:(s + 1) * 128], bT.rearrange("p (c t) -> p c t", c=8),
                   gT[:, l, n, :].unsqueeze(2).to_broadcast([128, 8, 128]), ALU.mult,
                   reads=[PB[b], RgT], writes=[RxT[s]])

        def load_gbc(l, n):
            t, r = gbc.next()
            dma(sp, t[:], ng_d[l, n].partition_broadcast(128), writes=[r], key="gbc")
            return t, r

        def post_norm(y_ap, y_reads, g_t, Rg, factor, s):
            ss_t, Rss = ss_r.next()
            op(act, "activation", junk[:].rearrange("p (a b) -> p a b", a=2), y_ap, AF.Square, accum_out=ss_t[:, 0:1],
               reads=y_reads, writes=[Rss])
            rs_t, Rrs = emit_rstd(ss_t, Rss, 1, D, EPS)
            tmp_t, Rtmp = tmp_r.next()
            op(dve, "scalar_tensor_tensor", tmp_t[:].rearrange("p (a b) -> p a b", a=2), y_ap, rs_t[:, 0:1],
               g_t[:].rearrange("p (a b) -> p a b", a=2), ALU.mult, ALU.mult, reads=list(y_reads) + [Rrs, Rg], writes=[Rtmp])
            op(dve, "scalar_tensor_tensor", h[:, s, :], tmp_t[:], float(factor), h[:, s, :], ALU.mult, ALU.add,
               reads=[Rtmp, Rh[s]], writes=[Rh[s]])

        def ffn(l, which):
            n_pre, n_post = (0, 1) if which == 1 else (4, 5)
            sgu, Rsgu = S[f"gu{which}", l], RS[f"gu{which}", l]
            sdn, Rsdn = S[f"d{which}", l], RS[f"d{which}", l]
            g_t, Rg = load_gbc(l, n_post)
            prenorm_T(l, n_pre)
            emit_casts(6 if which == 1 else 8)
            if which == 1:
                dma(sp, win_t[:], S["win", l].rearrange("(c p) n -> p c n", p=128), reads=[RS["win", l]], writes=[Rwin], key="wres")

            def load_gu(g):
                t, r = wgu.next()
                c0 = g * 256
                dma(sp, t[:, 0, :, :], sgu[:, c0:c0 + 256].rearrange("(c p) n -> p c n", p=128), reads=[Rsgu], writes=[r], key="wgu")
                dma(sp, t[:, 1, :, :], sgu[:, DFF + c0:DFF + c0 + 256].rearrange("(c p) n -> p c n", p=128), reads=[Rsgu], writes=[r], key="wgu")
                return t, r

            nxt = load_gu(0)
            for g in range(11):
                w_t, Rw = nxt
                if g + 1 < 11:
                    nxt = load_gu(g + 1)
                for jj in range(2):
                    j = 2 * g + jj
                    bA = nb()
                    bB = nb()
                    for (gu, bk) in ((0, bA), (1, bB)):
                        for kc in range(8):
                            op(pe, "matmul", psum[:, bk, :], w_t[:, gu, kc, jj * 128:(jj + 1) * 128], xT[:, kc, :],
                               start=(kc == 0), stop=(kc == 7), reads=[Rw] + RxT, writes=[PB[bk]], inc=(kc == 7))
                    sg_t, Rsg = silu_r.next()
                    op(act, "activation", sg_t[:], psum[:, bA, :], AF.Silu, reads=[PB[bA]], writes=[Rsg])
                    op(dve, "tensor_tensor", aT[:, j, :], sg_t[:], psum[:, bB, :], ALU.mult, reads=[Rsg, PB[bB]], writes=[RaT[j]])

            def load_d(jg):
                t, r = wd.next()
                dma(sp, t[:], sdn[jg * 256:(jg + 1) * 256, :].rearrange("(j p) n -> p j n", p=128), reads=[Rsdn], writes=[r], key="wd")
                return t, r

            nxt = load_d(0)
            for jg in range(11):
                w_t, Rw = nxt
                if jg + 1 < 11:
                    nxt = load_d(jg + 1)
                for jj in range(2):
                    j = 2 * jg + jj
                    for s in range(NSUB):
                        for hh in range(2):
                            last = (jj == 1 and s == NSUB - 1 and hh == 1)
                            op(pe, "matmul", psum[:, 2 * s + hh, :], aT[:, j, s * 128:(s + 1) * 128], w_t[:, jj, hh * 512:(hh + 1) * 512],
                               start=(j == 0), stop=(j == NF - 1), reads=[RaT[j], Rw], writes=[PB[2 * s + hh]], inc=last)
            bank_ctr[0] = 0
            for s in range(NSUB):
                post_norm(psum[:, 2 * s:2 * s + 2, :], [PB[2 * s], PB[2 * s + 1]], g_t, Rg, 0.5, s)

        def rope_tables(st):
            for s in range(NSUB):
                n = st * NSUB + s
                op(dve, "tensor_scalar", rp_t[:, s, 0:32], cst[:, C_INV:C_INV + 32], posf[:, n:n + 1], 1.0 / (2 * np.pi),
                   ALU.mult, ALU.mult, reads=[Rcst, Rposf], writes=[Rrpt])
            op(dve, "tensor_scalar", rp_t[:, :, 32:64], rp_t[:, :, 0:32], 0.25, None, ALU.add, reads=[Rrpt], writes=[Rrpt])
            op(dve, "tensor_copy", rp_i, rp_t, reads=[Rrpt], writes=[Rrpi])
            op(dve, "tensor_copy", rp_f, rp_i, reads=[Rrpi], writes=[Rrpf])
            op(dve, "tensor_tensor", rp_t, rp_t, rp_f, ALU.subtract, reads=[Rrpt, Rrpf], writes=[Rrpt])
            op(dve, "tensor_single_scalar", rp_m, rp_t, 0.5, ALU.is_gt, reads=[Rrpt], writes=[Rrpm])
            op(dve, "tensor_tensor", rp_t, rp_t, rp_m, ALU.subtract, reads=[Rrpt, Rrpm], writes=[Rrpt])
            op(dve, "tensor_single_scalar", rp_m, rp_t, -0.5, ALU.is_lt, reads=[Rrpt], writes=[Rrpm])
            op(dve, "tensor_tensor", rp_t, rp_t, rp_m, ALU.add, reads=[Rrpt, Rrpm], writes=[Rrpt])
            op(act, "activation", rp_f, rp_t, AF.Sin, scale=6.28318, reads=[Rrpt], writes=[Rrpf])
            op(dve, "tensor_copy", cos2[:, :, 0:32], rp_f[:, :, 32:64], reads=[Rrpf], writes=[Rcos2])
            op(dve, "tensor_copy", cos2[:, :, 32:64], rp_f[:, :, 32:64], reads=[Rrpf, Rcos2], writes=[Rcos2])
            op(dve, "tensor_scalar", sin2[:, :, 0, :], rp_f[:, :, 0:32], -1.0, None, ALU.mult, reads=[Rrpf], writes=[Rsin2])
            op(dve, "tensor_copy", sin2[:, :, 1, :], rp_f[:, :, 0:32], reads=[Rrpf, Rsin2], writes=[Rsin2])

        def run_gens(gens):
            gens = list(gens)
            while gens:
                for g in list(gens):
                    try:
                        next(g)
                    except StopIteration:
                        gens.remove(g)

        def mixer(l, st):
            g_t, Rg = load_gbc(l, 3)
            prenorm_T(l, 2)
            Sf_t, RSf = Sf[l]

            def gen_Z(s):
                F = fsets[s % 2]
                sl_cur = s + 1
                cbs = [(0, 512), (512, 512), (1024, 512), (1536, 256), (1792, 512)]
                for ci, (c0, w) in enumerate(cbs):
                    b = ci % 2
                    for kc in range(8):
                        op(pe, "matmul", psum[:, b, 0:w], xT[:, kc, s * 128:(s + 1) * 128], win_t[:, kc, c0:c0 + w],
                           start=(kc == 0), stop=(kc == 7), reads=[RxT[s], Rwin], writes=[PB[b]], inc=(kc == 7))
                    if ci == 0:
                        op(act, "activation", F["hq"][0][:], psum[:, b, 0:256], AF.Copy, reads=[PB[b]], writes=[F["hq"][1]])
                        op(act, "activation", F["sgf"][0][:], psum[:, b, 256:512], AF.Sigmoid, reads=[PB[b]], writes=[F["sgf"][1]])
                    elif ci == 1:
                        op(act, "activation", F["v"][0][:], psum[:, b, 0:256], AF.Copy, reads=[PB[b]], writes=[F["v"][1]])
                        op(act, "activation", F["sgate"][0][:], psum[:, b, 256:512], AF.Sigmoid, reads=[PB[b]], writes=[F["sgate"][1]])
                    elif ci == 2:
                        op(act, "activation", F["qk"][0][:, 0:8, :].rearrange("p a b -> p (a b)"), psum[:, b, :], AF.Copy,
                           reads=[PB[b]], writes=[F["qk"][1]])
                    elif ci == 3:
                        op(act, "activation", F["qk"][0][:, 8:10, :].rearrange("p a b -> p (a b)"), psum[:, b, 0:128], AF.Copy,
                           reads=[PB[b], F["qk"][1]], writes=[F["qk"][1]])
                        op(act, "activation", Vau[l][:, sl_cur, :, 0:64], psum[:, b, 128:256].rearrange("p (k d) -> p k d", k=2),
                           AF.Copy, reads=[PB[b]], writes=[RVau[l][sl_cur]])
                    else:
                        op(act, "activation", F["ge"][0][:], psum[:, b, :], AF.Copy, reads=[PB[b]], writes=[F["ge"][1]])
                    yield
                gx, Rgx = F["ge"]
                op(dve, "tensor_tensor", ge_t[:], gx[:], gx[:], ALU.mult, reads=[Rgx], writes=[Rget])
                op(dve, "tensor_scalar", ge_t[:], ge_t[:], 0.044715, 1.0, ALU.mult, ALU.add, reads=[Rget], writes=[Rget])
                op(dve, "tensor_tensor", ge_t[:], ge_t[:], gx[:], ALU.mult, reads=[Rget, Rgx], writes=[Rget])
                yield
                op(act, "activation", ge_t[:], ge_t[:], AF.Sigmoid, scale=GELU_C, reads=[Rget], writes=[Rget])
                yield
                op(dve, "tensor_tensor", gx[:], gx[:], ge_t[:], ALU.mult, reads=[Rgx, Rget], writes=[Rgx])
                op(pool, "tensor_tensor", F["sgate"][0][:].rearrange("p (a b) -> p a b", a=4), F["sgate"][0][:].rearrange("p (a b) -> p a b", a=4),
                   hgn[:, l, :].unsqueeze(1).to_broadcast([128, 4, 64]), ALU.mult, reads=[F["sgate"][1], Rhgn], writes=[F["sgate"][1]])
                yield

            def gen_A(s):
                F = fsets[s % 2]
                hq_sb, Rhq = F["hq"]
                sgf, Rsgf = F["sgf"]
                sgate, Rsgate = F["sgate"]
                v_bf, Rv = F["v"]
                mixed, Rmix = mixeds[s % 2]
                op(dve, "tensor_tensor", ff[:], sgf[:], oml[:, l, :], ALU.mult, reads=[Rsgf, Roml], writes=[Rff])
                op(dve, "tensor_tensor", ff[:], ff[:], lbm[:, l, :], ALU.add, reads=[Rff, Rlbm], writes=[Rff])
                yield
                op(act, "activation", ff[:], ff[:], AF.Ln, reads=[Rff], writes=[Rff])
                op(dve, "tensor_scalar", kk[:], sgf[:], -1.0, 1.0, ALU.mult, ALU.add, reads=[Rsgf], writes=[Rkk])
                op(dve, "tensor_tensor", kk[:], kk[:], oml[:, l, :], ALU.mult, reads=[Rkk, Roml], writes=[Rkk])
                yield
                op(dve, "tensor_copy", lhi[:], logf[:], reads=[Rlogf], writes=[Rlhi])
                op(dve, "tensor_tensor", llo[:], logf[:], lhi[:], ALU.subtract, reads=[Rlogf, Rlhi], writes=[Rllo])
                yield
                bb = 2
                op(pe, "matmul", psum[:, bb, 0:256], LTb[:], lhi[:], start=True, stop=False, reads=[RLT, Rlhi], writes=[PB[bb]], inc=False)
                op(pe, "matmul", psum[:, bb, 0:256], LTb[:], llo[:], start=False, stop=True, reads=[RLT, Rllo], writes=[PB[bb]], inc=True)
                be = 3
                for hh in range(4):
                    op(pe, "matmul", psum[0:64, be, hh * 4:(hh + 1) * 4], lhi[:, hh * 64:(hh + 1) * 64], selb[:], start=True, stop=False,
                       reads=[Rlhi, Rsel], writes=[PB[be]], inc=False)
                    op(pe, "matmul", psum[0:64, be, hh * 4:(hh + 1) * 4], llo[:, hh * 64:(hh + 1) * 64], selb[:], start=False, stop=True,
                       reads=[Rllo, Rsel], writes=[PB[be]], inc=(hh == 3))
                yield
                op(act, "activation", eb[:], psum[:, bb, 0:256], AF.Exp, reads=[PB[bb]], writes=[Reb])
                op(act, "activation", enb[:], psum[:, bb, 0:256], AF.Exp, scale=-1.0, reads=[PB[bb]], writes=[Renb])
                op(act, "activation", E_sb[:].rearrange("p a b -> p (a b)"), psum[0:64, be, 0:16], AF.Exp, reads=[PB[be]], writes=[RE])
                yield
                op(dve, "tensor_tensor", qt[:], hq_sb[:], eb[:], ALU.mult, reads=[Rhq, Reb], writes=[Rqt])
                op(dve, "tensor_tensor", ktb[:], kk[:], enb[:], ALU.mult, reads=[Rkk, Renb], writes=[Rktb])
                yield
                op(dve, "tensor_scalar", kt0[:], ktb[:], cst[:, C_RMASK:C_RMASK + 1], None, ALU.mult, reads=[Rktb, Rcst], writes=[Rkt0])
                op(dve, "tensor_scalar", kt1[:], ktb[:], cst[:, C_RMASK + 1:C_RMASK + 2], None, ALU.mult, reads=[Rktb, Rcst], writes=[Rkt1])
                bt = 2
                bT = psum[:, bt, :].bitcast(BF16)
                for hh in range(4):
                    op(pe, "transpose", bT[0:64, hh * 128:(hh + 1) * 128], qt[:, hh * 64:(hh + 1) * 64], identb[:],
                       reads=[Rqt, Rident], writes=[PB[bt]], inc=False)
                for hh in range(4):
                    op(pe, "transpose", bT[0:64, (4 + hh) * 128:(5 + hh) * 128], ktb[:, hh * 64:(hh + 1) * 64], identb[:],
                       reads=[Rktb, Rident], writes=[PB[bt]], inc=(hh == 3))
                yield
                op(act, "activation", qkT[:].rearrange("p a b -> p (a b)"), bT[0:64, :], AF.Copy, reads=[PB[bt]], writes=[RqkT])
                yield
                op(pool, "tensor_copy", qT0[:, :, 0:64], qkT[:, 0:4, 0:64], reads=[RqkT], writes=[RqT0])
                op(pool, "tensor_copy", qT1[:, :, 64:128], qkT[:, 0:4, 64:128], reads=[RqkT], writes=[RqT1])
                ba = 3
                for hh in range(4):
                    op(pe, "matmul", psum[:, ba, hh * 128:(hh + 1) * 128], qkT[:, 4 + hh, :], qkT[:, hh, :], start=True, stop=True,
                       reads=[RqkT], writes=[PB[ba]], inc=(hh == 3))
                bu = 2
                for c in range(2):
                    ktc, Rktc = (kt0, Rkt0) if c == 0 else (kt1, Rkt1)
                    for hh in range(4):
                        o0 = (c * 4 + hh) * 64
                        op(pe, "matmul", psum[0:64, bu, o0:o0 + 64], ktc[:, hh * 64:(hh + 1) * 64], v_bf[:, hh * 64:(hh + 1) * 64],
                           start=True, stop=True, reads=[Rktc, Rv], writes=[PB[bu]], inc=(c == 1 and hh == 3))
                yield
                op(dve, "tensor_tensor", attT[:], psum[:, ba, :].rearrange("p (a b) -> p a b", a=4),
                   cst[:, C_MHG:C_MHG + 128].unsqueeze(1).to_broadcast([128, 4, 128]), ALU.mult,
                   reads=[PB[ba], Rcst], writes=[RattT])
                for c in range(2):
                    op(dve, "tensor_tensor", UE[:, c, :, :], psum[0:64, bu, c * 256:(c + 1) * 256].rearrange("p (a b) -> p a b", a=4),
                       E_sb[:, :, 2 * c + 1:2 * c + 2].to_broadcast([64, 4, 64]), ALU.mult, reads=[PB[bu], RE], writes=[RUE])
                yield
                bo = 3
                for hh in range(4):
                    op(pe, "matmul", psum[:, bo, hh * 64:(hh + 1) * 64], attT[:, hh, :], v_bf[:, hh * 64:(hh + 1) * 64], start=True, stop=True,
                       reads=[RattT, Rv], writes=[PB[bo]], inc=(hh == 3))
                for c in range(2):
                    op(dve, "tensor_tensor", Smf[:], Sf_t[:], E_sb[:, :, 2 * c:2 * c + 1].to_broadcast([64, 4, 64]), ALU.mult,
                       reads=[RSf, RE], writes=[RSmf])
                    op(act, "activation", Smb[:, c, :, :], Smf[:], AF.Copy, reads=[RSmf], writes=[RSmb[c]])
                    op(dve, "tensor_tensor", St1[:], Smf[:], E_sb[:, :, 2 * c + 1:2 * c + 2].to_broadcast([64, 4, 64]), ALU.mult,
                       reads=[RSmf, RE], writes=[RSt1])
                    op(dve, "tensor_tensor", Sf_t[:], St1[:], UE[:, c, :, :], ALU.add, reads=[RSt1, RUE], writes=[RSf])
                    yield
                bi = 2
                for hh in range(4):
                    for c in range(2):
                        qTc, RqTc = (qT0, RqT0) if c == 0 else (qT1, RqT1)
                        op(pe, "matmul", psum[:, bi, hh * 64:(hh + 1) * 64], qTc[:, hh, :], Smb[:, c, hh, :], start=(c == 0), stop=(c == 1),
                           reads=[RqTc, RSmb[c]], writes=[PB[bi]], inc=(hh == 3 and c == 1))
                op(act, "activation", o_sb[:], psum[:, bo, 0:256], AF.Copy, reads=[PB[bo]], writes=[Ro])
                yield
                op(dve, "tensor_tensor", o_sb[:], o_sb[:], psum[:, bi, 0:256], ALU.add, reads=[Ro, PB[bi]], writes=[Ro])
                op(dve, "tensor_tensor", o_sq[:], o_sb[:], o_sb[:], ALU.mult, reads=[Ro], writes=[Rosq])
                ss_t, Rss = ss_r.next()
                op(dve, "tensor_reduce", ss_t[:, 0:4], o_sq[:].rearrange("p (a b) -> p a b", a=4), AX.X, ALU.add, reads=[Rosq], writes=[Rss])
                rs_t, Rrs = emit_rstd(ss_t, Rss, 4, 64, EPS)
                yield
                op(dve, "tensor_tensor", o_sq[:].rearrange("p (a b) -> p a b", a=4), o_sb[:].rearrange("p (a b) -> p a b", a=4),
                   rs_t[:, 0:4].unsqueeze(2).to_broadcast([128, 4, 64]), ALU.mult, reads=[Ro, Rrs, Rosq], writes=[Rosq])
                op(dve, "tensor_tensor", mixed[:, 0:256], o_sq[:], sgate[:], ALU.mult, reads=[Rosq, Rsgate], writes=[Rmix[0]])
                yield

            def gen_B(s):
                F = fsets[s % 2]
                qk_sb, Rqk = F["qk"]
                mixed, Rmix = mixeds[s % 2]
                nblk = st * NSUB + s
                sl_cur, sl_prev = s + 1, s
                qk3 = qk_sb[:].rearrange("p h (t d) -> p h t d", t=2)
                op(dve, "tensor_tensor", rsw[:].rearrange("p h (t d) -> p h t d", t=2)[:, :, 0, :], qk3[:, :, 1, :],
                   sin2[:, s, 0, :].unsqueeze(1).to_broadcast([128, 10, 32]), ALU.mult, reads=[Rqk, Rsin2], writes=[Rrsw])
                op(dve, "tensor_tensor", rsw[:].rearrange("p h (t d) -> p h t d", t=2)[:, :, 1, :], qk3[:, :, 0, :],
                   sin2[:, s, 1, :].unsqueeze(1).to_broadcast([128, 10, 32]), ALU.mult, reads=[Rqk, Rsin2, Rrsw], writes=[Rrsw])
                yield
                op(dve, "tensor_tensor", rtmp[:], qk_sb[:], cos2[:, s, :].unsqueeze(1).to_broadcast([128, 10, 64]), ALU.mult,
                   reads=[Rqk, Rcos2], writes=[Rrtmp])
                op(dve, "tensor_tensor", qk_r[:], rtmp[:], rsw[:], ALU.add, reads=[Rrtmp, Rrsw], writes=[Rqkr])
                yield
                bq = 4
                bqT = psum[:, bq, :].bitcast(BF16)
                for hh in range(8):
                    op(pe, "transpose", bqT[0:64, hh * 128:(hh + 1) * 128], qk_r[:, hh, :], identb[:], reads=[Rqkr, Rident],
                       writes=[PB[bq]], inc=(hh == 7))
                bk = 5
                bkT = psum[:, bk, :].bitcast(BF16)
                for kv in range(2):
                    op(pe, "transpose", bkT[0:64, kv * 128:(kv + 1) * 128], qk_r[:, 8 + kv, :], identb[:], reads=[Rqkr, Rident],
                       writes=[PB[bk]], inc=(kv == 1))
                yield
                op(act, "activation", qT_sb[:].rearrange("p a b -> p (a b)"), bqT[0:64, :], AF.Copy, reads=[PB[bq]], writes=[RqTs])
                op(act, "activation", kTb[l][:, :, sl_cur, :], bkT[0:64, 0:256].rearrange("p (a b) -> p a b", a=2), AF.Copy,
                   reads=[PB[bk]], writes=[RkTb[l][sl_cur]])
                yield
                slots = [(sl_cur, PTc, RPTc, mcurb, Rmcur)]
                if nblk > 0:
                    slots.append((sl_prev, PTp, RPTp, mprevb, Rmprev))
                for kv in range(2):
                    bss = []
                    for si_, (sl, PT, RPT, mk, Rmk) in enumerate(slots):
                        bs = 4 + si_
                        bss.append(bs)
                        op(pe, "matmul", psum[:, bs, :], kTb[l][:, kv, sl, :], qT_sb[:, 4 * kv:4 * kv + 4, :].rearrange("p a b -> p (a b)"),
                           start=True, stop=True, reads=[RkTb[l][sl], RqTs], writes=[PB[bs]], inc=True)
                    yield
                    for bs, (sl, PT, RPT, mk, Rmk) in zip(bss, slots):
                        op(act, "activation", PT[:, kv, :], psum[:, bs, :], AF.Exp, scale=0.125, reads=[PB[bs]], writes=[RPT[kv]])
                    yield
                    for bs, (sl, PT, RPT, mk, Rmk) in zip(bss, slots):
                        op(dve, "tensor_tensor", PT[:, kv, :].rearrange("p (a b) -> p a b", a=4), PT[:, kv, :].rearrange("p (a b) -> p a b", a=4),
                           mk[:].unsqueeze(1).to_broadcast([128, 4, 128]), ALU.mult, reads=[RPT[kv], Rmk], writes=[RPT[kv]])
                    yield
                for kv in range(2):
                    bp = 4 + kv
                    for g in range(4):
                        for si, (sl, PT, RPT, mk, Rmk) in enumerate(slots):
                            op(pe, "matmul", psum[:, bp, g * 65:(g + 1) * 65], PT[:, kv, g * 128:(g + 1) * 128], Vau[l][:, sl, kv, :],
                               start=(si == 0), stop=(si == len(slots) - 1), reads=[RPT[kv], RVau[l][sl]], writes=[PB[bp]],
                               inc=(g == 3 and si == len(slots) - 1))
                    yield
                    pv3 = psum[:, bp, 0:260].rearrange("p (a b) -> p a b", a=4)
                    op(dve, "tensor_tensor", den[:, 4 * kv:4 * kv + 4], pv3[:, :, 64], esk[:, l, 4 * kv:4 * kv + 4], ALU.add,
                       reads=[PB[bp], Resk], writes=[Rden])
                    op(dve, "reciprocal", rden[:, 4 * kv:4 * kv + 4], den[:, 4 * kv:4 * kv + 4], reads=[Rden], writes=[Rrden])
                    op(dve, "tensor_tensor", mixed[:, 256 + 256 * kv:512 + 256 * kv].rearrange("p (a b) -> p a b", a=4), pv3[:, :, 0:64],
                       rden[:, 4 * kv:4 * kv + 4].unsqueeze(2).to_broadcast([128, 4, 64]), ALU.mult,
                       reads=[PB[bp], Rrden], writes=[Rmix[1]])
                    yield
                F = fsets[s % 2]
                ge_x, Rgex = F["ge"]
                mixed, Rmix = mixeds[s % 2]
                op(dve, "bn_stats", st6[:], ge_x[:, 256:512], reads=[Rgex], writes=[Rst6])
                op(dve, "bn_aggr", mv[:], st6[:], reads=[Rst6], writes=[Rmv])
                ms_t, Rms = ms_r.next()
                rs2_t, Rrs2 = rstd_r.next()
                op(pool, "tensor_scalar", ms_t[:, 0:1], mv[:, 1:2], 1.0, EPS, ALU.mult, ALU.add, reads=[Rmv], writes=[Rms])
                op(pool, "tensor_tensor", rs2_t[:, 0:1], ms_t[:, 0:1], neghalf[:, 0:1], ALU.pow, reads=[Rms, Rneg], writes=[Rrs2])
                yield
                op(dve, "tensor_scalar", vn[:], ge_x[:, 256:512], mv[:, 0:1], rs2_t[:, 0:1], ALU.subtract, ALU.mult,
                   reads=[Rgex, Rmv, Rrs2], writes=[Rvn])
                op(dve, "tensor_tensor", vnb[:], vn[:], lng[:, l, :], ALU.mult, reads=[Rvn, Rlng], writes=[Rvnb])
                yield
                bm = 4
                for g in range(4):
                    op(pe, "matmul", psum[:, bm, g * 64:(g + 1) * 64], sgwT[:, l * 4 + g, :], vnb[:, g * 64:(g + 1) * 64], start=True, stop=True,
                       reads=[RsgwT, Rvnb], writes=[PB[bm]], inc=(g == 3))
                yield
                op(dve, "tensor_tensor", vn[:].rearrange("p (a b) -> p a b", a=4), psum[:, bm, 0:256].rearrange("p (a b) -> p a b", a=4),
                   sgb[:, l, :].unsqueeze(2).to_broadcast([128, 4, 64]), ALU.add, reads=[PB[bm], Rsgb, Rvn], writes=[Rvn])
                op(dve, "tensor_tensor", mixed[:, 768:1024], vn[:], ge_x[:, 0:256], ALU.mult, reads=[Rvn, Rgex], writes=[Rmix[2]])
                yield

            def gen_tail(s):
                mixed, Rmix = mixeds[s % 2]
                bx = 6
                bxT = psum[:, bx, :].bitcast(BF16)
                for kc in range(8):
                    op(pe, "transpose", bxT[:, kc * 128:(kc + 1) * 128], mixed[:, kc * 128:(kc + 1) * 128], identb[:],
                       reads=Rmix + [Rident], writes=[PB[bx]], inc=(kc == 7))
                yield
                op(act, "activation", mT[:, :, s * 128:(s + 1) * 128], bxT.rearrange("p (c t) -> p c t", c=8), AF.Copy,
                   reads=[PB[bx]], writes=[RmT[s]])
                yield
                b2 = 6
                for hh in range(2):
                    for kc in range(8):
                        op(pe, "matmul", psum[:, b2 + hh, :], mT[:, kc, s * 128:(s + 1) * 128], wout_v[:, kc, hh * 512:(hh + 1) * 512],
                           start=(kc == 0), stop=(kc == 7), reads=[RmT[s]] + RaT, writes=[PB[b2 + hh]], inc=(kc == 7))
                    yield
                post_norm(psum[:, b2:b2 + 2, :], [PB[b2], PB[b2 + 1]], g_t, Rg, 1.0, s)
                yield

            run_gens([gen_Z(0)])
            for i in range(NSUB):
                gens = []
                if i + 1 < NSUB:
                    gens.append(gen_Z(i + 1))
                gens += [gen_A(i), gen_B(i)]
                if i >= 1:
                    gens.append(gen_tail(i - 1))
                run_gens(gens)
            run_gens([gen_tail(NSUB - 1)])
            op(pool, "tensor_copy", kTb[l][:, :, 0, :], kTb[l][:, :, 4, :], reads=[RkTb[l][4]], writes=[RkTb[l][0]])
            op(pool, "tensor_copy", Vau[l][:, 0, :, :], Vau[l][:, 4, :, :], reads=[RVau[l][4]], writes=[RVau[l][0]])

        def ple(l, st):
            g_t, Rg = load_gbc(l, 7)
            prenorm_T(l, 6)
            for s in range(NSUB):
                b2 = nb2()
                for hh in range(2):
                    for kc in range(8):
                        op(pe, "matmul", psum[:, b2 + hh, :], xT[:, kc, s * 128:(s + 1) * 128], wgate_v[:, kc, hh * 512:(hh + 1) * 512],
                           start=(kc == 0), stop=(kc == 7), reads=[RxT[s], Rwin], writes=[PB[b2 + hh]], inc=(kc == 7))
                ga_t, Rga = gate_r.next()
                op(act, "activation", ga_t[:].rearrange("p (a b) -> p a b", a=2), psum[:, b2:b2 + 2, :], AF.Sigmoid,
                   reads=[PB[b2], PB[b2 + 1]], writes=[Rga])
                ps_t, Rps = psb.next()
                dma(sp, ps_t[:], p_d[l, st * G + s * 128:st * G + (s + 1) * 128, :], writes=[Rps], key="ldx")
                op(dve, "tensor_copy", p_bf[:], ps_t[:], reads=[Rps], writes=[Rpbf])
                bt = nb()
                bT = psum[:, bt, :].bitcast(BF16)
                for kc in range(2):
                    op(pe, "transpose", bT[:, kc * 128:(kc + 1) * 128], p_bf[:, kc * 128:(kc + 1) * 128], identb[:],
                       reads=[Rpbf, Rident], writes=[PB[bt]], inc=(kc == 1))
                op(act, "activation", pT[:].rearrange("p a b -> p (a b)"), bT[:, 0:256], AF.Copy, reads=[PB[bt]], writes=[RpT])
                b3 = nb2()
                for hh in range(2):
                    for kc in range(2):
                        op(pe, "matmul", psum[:, b3 + hh, :], pT[:, kc, :], wple_v[:, kc, hh * 512:(hh + 1) * 512],
                           start=(kc == 0), stop=(kc == 1), reads=[RpT, Rwin], writes=[PB[b3 + hh]], inc=(kc == 1))
                op(dve, "tensor_tensor", ga_t[:].rearrange("p (a b) -> p a b", a=2), psum[:, b3:b3 + 2, :],
                   ga_t[:].rearrange("p (a b) -> p a b", a=2), ALU.mult, reads=[PB[b3], PB[b3 + 1], Rga], writes=[Rga])
                post_norm(ga_t[:].rearrange("p (a b) -> p a b", a=2), [Rga], g_t, Rg, 1.0, s)

        phase_ctr = 0
        for st in range(n_st):
            t0 = st * G
            dma(sp, h[:], x_d[t0:t0 + G, :].rearrange("(s p) d -> p s d", p=128), writes=Rh, key="ldx")
            rope_tables(st)
            for l in range(n_layers):
                np_ = n_phases if l == n_layers - 1 else 4
                if np_ >= 1:
                    ffn(l, 1)
                if np_ >= 2:
                    dma(sp, wout_v, S["wout", l].rearrange("(c p) n -> p c n", p=128), reads=[RS["wout", l]], writes=RaT, key="wres")
                    mixer(l, st)
                if np_ >= 3:
                    dma(sp, wgate_v, S["gate", l].rearrange("(c p) n -> p c n", p=128), reads=[RS["gate", l]], writes=[Rwin], key="wres")
                    dma(sp, wple_v, S["ple", l].rearrange("(c p) n -> p c n", p=128), reads=[RS["ple", l]], writes=[Rwin], key="wres")
                    ffn(l, 2)
                if np_ >= 4:
                    ple(l, st)
            dma(sp, out_d[t0:t0 + G, :].rearrange("(s p) d -> p s d", p=128), h[:], reads=Rh, key="st")

        for ent in fw.st_sems:
            sp.q.append(("w", ent[0], ent[1]))
        with nc.Block() as block:
            fw.replay(block)
    return nc


_CACHE = {}


def kernel(x, p, positions, norm_gains, w_in, w_out, ffn1_gate_up, ffn1_down, ffn2_gate_up, ffn2_down,
           hgrn_lb_logits, hgrn_norm_gain, attn_sinks, sg_ln_gain, sg_spatial_w, sg_spatial_b, ple_proj, ple_gate,
           _n_st=8, _n_layers=2, _n_phases=4, _cores=NCORES):
    f32 = lambda a: np.ascontiguousarray(np.asarray(a), dtype=np.float32)
    x = f32(x)
    p = f32(p)
    positions = np.ascontiguousarray(np.asarray(positions), dtype=np.int32)
    shared = {
        "norm_gains": f32(norm_gains), "w_in": f32(w_in), "w_out": f32(w_out),
        "ffn1_gate_up": f32(ffn1_gate_up), "ffn1_down": f32(ffn1_down),
        "ffn2_gate_up": f32(ffn2_gate_up), "ffn2_down": f32(ffn2_down),
        "hgrn_lb_logits": f32(hgrn_lb_logits), "hgrn_norm_gain": f32(hgrn_norm_gain),
        "attn_sinks": f32(attn_sinks), "sg_ln_gain": f32(sg_ln_gain),
        "sg_spatial_w": f32(sg_spatial_w), "sg_spatial_b": f32(sg_spatial_b),
        "ple_proj": f32(ple_proj), "ple_gate": f32(ple_gate),
        "consts": host_consts(),
    }
    key = (_n_st, _n_layers, _n_phases)
    nc = build(*key)
    in_maps = []
    for c in range(_cores):
        m = dict(shared)
        m["x"] = np.ascontiguousarray(x[c])
        m["p"] = np.ascontiguousarray(p[:, c])
        m["positions"] = np.ascontiguousarray(positions[c])
        in_maps.append(m)
    res = run_bass_kernel_spmd(nc, in_maps, core_ids=list(range(_cores)))
    out = np.stack([np.asarray(r["out"], dtype=np.float32) for r in res.results], axis=0)
    return out
```

```python
import numpy as np
import os
MIXSTOP = float(os.environ.get('MIXSTOP', '99'))
from contextlib import ExitStack
import concourse.bass as bass
import concourse.mybir as mybir
from concourse.bass_utils import run_bass_kernel_spmd

F32 = mybir.dt.float32
BF16 = mybir.dt.bfloat16
I32 = mybir.dt.int32
AF = mybir.ActivationFunctionType
ALU = mybir.AluOpType
AX = mybir.AxisListType

EPOCH = int(os.environ.get('EPOCH', '2000'))
DMA_EPOCH = 100
NCORES = 8
SEQ = 4096
D = 1024
DFF = 2816
NF = 22
INW = 2304
EPS = 1e-6
G = 512
NSUB = 4
GELU_C = 2.0 * 0.7978845608028654


class Res:
    __slots__ = ("name", "w", "r")

    def __init__(self, name):
        self.name = name
        self.w = None
        self.r = {}


class EngW:
    def __init__(self, fw, name, is_pe=False):
        self.fw = fw
        self.name = name
        self.is_pe = is_pe
        self.sems = []
        self.count = 0
        self.seen = {}
        self.pending = []
        self.q = []

    def next_token(self):
        e = self.count // EPOCH
        while len(self.sems) <= e:
            self.sems.append(self.fw.new_sem(f"{self.name}_e{len(self.sems)}"))
        tok = (self.sems[e], (self.count % EPOCH) + 1)
        self.count += 1
        return tok


class FW:
    def __init__(self, nc, stack):
        self.nc = nc
        self.stack = stack
        self.pe = EngW(self, "pe", is_pe=True)
        self.act = EngW(self, "act")
        self.dve = EngW(self, "dve")
        self.pool = EngW(self, "pool")
        self.sp = EngW(self, "sp")
        self.dma_sems = {}

    def new_sem(self, name):
        return self.stack.enter_context(self.nc.semaphore(name))

    def sb(self, name, shape, dt):
        return self.stack.enter_context(self.nc.sbuf_tensor(name, list(shape), dt))

    def ps(self, name, shape, dt):
        return self.stack.enter_context(self.nc.psum_tensor(name, list(shape), dt))

    def _deps(self, E, reads, writes, skip_pending=False):
        deps = {}

        def need(tok):
            if tok is None:
                return
            if tok == "PENDING":
                if skip_pending:
                    return
                raise RuntimeError("dependency on a pending PE op")
            sem, val = tok
            if deps.get(id(sem), (None, 0))[1] < val:
                deps[id(sem)] = (sem, val)

        for r in reads:
            if r.w is None and r.name.startswith("s_"):
                raise RuntimeError(f"read of weight scratch {r.name} emitted before its cast")
            need(r.w)
        for r in writes:
            need(r.w)
            for tok in r.r.values():
                need(tok)
        own = set(id(s) for s in E.sems) if E.is_pe else ()
        for k, (sem, val) in deps.items():
            if k in own:
                continue
            if E.seen.get(k, 0) < val:
                E.q.append(("w", sem, val))
                E.seen[k] = val

    def _commit(self, key, tok, reads, writes):
        for r in reads:
            r.r[key] = tok
        for r in writes:
            r.w = tok
            r.r = {}

    def op(self, E, name, *args, reads=(), writes=(), inc=True, **kw):
        self._deps(E, reads, writes, skip_pending=E.is_pe)
        if inc:
            tok = E.next_token()
            E.q.append(("i", name, args, kw, tok[0], 1))
            if E.pending:
                for (rs, ws) in E.pending:
                    self._commit(E.name, tok, rs, ws)
                E.pending = []
            self._commit(E.name, tok, reads, writes)
        else:
            assert E.is_pe
            E.q.append(("i", name, args, kw, None, 0))
            E.pending.append((list(reads), list(writes)))
            for r in reads:
                r.r[E.name] = "PENDING"
            for r in writes:
                r.w = "PENDING"
                r.r = {}

    def dma(self, Q, out, in_, reads=(), writes=(), key="dma", **kw):
        self._deps(Q, reads, writes)
        ent = self.dma_sems.get(key)
        if ent is None or ent[1] >= 16 * DMA_EPOCH:
            self.n_dma_sem = getattr(self, "n_dma_sem", 0) + 1
            ent = [self.new_sem(f"dma_{key}_{self.n_dma_sem}"), 0]
            self.dma_sems[key] = ent
            if key == "st":
                self.st_sems = getattr(self, "st_sems", []) + [ent]
        ent[1] += 16
        kw = dict(kw)
        kw["out"] = out
        kw["in_"] = in_
        Q.q.append(("i", "dma_start", (), kw, ent[0], 16))
        tok = (ent[0], ent[1])
        self._commit("dma_" + key, tok, reads, writes)
        return tok

    def replay(self, block):
        def run(E):
            def body(eng):
                for ent in E.q:
                    if ent[0] == "w":
                        eng.wait_ge(ent[1], ent[2])
                    else:
                        _, name, args, kw, sem, incv = ent
                        inst = getattr(eng, name)(*args, **kw)
                        if sem is not None:
                            inst.then_inc(sem, incv)
            return body
        block.tensor(run(self.pe))
        block.scalar(run(self.act))
        block.vector(run(self.dve))
        block.gpsimd(run(self.pool))
        block.sync(run(self.sp))


class Rot:
    def __init__(self, fw, name, shape, dt, n):
        self.t = [fw.sb(f"{name}{i}", shape, dt) for i in range(n)]
        self.r = [Res(f"{name}{i}") for i in range(n)]
        self.i = 0

    def next(self):
        k = self.i % len(self.t)
        self.i += 1
        return self.t[k], self.r[k]


C_IDENT, C_LT, C_MHG, C_MCUR, C_MPREV, C_TRIL, C_SEL, C_RMASK, C_INV = 0, 128, 256, 384, 512, 640, 768, 772, 774
C_TOT = 806


def host_consts():
    c = np.zeros((128, C_TOT), np.float32)
    i = np.arange(128)
    c[:, C_IDENT:C_IDENT + 128] = np.eye(128)
    J, I = np.meshgrid(i, i, indexing="ij")
    same = (J // 64) == (I // 64)
    mid = 64 * (I // 64) + 31
    c[:, C_LT:C_LT + 128] = same * ((J <= I).astype(np.float32) - (J <= mid).astype(np.float32))
    c[:, C_MHG:C_MHG + 128] = same & (J <= I)
    c[:, C_MCUR:C_MCUR + 128] = (J <= I)
    c[:, C_MPREV:C_MPREV + 128] = (J > I)
    c[:, C_TRIL:C_TRIL + 128] = (I <= J)
    for ch in range(2):
        inch = (i // 64) == ch
        m = 64 * ch + 31
        c[:, C_SEL + 2 * ch] = inch & (i <= m)
        c[:, C_SEL + 2 * ch + 1] = inch & (i > m)
        c[:, C_RMASK + ch] = inch
    inv = (10000.0 ** (-np.arange(32, dtype=np.float32) / 32)).astype(np.float32)
    c[:, C_INV:C_INV + 32] = inv[None, :]
    return c


def build(n_st=8, n_layers=2, n_phases=4):
    nc = bass.Bass("TRN2", target_bir_lowering=False)

    def din(name, shape, dt=F32):
        return nc.dram_tensor(name, list(shape), dt, kind="ExternalInput").ap()

    x_d = din("x", [SEQ, D])
    p_d = din("p", [2, SEQ, 256])
    pos_d = din("positions", [SEQ], I32)
    ng_d = din("norm_gains", [2, 8, D])
    W_d = {
        "win": din("w_in", [2, D, INW]), "wout": din("w_out", [2, D, D]),
        "gu1": din("ffn1_gate_up", [2, D, 2 * DFF]), "d1": din("ffn1_down", [2, DFF, D]),
        "gu2": din("ffn2_gate_up", [2, D, 2 * DFF]), "d2": din("ffn2_down", [2, DFF, D]),
        "ple": din("ple_proj", [2, 256, D]), "gate": din("ple_gate", [2, D, D]),
    }
    lbl_d = din("hgrn_lb_logits", [2, 256])
    hgn_d = din("hgrn_norm_gain", [2, 64])
    snk_d = din("attn_sinks", [2, 8])
    lng_d = din("sg_ln_gain", [2, 256])
    sgw_d = din("sg_spatial_w", [2, 4, 128, 128])
    sgb_d = din("sg_spatial_b", [2, 4, 128])
    cst_d = din("consts", [128, C_TOT])
    out_d = nc.dram_tensor("out", [SEQ, D], F32, kind="ExternalOutput").ap()

    S = {}
    RS = {}
    for k, ap in W_d.items():
        for l in range(2):
            shp = list(ap.shape[1:])
            S[k, l] = nc.dram_tensor(f"s_{k}{l}", shp, BF16, kind="Internal").ap()
            RS[k, l] = Res(f"s_{k}{l}")

    with ExitStack() as st_:
        fw = FW(nc, st_)
        pe, act, dve, pool, sp = fw.pe, fw.act, fw.dve, fw.pool, fw.sp
        op, dma = fw.op, fw.dma

        order = []
        for l in range(2):
            order += [("gu1", l), ("d1", l), ("win", l), ("wout", l), ("gu2", l), ("d2", l), ("gate", l), ("ple", l)]
        CH = 1024
        cast_todo = []
        cast_pos = {}
        for (k, l) in order:
            if l >= n_layers:
                continue
            nrow = int(np.prod(W_d[k][l].shape)) // 1024
            for r0 in range(0, nrow, CH):
                cast_todo.append((k, l, r0, min(nrow, r0 + CH)))
            cast_pos[k, l] = len(cast_todo)
        cast_done = [0]

        def pump(n=1):
            for _ in range(n):
                if cast_done[0] >= len(cast_todo):
                    return
                k, l, r0, r1 = cast_todo[cast_done[0]]
                cast_done[0] += 1
                dst = S[k, l].rearrange("a b -> (a b)").rearrange("(r c) -> r c", c=1024)
                src = W_d[k][l].rearrange("a b -> (a b)").rearrange("(r c) -> r c", c=1024)
                dma(pool, dst[r0:r1, :], src[r0:r1, :], writes=[RS[k, l]], key="cast")

        def ensure_cast(k, l):
            pump(max(0, cast_pos[k, l] - cast_done[0]))

        ensure_cast("d1", 0)

        def T(name, shape, dt):
            return fw.sb(name, shape, dt), Res(name)

        cst, Rcst = T("cst", [128, C_TOT], F32)
        identb, Rident = T("identb", [128, 128], BF16)
        LTb, RLT = T("LTb", [128, 128], BF16)
        selb, Rsel = T("selb", [128, 4], BF16)
        mcurb, Rmcur = T("mcurb", [128, 128], BF16)
        mprevb, Rmprev = T("mprevb", [128, 128], BF16)
        neghalf, Rneg = T("neghalf", [128, 8], F32)
        gT, RgT = T("gT", [128, 2, 8, 8], F32)
        lbm, Rlbm = T("lbm", [128, 2, 256], F32)
        oml, Roml = T("oml", [128, 2, 256], F32)
        hgn, Rhgn = T("hgn", [128, 2, 64], F32)
        esk, Resk = T("esk", [128, 2, 8], F32)
        lng, Rlng = T("lng", [128, 2, 256], F32)
        sgb, Rsgb = T("sgb", [128, 2, 4], F32)
        sgwT, RsgwT = T("sgwT", [128, 8, 128], BF16)
        posi, Rposi = T("posi", [128, 32], I32)
        posf, Rposf = T("posf", [128, 32], F32)

        h, _ = T("h", [128, NSUB, D], F32)
        Rh = [Res(f"h{s}") for s in range(NSUB)]
        xT, _ = T("xT", [128, 8, G], BF16)
        RxT = [Res(f"xT{s}") for s in range(NSUB)]
        mT, RmT = xT, RxT
        aT, _ = T("aT", [128, NF, G], BF16)
        RaT = [Res(f"aT{j}") for j in range(NF)]
        wgu = Rot(fw, "wgu", [128, 2, 8, 256], BF16, 2)
        wd = Rot(fw, "wd", [128, 2, D], BF16, 2)
        win_t, Rwin = T("win_t", [128, 8, INW], BF16)
        wout_v = aT[:, 0:16, :].rearrange("p (c a) t -> p c (a t)", a=2)
        wgate_v = win_t[:, :, 0:1024]
        wple_v = win_t[:, 0:2, 1024:2048]
        gbc = Rot(fw, "gbc", [128, D], F32, 1)
        psb = Rot(fw, "psb", [128, 256], F32, 1)
        junk, _ = T("junk", [128, D], BF16)

        psum = fw.ps("psum", [128, 8, 512], F32)
        PB = [Res(f"bank{i}") for i in range(8)]
        bank_ctr = [0]

        def nb():
            b = bank_ctr[0] % 8
            bank_ctr[0] += 1
            return b

        def nb2():
            if bank_ctr[0] % 2:
                bank_ctr[0] += 1
            b = bank_ctr[0] % 8
            bank_ctr[0] += 2
            return b

        ss_r = Rot(fw, "ss", [128, 8], F32, 4)
        ms_r = Rot(fw, "ms", [128, 8], F32, 4)
        rstd_r = Rot(fw, "rstd", [128, 8], F32, 4)
        xs_r = Rot(fw, "xs", [128, D], BF16, 2)
        silu_r = Rot(fw, "silu", [128, G], F32, 2)
        tmp_r = Rot(fw, "tmpf", [128, D], F32, 1)
        gate_r = Rot(fw, "gatef", [128, D], F32, 1)

        fsets = []
        for i_ in range(2):
            fsets.append({"hq": T(f"hq_sb{i_}", [128, 256], F32), "sgf": T(f"sgf{i_}", [128, 256], F32),
                          "sgate": T(f"sgate{i_}", [128, 256], F32), "v": T(f"v_bf{i_}", [128, 256], BF16),
                          "qk": T(f"qk_sb{i_}", [128, 10, 64], F32), "ge": T(f"ge_x{i_}", [128, 512], F32)})
        ge_x, Rgex = fsets[0]["ge"]
        ge_t, Rget = T("ge_t", [128, 512], F32)
        ge_s, Rges = ge_t, Rget
        ff, Rff = T("ff", [128, 256], F32)
        logf, Rlogf = ff, Rff
        kk, Rkk = T("kk", [128, 256], F32)
        lhi, Rlhi = T("lhi", [128, 256], BF16)
        llo, Rllo = T("llo", [128, 256], BF16)
        eb, Reb = T("eb", [128, 256], F32)
        enb, Renb = T("enb", [128, 256], F32)
        E_sb, RE = T("E_sb", [64, 4, 4], F32)
        qt, Rqt = T("qt", [128, 256], BF16)
        ktb, Rktb = T("ktb", [128, 256], BF16)
        kt0, Rkt0 = T("kt0", [128, 256], BF16)
        kt1, Rkt1 = T("kt1", [128, 256], BF16)
        qkT, RqkT = T("qkT", [64, 8, 128], BF16)
        qT0, RqT0 = T("qT0", [64, 4, 128], BF16)
        qT1, RqT1 = T("qT1", [64, 4, 128], BF16)
        attT, RattT = T("attT", [128, 4, 128], BF16)
        UE, RUE = T("UE", [64, 2, 4, 64], F32)
        Smf, RSmf = T("Smf", [64, 4, 64], F32)
        Smb, _ = T("Smb", [64, 2, 4, 64], BF16)
        RSmb = [Res("Smb0"), Res("Smb1")]
        St1, RSt1 = T("St1", [64, 4, 64], F32)
        o_sb, Ro = T("o_sb", [128, 256], F32)
        o_sq, Rosq = kk, Rkk
        rtmp, Rrtmp = T("rtmp", [128, 10, 64], F32)
        rsw, Rrsw = T("rsw", [128, 10, 64], F32)
        qk_r, Rqkr = T("qk_r", [128, 10, 64], BF16)
        qT_sb, RqTs = T("qT_sb", [64, 8, 128], BF16)
        PTc, _ = T("PTc", [128, 2, 512], BF16)
        PTp, _ = T("PTp", [128, 2, 512], BF16)
        RPTc = [Res("PTc0"), Res("PTc1")]
        RPTp = [Res("PTp0"), Res("PTp1")]
        den, Rden = T("den", [128, 8], F32)
        rden, Rrden = T("rden", [128, 8], F32)
        st6, Rst6 = T("st6", [128, 6], F32)
        mv, Rmv = T("mv", [128, 2], F32)
        vn, Rvn = rtmp[:, 0:4, :].rearrange("p a b -> p (a b)"), Rrtmp
        vnb, Rvnb = T("vnb", [128, 256], BF16)
        mixed_one = (fw.sb("mixed", [128, D], BF16), [Res("mix_a"), Res("mix_b"), Res("mix_c")])
        mixeds = [mixed_one, mixed_one]
        p_bf, Rpbf = T("p_bf", [128, 256], BF16)
        pT, RpT = T("pT", [128, 2, 128], BF16)
        cos2, Rcos2 = T("cos2", [128, NSUB, 64], F32)
        sin2, Rsin2 = T("sin2", [128, NSUB, 2, 32], F32)
        rp_t, Rrpt = rtmp[:, 0:4, :], Rrtmp
        rp_m, Rrpm = rtmp[:, 4:8, :], Rrtmp
        rp_f, Rrpf = rsw[:, 0:4, :], Rrsw
        rp_i, Rrpi = rsw[:, 4:8, :].bitcast(I32), Rrsw

        Sf = [T(f"Sf{l}", [64, 4, 64], F32) for l in range(2)]
        kTb = [fw.sb(f"kTb{l}", [64, 2, 5, 128], BF16) for l in range(2)]
        RkTb = [[Res(f"kTb{l}_{i}") for i in range(5)] for l in range(2)]
        Vau = [fw.sb(f"Vau{l}", [128, 5, 2, 65], BF16) for l in range(2)]
        RVau = [[Res(f"Vau{l}_{i}") for i in range(5)] for l in range(2)]

        dma(sp, cst[:], cst_d, writes=[Rcst], key="ld")
        op(dve, "tensor_copy", identb[:], cst[:, C_IDENT:C_IDENT + 128], reads=[Rcst], writes=[Rident])
        op(dve, "tensor_copy", LTb[:], cst[:, C_LT:C_LT + 128], reads=[Rcst], writes=[RLT])
        op(dve, "tensor_copy", selb[:], cst[:, C_SEL:C_SEL + 4], reads=[Rcst], writes=[Rsel])
        op(dve, "tensor_copy", mcurb[:], cst[:, C_MCUR:C_MCUR + 128], reads=[Rcst], writes=[Rmcur])
        op(dve, "tensor_copy", mprevb[:], cst[:, C_MPREV:C_MPREV + 128], reads=[Rcst], writes=[Rmprev])
        op(pool, "memset", neghalf[:], -0.5, writes=[Rneg])
        for l in range(2):
            for n in range(8):
                dma(sp, gT[:, l, n, :], ng_d[l, n].rearrange("(c p) -> p c", p=128), writes=[RgT], key="ld",
                    allow_slow_non_contiguous=True)
        lgt, Rlgt = gate_r.t[0][:, 0:512].rearrange("p (a b) -> p a b", a=2), gate_r.r[0]
        dma(sp, lgt, lbl_d.partition_broadcast(128), writes=[Rlgt], key="ld")
        dma(sp, hgn[:], hgn_d.partition_broadcast(128), writes=[Rhgn], key="ld")
        dma(sp, esk[:], snk_d.partition_broadcast(128), writes=[Resk], key="ld")
        dma(sp, lng[:], lng_d.partition_broadcast(128), writes=[Rlng], key="ld")
        dma(sp, sgb[:], sgb_d.rearrange("l g t -> t l g"), writes=[Rsgb], key="ld", allow_slow_non_contiguous=True)
        dma(sp, posi[:], pos_d.rearrange("(n p) -> p n", p=128), writes=[Rposi], key="ld", allow_slow_non_contiguous=True)
        op(dve, "tensor_copy", posf[:], posi[:], reads=[Rposi], writes=[Rposf])
        op(act, "activation", esk[:], esk[:], AF.Exp, reads=[Resk], writes=[Resk])
        d01, Rd01 = ge_x[:, 0:256], Rgex
        p0, Rp0 = ge_x[:, 256:512], Rgex
        p1, Rp1 = ge_t[:, 0:256], Rget
        op(dve, "tensor_tensor", d01, lgt[:, 0, :], lgt[:, 1, :], ALU.subtract, reads=[Rlgt], writes=[Rd01])
        op(act, "activation", p0, d01, AF.Sigmoid, reads=[Rd01], writes=[Rp0])
        op(act, "activation", p1, d01, AF.Sigmoid, scale=-1.0, reads=[Rd01], writes=[Rp1])
        op(dve, "tensor_tensor", lbm[:, 0, :], p0, p0, ALU.subtract, reads=[Rp0], writes=[Rlbm])
        op(dve, "tensor_tensor", lbm[:, 1, :], p0, p1, ALU.add, reads=[Rp0, Rp1, Rlbm], writes=[Rlbm])
        op(dve, "tensor_tensor", lbm[:, 1, :], lbm[:, 1, :], p0, ALU.subtract, reads=[Rp0, Rlbm], writes=[Rlbm])
        op(dve, "tensor_scalar", oml[:], lbm[:], -1.0, 1.0, ALU.mult, ALU.add, reads=[Rlbm], writes=[Roml])
        op(dve, "tensor_scalar", lbm[:], lbm[:], 1e-30, None, ALU.max, reads=[Rlbm, Roml], writes=[Rlbm])
        sgw_f, Rsgwf = tmp_r.t[0][:].rearrange("p (a b) -> p a b", a=8), tmp_r.r[0]
        sgw_b, Rsgwb = xs_r.t[0][:].rearrange("p (a b) -> p a b", a=8), xs_r.r[0]
        dma(sp, sgw_f, sgw_d.rearrange("l g t s -> t (l g) s"), writes=[Rsgwf], key="ld")
        op(dve, "tensor_tensor", sgw_b, sgw_f, cst[:, C_TRIL:C_TRIL + 128].unsqueeze(1).to_broadcast([128, 8, 128]),
           ALU.mult, reads=[Rsgwf, Rcst], writes=[Rsgwb])
        b = nb()
        bT = psum[:, b, :].bitcast(BF16)
        for i in range(8):
            op(pe, "transpose", bT[:, i * 128:(i + 1) * 128], sgw_b[:, i, :], identb[:], reads=[Rsgwb, Rident],
               writes=[PB[b]], inc=(i == 7))
        op(dve, "tensor_copy", sgwT[:].rearrange("p a b -> p (a b)"), bT, reads=[PB[b]], writes=[RsgwT])
        for l in range(2):
            op(pool, "memset", Sf[l][0][:], 0.0, writes=[Sf[l][1]])
            op(pool, "memset", Vau[l][:], 1.0, writes=RVau[l])
            op(pool, "memset", kTb[l][:], 0.0, writes=RkTb[l])
        op(pool, "memset", qT0[:], 0.0, writes=[RqT0])
        op(pool, "memset", qT1[:], 0.0, writes=[RqT1])

        def rstd_from(ss_ap, k, n, eps):
            ms_t, Rms = ms_r.next()
            rs_t, Rrs = rstd_r.next()
            return ms_t, Rms, rs_t, Rrs

        def emit_rstd(ss_t, Rss, k, n, eps):
            ms_t, Rms = ms_r.next()
            rs_t, Rrs = rstd_r.next()
            op(pool, "tensor_scalar", ms_t[:, 0:k], ss_t[:, 0:k], 1.0 / n, eps, ALU.mult, ALU.add, reads=[Rss], writes=[Rms])
            op(pool, "tensor_tensor", rs_t[:, 0:k], ms_t[:, 0:k], neghalf[:, 0:k], ALU.pow, reads=[Rms, Rneg], writes=[Rrs])
            pump(1)
            return rs_t, Rrs

        def prenorm_T(l, n):
            for s in range(NSUB):
                ss_t, Rss = ss_r.next()
                op(act, "activation", junk[:], h[:, s, :], AF.Square, accum_out=ss_t[:, 0:1], reads=[Rh[s]], writes=[Rss])
                rs_t, Rrs = emit_rstd(ss_t, Rss, 1, D, EPS)
                xs_t, Rxs = xs_r.next()
                op(dve, "tensor_scalar", xs_t[:], h[:, s, :], rs_t[:, 0:1], None, ALU.mult, reads=[Rh[s], Rrs], writes=[Rxs])
                b = nb()
                bT = psum[:, b, :].bitcast(BF16)
                for kc in range(8):
                    op(pe, "transpose", bT[:, kc * 128:(kc + 1) * 128], xs_t[:, kc * 128:(kc + 1) * 128], identb[:],
                       reads=[Rxs, Rident], writes=[PB[b]], inc=(kc == 7))
                op(dve, "tensor_tensor", xT[:, :, s * 128:(s + 1) * 128], bT.rearrange("p (c t) -> p c t", c=8),
                   gT[:, l, n, :].unsqueeze(2).to_broadcast([128, 8, 128]), ALU.mult,
                   reads=[PB[b], RgT], writes=[RxT[s]])

        def load_gbc(l, n):
            t, r = gbc.next()
            dma(sp, t[:], ng_d[l, n].partition_broadcast(128), writes=[r], key="gbc")
            return t, r

        def post_norm(y_ap, y_reads, g_t, Rg, factor, s):
            ss_t, Rss = ss_r.next()
            op(act, "activation", junk[:].rearrange("p (a b) -> p a b", a=2), y_ap, AF.Square, accum_out=ss_t[:, 0:1],
               reads=y_reads, writes=[Rss])
            rs_t, Rrs = emit_rstd(ss_t, Rss, 1, D, EPS)
            tmp_t, Rtmp = tmp_r.next()
            op(dve, "scalar_tensor_tensor", tmp_t[:].rearrange("p (a b) -> p a b", a=2), y_ap, rs_t[:, 0:1],
               g_t[:].rearrange("p (a b) -> p a b", a=2), ALU.mult, ALU.mult, reads=list(y_reads) + [Rrs, Rg], writes=[Rtmp])
            op(dve, "scalar_tensor_tensor", h[:, s, :], tmp_t[:], float(factor), h[:, s, :], ALU.mult, ALU.add,
               reads=[Rtmp, Rh[s]], writes=[Rh[s]])

        def ffn(l, which):
            n_pre, n_post = (0, 1) if which == 1 else (4, 5)
            sgu, Rsgu = S[f"gu{which}", l], RS[f"gu{which}", l]
            sdn, Rsdn = S[f"d{which}", l], RS[f"d{which}", l]
            g_t, Rg = load_gbc(l, n_post)
            prenorm_T(l, n_pre)
            if which == 1:
                ensure_cast("win", l)
                dma(sp, win_t[:], S["win", l].rearrange("(c p) n -> p c n", p=128), reads=[RS["win", l]], writes=[Rwin], key="wres")

            ensure_cast(f"gu{which}", l)
            ensure_cast(f"d{which}", l)

            def load_gu(g):
                t, r = wgu.next()
                c0 = g * 256
                dma(sp, t[:, 0, :, :], sgu[:, c0:c0 + 256].rearrange("(c p) n -> p c n", p=128), reads=[Rsgu], writes=[r], key="wgu")
                dma(sp, t[:, 1, :, :], sgu[:, DFF + c0:DFF + c0 + 256].rearrange("(c p) n -> p c n", p=128), reads=[Rsgu], writes=[r], key="wgu")
                return t, r

            nxt = load_gu(0)
            for g in range(11):
                w_t, Rw = nxt
                if g + 1 < 11:
                    nxt = load_gu(g + 1)
                for jj in range(2):
                    j = 2 * g + jj
                    bA = nb()
                    bB = nb()
                    for (gu, bk) in ((0, bA), (1, bB)):
                        for kc in range(8):
                            op(pe, "matmul", psum[:, bk, :], w_t[:, gu, kc, jj * 128:(jj + 1) * 128], xT[:, kc, :],
                               start=(kc == 0), stop=(kc == 7), reads=[Rw] + RxT, writes=[PB[bk]], inc=(kc == 7))
                    sg_t, Rsg = silu_r.next()
                    op(act, "activation", sg_t[:], psum[:, bA, :], AF.Silu, reads=[PB[bA]], writes=[Rsg])
                    op(dve, "tensor_tensor", aT[:, j, :], sg_t[:], psum[:, bB, :], ALU.mult, reads=[Rsg, PB[bB]], writes=[RaT[j]])

            def load_d(jg):
                t, r = wd.next()
                dma(sp, t[:], sdn[jg * 256:(jg + 1) * 256, :].rearrange("(j p) n -> p j n", p=128), reads=[Rsdn], writes=[r], key="wd")
                return t, r

            nxt = load_d(0)
            for jg in range(11):
                w_t, Rw = nxt
                if jg + 1 < 11:
                    nxt = load_d(jg + 1)
                for jj in range(2):
                    j = 2 * jg + jj
                    for s in range(NSUB):
                        for hh in range(2):
                            last = (jj == 1 and s == NSUB - 1 and hh == 1)
                            op(pe, "matmul", psum[:, 2 * s + hh, :], aT[:, j, s * 128:(s + 1) * 128], w_t[:, jj, hh * 512:(hh + 1) * 512],
                               start=(j == 0), stop=(j == NF - 1), reads=[RaT[j], Rw], writes=[PB[2 * s + hh]], inc=last)
            bank_ctr[0] = 0
            for s in range(NSUB):
                post_norm(psum[:, 2 * s:2 * s + 2, :], [PB[2 * s], PB[2 * s + 1]], g_t, Rg, 0.5, s)

        def rope_tables(st):
            for s in range(NSUB):
                n = st * NSUB + s
                op(dve, "tensor_scalar", rp_t[:, s, 0:32], cst[:, C_INV:C_INV + 32], posf[:, n:n + 1], 1.0 / (2 * np.pi),
                   ALU.mult, ALU.mult, reads=[Rcst, Rposf], writes=[Rrpt])
            op(dve, "tensor_scalar", rp_t[:, :, 32:64], rp_t[:, :, 0:32], 0.25, None, ALU.add, reads=[Rrpt], writes=[Rrpt])
            op(dve, "tensor_copy", rp_i, rp_t, reads=[Rrpt], writes=[Rrpi])
            op(dve, "tensor_copy", rp_f, rp_i, reads=[Rrpi], writes=[Rrpf])
            op(dve, "tensor_tensor", rp_t, rp_t, rp_f, ALU.subtract, reads=[Rrpt, Rrpf], writes=[Rrpt])
            op(dve, "tensor_single_scalar", rp_m, rp_t, 0.5, ALU.is_gt, reads=[Rrpt], writes=[Rrpm])
            op(dve, "tensor_tensor", rp_t, rp_t, rp_m, ALU.subtract, reads=[Rrpt, Rrpm], writes=[Rrpt])
            op(dve, "tensor_single_scalar", rp_m, rp_t, -0.5, ALU.is_lt, reads=[Rrpt], writes=[Rrpm])
            op(dve, "tensor_tensor", rp_t, rp_t, rp_m, ALU.add, reads=[Rrpt, Rrpm], writes=[Rrpt])
            op(act, "activation", rp_f, rp_t, AF.Sin, scale=6.28318, reads=[Rrpt], writes=[Rrpf])
            op(dve, "tensor_copy", cos2[:, :, 0:32], rp_f[:, :, 32:64], reads=[Rrpf], writes=[Rcos2])
            op(dve, "tensor_copy", cos2[:, :, 32:64], rp_f[:, :, 32:64], reads=[Rrpf, Rcos2], writes=[Rcos2])
            op(dve, "tensor_scalar", sin2[:, :, 0, :], rp_f[:, :, 0:32], -1.0, None, ALU.mult, reads=[Rrpf], writes=[Rsin2])
            op(dve, "tensor_copy", sin2[:, :, 1, :], rp_f[:, :, 0:32], reads=[Rrpf, Rsin2], writes=[Rsin2])

        def run_gens(gens):
            gens = list(gens)
            while gens:
                for g in list(gens):
                    try:
                        next(g)
                    except StopIteration:
                        gens.remove(g)

        def mixer(l, st):
            g_t, Rg = load_gbc(l, 3)
            prenorm_T(l, 2)
            Sf_t, RSf = Sf[l]

            def gen_Z(s):
                F = fsets[s % 2]
                sl_cur = s + 1
                cbs = [(0, 512), (512, 512), (1024, 512), (1536, 256), (1792, 512)]
                for ci, (c0, w) in enumerate(cbs):
                    b = ci % 2
                    for kc in range(8):
                        op(pe, "matmul", psum[:, b, 0:w], xT[:, kc, s * 128:(s + 1) * 128], win_t[:, kc, c0:c0 + w],
                           start=(kc == 0), stop=(kc == 7), reads=[RxT[s], Rwin], writes=[PB[b]], inc=(kc == 7))
                    if ci == 0:
                        op(act, "activation", F["hq"][0][:], psum[:, b, 0:256], AF.Copy, reads=[PB[b]], writes=[F["hq"][1]])
                        op(act, "activation", F["sgf"][0][:], psum[:, b, 256:512], AF.Sigmoid, reads=[PB[b]], writes=[F["sgf"][1]])
                    elif ci == 1:
                        op(act, "activation", F["v"][0][:], psum[:, b, 0:256], AF.Copy, reads=[PB[b]], writes=[F["v"][1]])
                        op(act, "activation", F["sgate"][0][:], psum[:, b, 256:512], AF.Sigmoid, reads=[PB[b]], writes=[F["sgate"][1]])
                    elif ci == 2:
                        op(act, "activation", F["qk"][0][:, 0:8, :].rearrange("p a b -> p (a b)"), psum[:, b, :], AF.Copy,
                           reads=[PB[b]], writes=[F["qk"][1]])
                    elif ci == 3:
                        op(act, "activation", F["qk"][0][:, 8:10, :].rearrange("p a b -> p (a b)"), psum[:, b, 0:128], AF.Copy,
                           reads=[PB[b], F["qk"][1]], writes=[F["qk"][1]])
                        op(act, "activation", Vau[l][:, sl_cur, :, 0:64], psum[:, b, 128:256].rearrange("p (k d) -> p k d", k=2),
                           AF.Copy, reads=[PB[b]], writes=[RVau[l][sl_cur]])
                    else:
                        op(act, "activation", F["ge"][0][:], psum[:, b, :], AF.Copy, reads=[PB[b]], writes=[F["ge"][1]])
                    yield
                gx, Rgx = F["ge"]
                op(dve, "tensor_tensor", ge_t[:], gx[:], gx[:], ALU.mult, reads=[Rgx], writes=[Rget])
                op(dve, "tensor_scalar", ge_t[:], ge_t[:], 0.044715, 1.0, ALU.mult, ALU.add, reads=[Rget], writes=[Rget])
                op(dve, "tensor_tensor", ge_t[:], ge_t[:], gx[:], ALU.mult, reads=[Rget, Rgx], writes=[Rget])
                yield
                op(act, "activation", ge_t[:], ge_t[:], AF.Sigmoid, scale=GELU_C, reads=[Rget], writes=[Rget])
                yield
                op(dve, "tensor_tensor", gx[:], gx[:], ge_t[:], ALU.mult, reads=[Rgx, Rget], writes=[Rgx])
                op(pool, "tensor_tensor", F["sgate"][0][:].rearrange("p (a b) -> p a b", a=4), F["sgate"][0][:].rearrange("p (a b) -> p a b", a=4),
                   hgn[:, l, :].unsqueeze(1).to_broadcast([128, 4, 64]), ALU.mult, reads=[F["sgate"][1], Rhgn], writes=[F["sgate"][1]])
                yield

            def gen_A(s):
                F = fsets[s % 2]
                hq_sb, Rhq = F["hq"]
                sgf, Rsgf = F["sgf"]
                sgate, Rsgate = F["sgate"]
                v_bf, Rv = F["v"]
                mixed, Rmix = mixeds[s % 2]
                op(dve, "tensor_tensor", ff[:], sgf[:], oml[:, l, :], ALU.mult, reads=[Rsgf, Roml], writes=[Rff])
                op(dve, "tensor_tensor", ff[:], ff[:], lbm[:, l, :], ALU.add, reads=[Rff, Rlbm], writes=[Rff])
                yield
                op(act, "activation", ff[:], ff[:], AF.Ln, reads=[Rff], writes=[Rff])
                op(dve, "tensor_scalar", kk[:], sgf[:], -1.0, 1.0, ALU.mult, ALU.add, reads=[Rsgf], writes=[Rkk])
                op(dve, "tensor_tensor", kk[:], kk[:], oml[:, l, :], ALU.mult, reads=[Rkk, Roml], writes=[Rkk])
                yield
                op(dve, "tensor_copy", lhi[:], logf[:], reads=[Rlogf], writes=[Rlhi])
                op(dve, "tensor_tensor", llo[:], logf[:], lhi[:], ALU.subtract, reads=[Rlogf, Rlhi], writes=[Rllo])
                yield
                bb = 2
                op(pe, "matmul", psum[:, bb, 0:256], LTb[:], lhi[:], start=True, stop=False, reads=[RLT, Rlhi], writes=[PB[bb]], inc=False)
                op(pe, "matmul", psum[:, bb, 0:256], LTb[:], llo[:], start=False, stop=True, reads=[RLT, Rllo], writes=[PB[bb]], inc=True)
                be = 3
                for hh in range(4):
                    op(pe, "matmul", psum[0:64, be, hh * 4:(hh + 1) * 4], lhi[:, hh * 64:(hh + 1) * 64], selb[:], start=True, stop=False,
                       reads=[Rlhi, Rsel], writes=[PB[be]], inc=False)
                    op(pe, "matmul", psum[0:64, be, hh * 4:(hh + 1) * 4], llo[:, hh * 64:(hh + 1) * 64], selb[:], start=False, stop=True,
                       reads=[Rllo, Rsel], writes=[PB[be]], inc=(hh == 3))
                yield
                op(act, "activation", eb[:], psum[:, bb, 0:256], AF.Exp, reads=[PB[bb]], writes=[Reb])
                op(act, "activation", enb[:], psum[:, bb, 0:256], AF.Exp, scale=-1.0, reads=[PB[bb]], writes=[Renb])
                op(act, "activation", E_sb[:].rearrange("p a b -> p (a b)"), psum[0:64, be, 0:16], AF.Exp, reads=[PB[be]], writes=[RE])
                yield
                op(dve, "tensor_tensor", qt[:], hq_sb[:], eb[:], ALU.mult, reads=[Rhq, Reb], writes=[Rqt])
                op(dve, "tensor_tensor", ktb[:], kk[:], enb[:], ALU.mult, reads=[Rkk, Renb], writes=[Rktb])
                yield
                op(dve, "tensor_scalar", kt0[:], ktb[:], cst[:, C_RMASK:C_RMASK + 1], None, ALU.mult, reads=[Rktb, Rcst], writes=[Rkt0])
                op(dve, "tensor_scalar", kt1[:], ktb[:], cst[:, C_RMASK + 1:C_RMASK + 2], None, ALU.mult, reads=[Rktb, Rcst], writes=[Rkt1])
                bt = 2
                bT = psum[:, bt, :].bitcast(BF16)
                for hh in range(4):
                    op(pe, "transpose", bT[0:64, hh * 128:(hh + 1) * 128], qt[:, hh * 64:(hh + 1) * 64], identb[:],
                       reads=[Rqt, Rident], writes=[PB[bt]], inc=False)
                for hh in range(4):
                    op(pe, "transpose", bT[0:64, (4 + hh) * 128:(5 + hh) * 128], ktb[:, hh * 64:(hh + 1) * 64], identb[:],
                       reads=[Rktb, Rident], writes=[PB[bt]], inc=(hh == 3))
                yield
                op(act, "activation", qkT[:].rearrange("p a b -> p (a b)"), bT[0:64, :], AF.Copy, reads=[PB[bt]], writes=[RqkT])
                yield
                op(pool, "tensor_copy", qT0[:, :, 0:64], qkT[:, 0:4, 0:64], reads=[RqkT], writes=[RqT0])
                op(pool, "tensor_copy", qT1[:, :, 64:128], qkT[:, 0:4, 64:128], reads=[RqkT], writes=[RqT1])
                ba = 3
                for hh in range(4):
                    op(pe, "matmul", psum[:, ba, hh * 128:(hh + 1) * 128], qkT[:, 4 + hh, :], qkT[:, hh, :], start=True, stop=True,
                       reads=[RqkT], writes=[PB[ba]], inc=(hh == 3))
                bu = 2
                for c in range(2):
                    ktc, Rktc = (kt0, Rkt0) if c == 0 else (kt1, Rkt1)
                    for hh in range(4):
                        o0 = (c * 4 + hh) * 64
                        op(pe, "matmul", psum[0:64, bu, o0:o0 + 64], ktc[:, hh * 64:(hh + 1) * 64], v_bf[:, hh * 64:(hh + 1) * 64],
                           start=True, stop=True, reads=[Rktc, Rv], writes=[PB[bu]], inc=(c == 1 and hh == 3))
                yield
                op(dve, "tensor_tensor", attT[:], psum[:, ba, :].rearrange("p (a b) -> p a b", a=4),
                   cst[:, C_MHG:C_MHG + 128].unsqueeze(1).to_broadcast([128, 4, 128]), ALU.mult,
                   reads=[PB[ba], Rcst], writes=[RattT])
                for c in range(2):
                    op(dve, "tensor_tensor", UE[:, c, :, :], psum[0:64, bu, c * 256:(c + 1) * 256].rearrange("p (a b) -> p a b", a=4),
                       E_sb[:, :, 2 * c + 1:2 * c + 2].to_broadcast([64, 4, 64]), ALU.mult, reads=[PB[bu], RE], writes=[RUE])
                yield
                bo = 3
                for hh in range(4):
                    op(pe, "matmul", psum[:, bo, hh * 64:(hh + 1) * 64], attT[:, hh, :], v_bf[:, hh * 64:(hh + 1) * 64], start=True, stop=True,
                       reads=[RattT, Rv], writes=[PB[bo]], inc=(hh == 3))
                for c in range(2):
                    op(dve, "tensor_tensor", Smf[:], Sf_t[:], E_sb[:, :, 2 * c:2 * c + 1].to_broadcast([64, 4, 64]), ALU.mult,
                       reads=[RSf, RE], writes=[RSmf])
                    op(act, "activation", Smb[:, c, :, :], Smf[:], AF.Copy, reads=[RSmf], writes=[RSmb[c]])
                    op(dve, "tensor_tensor", St1[:], Smf[:], E_sb[:, :, 2 * c + 1:2 * c + 2].to_broadcast([64, 4, 64]), ALU.mult,
                       reads=[RSmf, RE], writes=[RSt1])
                    op(dve, "tensor_tensor", Sf_t[:], St1[:], UE[:, c, :, :], ALU.add, reads=[RSt1, RUE], writes=[RSf])
                    yield
                bi = 2
                for hh in range(4):
                    for c in range(2):
                        qTc, RqTc = (qT0, RqT0) if c == 0 else (qT1, RqT1)
                        op(pe, "matmul", psum[:, bi, hh * 64:(hh + 1) * 64], qTc[:, hh, :], Smb[:, c, hh, :], start=(c == 0), stop=(c == 1),
                           reads=[RqTc, RSmb[c]], writes=[PB[bi]], inc=(hh == 3 and c == 1))
                op(act, "activation", o_sb[:], psum[:, bo, 0:256], AF.Copy, reads=[PB[bo]], writes=[Ro])
                yield
                op(dve, "tensor_tensor", o_sb[:], o_sb[:], psum[:, bi, 0:256], ALU.add, reads=[Ro, PB[bi]], writes=[Ro])
                op(dve, "tensor_tensor", o_sq[:], o_sb[:], o_sb[:], ALU.mult, reads=[Ro], writes=[Rosq])
                ss_t, Rss = ss_r.next()
                op(dve, "tensor_reduce", ss_t[:, 0:4], o_sq[:].rearrange("p (a b) -> p a b", a=4), AX.X, ALU.add, reads=[Rosq], writes=[Rss])
                rs_t, Rrs = emit_rstd(ss_t, Rss, 4, 64, EPS)
                yield
                op(dve, "tensor_tensor", o_sq[:].rearrange("p (a b) -> p a b", a=4), o_sb[:].rearrange("p (a b) -> p a b", a=4),
                   rs_t[:, 0:4].unsqueeze(2).to_broadcast([128, 4, 64]), ALU.mult, reads=[Ro, Rrs, Rosq], writes=[Rosq])
                op(dve, "tensor_tensor", mixed[:, 0:256], o_sq[:], sgate[:], ALU.mult, reads=[Rosq, Rsgate], writes=[Rmix[0]])
                yield

            def gen_B(s):
                F = fsets[s % 2]
                qk_sb, Rqk = F["qk"]
                mixed, Rmix = mixeds[s % 2]
                nblk = st * NSUB + s
                sl_cur, sl_prev = s + 1, s
                qk3 = qk_sb[:].rearrange("p h (t d) -> p h t d", t=2)
                op(dve, "tensor_tensor", rsw[:].rearrange("p h (t d) -> p h t d", t=2)[:, :, 0, :], qk3[:, :, 1, :],
                   sin2[:, s, 0, :].unsqueeze(1).to_broadcast([128, 10, 32]), ALU.mult, reads=[Rqk, Rsin2], writes=[Rrsw])
                op(dve, "tensor_tensor", rsw[:].rearrange("p h (t d) -> p h t d", t=2)[:, :, 1, :], qk3[:, :, 0, :],
                   sin2[:, s, 1, :].unsqueeze(1).to_broadcast([128, 10, 32]), ALU.mult, reads=[Rqk, Rsin2, Rrsw], writes=[Rrsw])
                yield
                op(dve, "tensor_tensor", rtmp[:], qk_sb[:], cos2[:, s, :].unsqueeze(1).to_broadcast([128, 10, 64]), ALU.mult,
                   reads=[Rqk, Rcos2], writes=[Rrtmp])
                op(dve, "tensor_tensor", qk_r[:], rtmp[:], rsw[:], ALU.add, reads=[Rrtmp, Rrsw], writes=[Rqkr])
                yield
                bq = 4
                bqT = psum[:, bq, :].bitcast(BF16)
                for hh in range(8):
                    op(pe, "transpose", bqT[0:64, hh * 128:(hh + 1) * 128], qk_r[:, hh, :], identb[:], reads=[Rqkr, Rident],
                       writes=[PB[bq]], inc=(hh == 7))
                bk = 5
                bkT = psum[:, bk, :].bitcast(BF16)
                for kv in range(2):
                    op(pe, "transpose", bkT[0:64, kv * 128:(kv + 1) * 128], qk_r[:, 8 + kv, :], identb[:], reads=[Rqkr, Rident],
                       writes=[PB[bk]], inc=(kv == 1))
                yield
                op(act, "activation", qT_sb[:].rearrange("p a b -> p (a b)"), bqT[0:64, :], AF.Copy, reads=[PB[bq]], writes=[RqTs])
                op(act, "activation", kTb[l][:, :, sl_cur, :], bkT[0:64, 0:256].rearrange("p (a b) -> p a b", a=2), AF.Copy,
                   reads=[PB[bk]], writes=[RkTb[l][sl_cur]])
                yield
                slots = [(sl_cur, PTc, RPTc, mcurb, Rmcur)]
                if nblk > 0:
                    slots.append((sl_prev, PTp, RPTp, mprevb, Rmprev))
                for kv in range(2):
                    bss = []
                    for si_, (sl, PT, RPT, mk, Rmk) in enumerate(slots):
                        bs = 4 + si_
                        bss.append(bs)
                        op(pe, "matmul", psum[:, bs, :], kTb[l][:, kv, sl, :], qT_sb[:, 4 * kv:4 * kv + 4, :].rearrange("p a b -> p (a b)"),
                           start=True, stop=True, reads=[RkTb[l][sl], RqTs], writes=[PB[bs]], inc=True)
                    yield
                    for bs, (sl, PT, RPT, mk, Rmk) in zip(bss, slots):
                        op(act, "activation", PT[:, kv, :], psum[:, bs, :], AF.Exp, scale=0.125, reads=[PB[bs]], writes=[RPT[kv]])
                    yield
                    for bs, (sl, PT, RPT, mk, Rmk) in zip(bss, slots):
                        op(dve, "tensor_tensor", PT[:, kv, :].rearrange("p (a b) -> p a b", a=4), PT[:, kv, :].rearrange("p (a b) -> p a b", a=4),
                           mk[:].unsqueeze(1).to_broadcast([128, 4, 128]), ALU.mult, reads=[RPT[kv], Rmk], writes=[RPT[kv]])
                    yield
                for kv in range(2):
                    bp = 4 + kv
                    for g in range(4):
                        for si, (sl, PT, RPT, mk, Rmk) in enumerate(slots):
                            op(pe, "matmul", psum[:, bp, g * 65:(g + 1) * 65], PT[:, kv, g * 128:(g + 1) * 128], Vau[l][:, sl, kv, :],
                               start=(si == 0), stop=(si == len(slots) - 1), reads=[RPT[kv], RVau[l][sl]], writes=[PB[bp]],
                               inc=(g == 3 and si == len(slots) - 1))
                    yield
                    pv3 = psum[:, bp, 0:260].rearrange("p (a b) -> p a b", a=4)
                    op(dve, "tensor_tensor", den[:, 4 * kv:4 * kv + 4], pv3[:, :, 64], esk[:, l, 4 * kv:4 * kv + 4], ALU.add,
                       reads=[PB[bp], Resk], writes=[Rden])
                    op(dve, "reciprocal", rden[:, 4 * kv:4 * kv + 4], den[:, 4 * kv:4 * kv + 4], reads=[Rden], writes=[Rrden])
                    op(dve, "tensor_tensor", mixed[:, 256 + 256 * kv:512 + 256 * kv].rearrange("p (a b) -> p a b", a=4), pv3[:, :, 0:64],
                       rden[:, 4 * kv:4 * kv + 4].unsqueeze(2).to_broadcast([128, 4, 64]), ALU.mult,
                       reads=[PB[bp], Rrden], writes=[Rmix[1]])
                    yield
                F = fsets[s % 2]
                ge_x, Rgex = F["ge"]
                mixed, Rmix = mixeds[s % 2]
                op(dve, "bn_stats", st6[:], ge_x[:, 256:512], reads=[Rgex], writes=[Rst6])
                op(dve, "bn_aggr", mv[:], st6[:], reads=[Rst6], writes=[Rmv])
                ms_t, Rms = ms_r.next()
                rs2_t, Rrs2 = rstd_r.next()
                op(pool, "tensor_scalar", ms_t[:, 0:1], mv[:, 1:2], 1.0, EPS, ALU.mult, ALU.add, reads=[Rmv], writes=[Rms])
                op(pool, "tensor_tensor", rs2_t[:, 0:1], ms_t[:, 0:1], neghalf[:, 0:1], ALU.pow, reads=[Rms, Rneg], writes=[Rrs2])
                yield
                op(dve, "tensor_scalar", vn[:], ge_x[:, 256:512], mv[:, 0:1], rs2_t[:, 0:1], ALU.subtract, ALU.mult,
                   reads=[Rgex, Rmv, Rrs2], writes=[Rvn])
                op(dve, "tensor_tensor", vnb[:], vn[:], lng[:, l, :], ALU.mult, reads=[Rvn, Rlng], writes=[Rvnb])
                yield
                bm = 4
                for g in range(4):
                    op(pe, "matmul", psum[:, bm, g * 64:(g + 1) * 64], sgwT[:, l * 4 + g, :], vnb[:, g * 64:(g + 1) * 64], start=True, stop=True,
                       reads=[RsgwT, Rvnb], writes=[PB[bm]], inc=(g == 3))
                yield
                op(dve, "tensor_tensor", vn[:].rearrange("p (a b) -> p a b", a=4), psum[:, bm, 0:256].rearrange("p (a b) -> p a b", a=4),
                   sgb[:, l, :].unsqueeze(2).to_broadcast([128, 4, 64]), ALU.add, reads=[PB[bm], Rsgb, Rvn], writes=[Rvn])
                op(dve, "tensor_tensor", mixed[:, 768:1024], vn[:], ge_x[:, 0:256], ALU.mult, reads=[Rvn, Rgex], writes=[Rmix[2]])
                yield

            def gen_tail(s):
                mixed, Rmix = mixeds[s % 2]
                bx = 6
                bxT = psum[:, bx, :].bitcast(BF16)
                for kc in range(8):
                    op(pe, "transpose", bxT[:, kc * 128:(kc + 1) * 128], mixed[:, kc * 128:(kc + 1) * 128], identb[:],
                       reads=Rmix + [Rident], writes=[PB[bx]], inc=(kc == 7))
                yield
                op(act, "activation", mT[:, :, s * 128:(s + 1) * 128], bxT.rearrange("p (c t) -> p c t", c=8), AF.Copy,
                   reads=[PB[bx]], writes=[RmT[s]])
                yield
                b2 = 6
                for hh in range(2):
                    for kc in range(8):
                        op(pe, "matmul", psum[:, b2 + hh, :], mT[:, kc, s * 128:(s + 1) * 128], wout_v[:, kc, hh * 512:(hh + 1) * 512],
                           start=(kc == 0), stop=(kc == 7), reads=[RmT[s]] + RaT, writes=[PB[b2 + hh]], inc=(kc == 7))
                    yield
                post_norm(psum[:, b2:b2 + 2, :], [PB[b2], PB[b2 + 1]], g_t, Rg, 1.0, s)
                yield

            run_gens([gen_Z(0)])
            for i in range(NSUB):
                gens = []
                if i + 1 < NSUB:
                    gens.append(gen_Z(i + 1))
                gens += [gen_A(i), gen_B(i)]
                if i >= 1:
                    gens.append(gen_tail(i - 1))
                run_gens(gens)
            run_gens([gen_tail(NSUB - 1)])
            op(pool, "tensor_copy", kTb[l][:, :, 0, :], kTb[l][:, :, 4, :], reads=[RkTb[l][4]], writes=[RkTb[l][0]])
            op(pool, "tensor_copy", Vau[l][:, 0, :, :], Vau[l][:, 4, :, :], reads=[RVau[l][4]], writes=[RVau[l][0]])

        def ple(l, st):
            g_t, Rg = load_gbc(l, 7)
            prenorm_T(l, 6)
            for s in range(NSUB):
                b2 = nb2()
                for hh in range(2):
                    for kc in range(8):
                        op(pe, "matmul", psum[:, b2 + hh, :], xT[:, kc, s * 128:(s + 1) * 128], wgate_v[:, kc, hh * 512:(hh + 1) * 512],
                           start=(kc == 0), stop=(kc == 7), reads=[RxT[s], Rwin], writes=[PB[b2 + hh]], inc=(kc == 7))
                ga_t, Rga = gate_r.next()
                op(act, "activation", ga_t[:].rearrange("p (a b) -> p a b", a=2), psum[:, b2:b2 + 2, :], AF.Sigmoid,
                   reads=[PB[b2], PB[b2 + 1]], writes=[Rga])
                ps_t, Rps = psb.next()
                dma(sp, ps_t[:], p_d[l, st * G + s * 128:st * G + (s + 1) * 128, :], writes=[Rps], key="ldx")
                op(dve, "tensor_copy", p_bf[:], ps_t[:], reads=[Rps], writes=[Rpbf])
                bt = nb()
                bT = psum[:, bt, :].bitcast(BF16)
                for kc in range(2):
                    op(pe, "transpose", bT[:, kc * 128:(kc + 1) * 128], p_bf[:, kc * 128:(kc + 1) * 128], identb[:],
                       reads=[Rpbf, Rident], writes=[PB[bt]], inc=(kc == 1))
                op(act, "activation", pT[:].rearrange("p a b -> p (a b)"), bT[:, 0:256], AF.Copy, reads=[PB[bt]], writes=[RpT])
                b3 = nb2()
                for hh in range(2):
                    for kc in range(2):
                        op(pe, "matmul", psum[:, b3 + hh, :], pT[:, kc, :], wple_v[:, kc, hh * 512:(hh + 1) * 512],
                           start=(kc == 0), stop=(kc == 1), reads=[RpT, Rwin], writes=[PB[b3 + hh]], inc=(kc == 1))
                op(dve, "tensor_tensor", ga_t[:].rearrange("p (a b) -> p a b", a=2), psum[:, b3:b3 + 2, :],
                   ga_t[:].rearrange("p (a b) -> p a b", a=2), ALU.mult, reads=[PB[b3], PB[b3 + 1], Rga], writes=[Rga])
                post_norm(ga_t[:].rearrange("p (a b) -> p a b", a=2), [Rga], g_t, Rg, 1.0, s)

        phase_ctr = 0
        for st in range(n_st):
            t0 = st * G
            dma(sp, h[:], x_d[t0:t0 + G, :].rearrange("(s p) d -> p s d", p=128), writes=Rh, key="ldx")
            rope_tables(st)
            for l in range(n_layers):
                np_ = n_phases if l == n_layers - 1 else 4
                if np_ >= 1:
                    ffn(l, 1)
                if np_ >= 2:
                    ensure_cast("wout", l)
                    dma(sp, wout_v, S["wout", l].rearrange("(c p) n -> p c n", p=128), reads=[RS["wout", l]], writes=RaT, key="wres")
                    mixer(l, st)
                if np_ >= 3:
                    ensure_cast("gate", l)
                    dma(sp, wgate_v, S["gate", l].rearrange("(c p) n -> p c n", p=128), reads=[RS["gate", l]], writes=[Rwin], key="wres")
                    ensure_cast("ple", l)
                    dma(sp, wple_v, S["ple", l].rearrange("(c p) n -> p c n", p=128), reads=[RS["ple", l]], writes=[Rwin], key="wres")
                    ffn(l, 2)
                if np_ >= 4:
                    ple(l, st)
            dma(sp, out_d[t0:t0 + G, :].rearrange("(s p) d -> p s d", p=128), h[:], reads=Rh, key="st")

        for ent in fw.st_sems:
            sp.q.append(("w", ent[0], ent[1]))
        with nc.Block() as block:
            fw.replay(block)
    return nc


_CACHE = {}


def kernel(x, p, positions, norm_gains, w_in, w_out, ffn1_gate_up, ffn1_down, ffn2_gate_up, ffn2_down,
           hgrn_lb_logits, hgrn_norm_gain, attn_sinks, sg_ln_gain, sg_spatial_w, sg_spatial_b, ple_proj, ple_gate,
           _n_st=8, _n_layers=2, _n_phases=4, _cores=NCORES):
    f32 = lambda a: np.ascontiguousarray(np.asarray(a), dtype=np.float32)
    x = f32(x)
    p = f32(p)
    positions = np.ascontiguousarray(np.asarray(positions), dtype=np.int32)
    shared = {
        "norm_gains": f32(norm_gains), "w_in": f32(w_in), "w_out": f32(w_out),
        "ffn1_gate_up": f32(ffn1_gate_up), "ffn1_down": f32(ffn1_down),
        "ffn2_gate_up": f32(ffn2_gate_up), "ffn2_down": f32(ffn2_down),
        "hgrn_lb_logits": f32(hgrn_lb_logits), "hgrn_norm_gain": f32(hgrn_norm_gain),
        "attn_sinks": f32(attn_sinks), "sg_ln_gain": f32(sg_ln_gain),
        "sg_spatial_w": f32(sg_spatial_w), "sg_spatial_b": f32(sg_spatial_b),
        "ple_proj": f32(ple_proj), "ple_gate": f32(ple_gate),
        "consts": host_consts(),
    }
    key = (_n_st, _n_layers, _n_phases)
    nc = build(*key)
    in_maps = []
    for c in range(_cores):
        m = dict(shared)
        m["x"] = np.ascontiguousarray(x[c])
        m["p"] = np.ascontiguousarray(p[:, c])
        m["positions"] = np.ascontiguousarray(positions[c])
        in_maps.append(m)
    res = run_bass_kernel_spmd(nc, in_maps, core_ids=list(range(_cores)))
    out = np.stack([np.asarray(r["out"], dtype=np.float32) for r in res.results], axis=0)
    return out
```

```python
import numpy as np
import os
MIXSTOP = float(os.environ.get('MIXSTOP', '99'))
from contextlib import ExitStack
import concourse.bass as bass
import concourse.mybir as mybir
from concourse.bass_utils import run_bass_kernel_spmd

F32 = mybir.dt.float32
BF16 = mybir.dt.bfloat16
I32 = mybir.dt.int32
AF = mybir.ActivationFunctionType
ALU = mybir.AluOpType
AX = mybir.AxisListType

EPOCH = int(os.environ.get('EPOCH', '2000'))
DMA_EPOCH = 150
NCORES = 8
SEQ = 4096
D = 1024
DFF = 2816
NF = 22
INW = 2304
EPS = 1e-6
G = 512
NSUB = 4
GELU_C = 2.0 * 0.7978845608028654


class Res:
    __slots__ = ("name", "w", "r")

    def __init__(self, name):
        self.name = name
        self.w = None
        self.r = {}


class EngW:
    def __init__(self, fw, name, is_pe=False):
        self.fw = fw
        self.name = name
        self.is_pe = is_pe
        self.sems = []
        self.count = 0
        self.seen = {}
        self.pending = []
        self.q = []

    def next_token(self):
        e = self.count // EPOCH
        while len(self.sems) <= e:
            self.sems.append(self.fw.new_sem(f"{self.name}_e{len(self.sems)}"))
        tok = (self.sems[e], (self.count % EPOCH) + 1)
        self.count += 1
        return tok


class FW:
    def __init__(self, nc, stack):
        self.nc = nc
        self.stack = stack
        self.pe = EngW(self, "pe", is_pe=True)
        self.act = EngW(self, "act")
        self.dve = EngW(self, "dve")
        self.pool = EngW(self, "pool")
        self.sp = EngW(self, "sp")
        self.dma_sems = {}

    def new_sem(self, name):
        return self.stack.enter_context(self.nc.semaphore(name))

    def sb(self, name, shape, dt):
        return self.stack.enter_context(self.nc.sbuf_tensor(name, list(shape), dt))

    def ps(self, name, shape, dt):
        return self.stack.enter_context(self.nc.psum_tensor(name, list(shape), dt))

    def _deps(self, E, reads, writes, skip_pending=False):
        deps = {}

        def need(tok):
            if tok is None:
                return
            if tok == "PENDING":
                if skip_pending:
                    return
                raise RuntimeError("dependency on a pending PE op")
            sem, val = tok
            if deps.get(id(sem), (None, 0))[1] < val:
                deps[id(sem)] = (sem, val)

        for r in reads:
            if r.w is None and r.name.startswith("s_"):
                raise RuntimeError(f"read of weight scratch {r.name} emitted before its cast")
            need(r.w)
        for r in writes:
            need(r.w)
            for tok in r.r.values():
                need(tok)
        own = set(id(s) for s in E.sems) if E.is_pe else ()
        for k, (sem, val) in deps.items():
            if k in own:
                continue
            if E.seen.get(k, 0) < val:
                E.q.append(("w", sem, val))
                E.seen[k] = val

    def _commit(self, key, tok, reads, writes):
        for r in reads:
            r.r[key] = tok
        for r in writes:
            r.w = tok
            r.r = {}

    def op(self, E, name, *args, reads=(), writes=(), inc=True, **kw):
        self._deps(E, reads, writes, skip_pending=E.is_pe)
        if inc:
            tok = E.next_token()
            E.q.append(("i", name, args, kw, tok[0], 1))
            if E.pending:
                for (rs, ws) in E.pending:
                    self._commit(E.name, tok, rs, ws)
                E.pending = []
            self._commit(E.name, tok, reads, writes)
        else:
            assert E.is_pe
            E.q.append(("i", name, args, kw, None, 0))
            E.pending.append((list(reads), list(writes)))
            for r in reads:
                r.r[E.name] = "PENDING"
            for r in writes:
                r.w = "PENDING"
                r.r = {}

    def dma(self, Q, out, in_, reads=(), writes=(), key="dma", final=False, **kw):
        key = ("w_" + writes[0].name) if writes else ("r_" + reads[0].name)
        self._deps(Q, reads, writes)
        ent = self.dma_sems.get(key)
        if ent is None or ent[1] >= 16 * DMA_EPOCH:
            self.n_dma_sem = getattr(self, "n_dma_sem", 0) + 1
            ent = [self.new_sem(f"dma_{key}_{self.n_dma_sem}"), 0]
            self.dma_sems[key] = ent
            if final:
                self.st_sems = getattr(self, "st_sems", []) + [ent]
        ent[1] += 16
        kw = dict(kw)
        kw["out"] = out
        kw["in_"] = in_
        Q.q.append(("i", "dma_start", (), kw, ent[0], 16))
        tok = (ent[0], ent[1])
        self._commit("dma_" + key, tok, reads, writes)
        return tok

    def replay(self, block):
        def run(E):
            def body(eng):
                for ent in E.q:
                    if ent[0] == "w":
                        eng.wait_ge(ent[1], ent[2])
                    else:
                        _, name, args, kw, sem, incv = ent
                        inst = getattr(eng, name)(*args, **kw)
                        if sem is not None:
                            inst.then_inc(sem, incv)
            return body
        block.tensor(run(self.pe))
        block.scalar(run(self.act))
        block.vector(run(self.dve))
        block.gpsimd(run(self.pool))
        block.sync(run(self.sp))


class Rot:
    def __init__(self, fw, name, shape, dt, n):
        self.t = [fw.sb(f"{name}{i}", shape, dt) for i in range(n)]
        self.r = [Res(f"{name}{i}") for i in range(n)]
        self.i = 0

    def next(self):
        k = self.i % len(self.t)
        self.i += 1
        return self.t[k], self.r[k]


C_IDENT, C_LT, C_MHG, C_MCUR, C_MPREV, C_TRIL, C_SEL, C_RMASK, C_INV = 0, 128, 256, 384, 512, 640, 768, 772, 774
C_TOT = 806


def host_consts():
    c = np.zeros((128, C_TOT), np.float32)
    i = np.arange(128)
    c[:, C_IDENT:C_IDENT + 128] = np.eye(128)
    J, I = np.meshgrid(i, i, indexing="ij")
    same = (J // 64) == (I // 64)
    mid = 64 * (I // 64) + 31
    c[:, C_LT:C_LT + 128] = same * ((J <= I).astype(np.float32) - (J <= mid).astype(np.float32))
    c[:, C_MHG:C_MHG + 128] = same & (J <= I)
    c[:, C_MCUR:C_MCUR + 128] = (J <= I)
    c[:, C_MPREV:C_MPREV + 128] = (J > I)
    c[:, C_TRIL:C_TRIL + 128] = (I <= J)
    for ch in range(2):
        inch = (i // 64) == ch
        m = 64 * ch + 31
        c[:, C_SEL + 2 * ch] = inch & (i <= m)
        c[:, C_SEL + 2 * ch + 1] = inch & (i > m)
        c[:, C_RMASK + ch] = inch
    inv = (10000.0 ** (-np.arange(32, dtype=np.float32) / 32)).astype(np.float32)
    c[:, C_INV:C_INV + 32] = inv[None, :]
    return c


def build(n_st=8, n_layers=2, n_phases=4):
    nc = bass.Bass("TRN2", target_bir_lowering=False)

    def din(name, shape, dt=F32):
        return nc.dram_tensor(name, list(shape), dt, kind="ExternalInput").ap()

    x_d = din("x", [SEQ, D])
    p_d = din("p", [2, SEQ, 256])
    pos_d = din("positions", [SEQ], I32)
    ng_d = din("norm_gains", [2, 8, D])
    W_d = {
        "win": din("w_in", [2, D, INW]), "wout": din("w_out", [2, D, D]),
        "gu1": din("ffn1_gate_up", [2, D, 2 * DFF]), "d1": din("ffn1_down", [2, DFF, D]),
        "gu2": din("ffn2_gate_up", [2, D, 2 * DFF]), "d2": din("ffn2_down", [2, DFF, D]),
        "ple": din("ple_proj", [2, 256, D]), "gate": din("ple_gate", [2, D, D]),
    }
    lbl_d = din("hgrn_lb_logits", [2, 256])
    hgn_d = din("hgrn_norm_gain", [2, 64])
    snk_d = din("attn_sinks", [2, 8])
    lng_d = din("sg_ln_gain", [2, 256])
    sgw_d = din("sg_spatial_w", [2, 4, 128, 128])
    sgb_d = din("sg_spatial_b", [2, 4, 128])
    cst_d = din("consts", [128, C_TOT])
    out_d = nc.dram_tensor("out", [SEQ, D], F32, kind="ExternalOutput").ap()

    S = {}
    RS = {}
    for k, ap in W_d.items():
        for l in range(2):
            shp = list(ap.shape[1:])
            S[k, l] = nc.dram_tensor(f"s_{k}{l}", shp, BF16, kind="Internal").ap()
            RS[k, l] = Res(f"s_{k}{l}")

    with ExitStack() as st_:
        fw = FW(nc, st_)
        pe, act, dve, pool, sp = fw.pe, fw.act, fw.dve, fw.pool, fw.sp
        op, dma = fw.op, fw.dma

        order = []
        for l in range(2):
            order += [("gu1", l), ("d1", l), ("win", l), ("wout", l), ("gu2", l), ("d2", l), ("gate", l), ("ple", l)]
        CH = 1024
        cast_todo = []
        cast_pos = {}
        for (k, l) in order:
            if l >= n_layers:
                continue
            nrow = int(np.prod(W_d[k][l].shape)) // 1024
            for r0 in range(0, nrow, CH):
                cast_todo.append((k, l, r0, min(nrow, r0 + CH)))
            cast_pos[k, l] = len(cast_todo)
        cast_done = [0]

        def pump(n=1):
            for _ in range(n):
                if cast_done[0] >= len(cast_todo):
                    return
                k, l, r0, r1 = cast_todo[cast_done[0]]
                cast_done[0] += 1
                dst = S[k, l].rearrange("a b -> (a b)").rearrange("(r c) -> r c", c=1024)
                src = W_d[k][l].rearrange("a b -> (a b)").rearrange("(r c) -> r c", c=1024)
                dma(pool, dst[r0:r1, :], src[r0:r1, :], writes=[RS[k, l]], key="cast")

        def ensure_cast(k, l):
            pump(max(0, cast_pos[k, l] - cast_done[0]))

        ensure_cast("d1", 0)

        def T(name, shape, dt):
            return fw.sb(name, shape, dt), Res(name)

        cst, Rcst = T("cst", [128, C_TOT], F32)
        identb, Rident = T("identb", [128, 128], BF16)
        LTb, RLT = T("LTb", [128, 128], BF16)
        selb, Rsel = T("selb", [128, 4], BF16)
        mcurb, Rmcur = T("mcurb", [128, 128], BF16)
        mprevb, Rmprev = T("mprevb", [128, 128], BF16)
        neghalf, Rneg = T("neghalf", [128, 8], F32)
        gT, RgT = T("gT", [128, 2, 8, 8], F32)
        lbm, Rlbm = T("lbm", [128, 2, 256], F32)
        oml, Roml = T("oml", [128, 2, 256], F32)
        hgn, Rhgn = T("hgn", [128, 2, 64], F32)
        esk, Resk = T("esk", [128, 2, 8], F32)
        lng, Rlng = T("lng", [128, 2, 256], F32)
        sgb, Rsgb = T("sgb", [128, 2, 4], F32)
        sgwT, RsgwT = T("sgwT", [128, 8, 128], BF16)
        posi, Rposi = T("posi", [128, 32], I32)
        posf, Rposf = T("posf", [128, 32], F32)

        h, _ = T("h", [128, NSUB, D], F32)
        Rh = [Res(f"h{s}") for s in range(NSUB)]
        xT, _ = T("xT", [128, 8, G], BF16)
        RxT = [Res(f"xT{s}") for s in range(NSUB)]
        mT, RmT = xT, RxT
        aT, _ = T("aT", [128, NF, G], BF16)
        RaT = [Res(f"aT{j}") for j in range(NF)]
        wgu = Rot(fw, "wgu", [128, 2, 8, 256], BF16, 2)
        wd = Rot(fw, "wd", [128, 2, D], BF16, 2)
        win_t, Rwin = T("win_t", [128, 8, INW], BF16)
        wout_v = aT[:, 0:16, :].rearrange("p (c a) t -> p c (a t)", a=2)
        wgate_v = win_t[:, :, 0:1024]
        wple_v = win_t[:, 0:2, 1024:2048]
        gbc = Rot(fw, "gbc", [128, D], F32, 1)
        psb = Rot(fw, "psb", [128, 256], F32, 1)
        junk, _ = T("junk", [128, D], BF16)

        psum = fw.ps("psum", [128, 8, 512], F32)
        PB = [Res(f"bank{i}") for i in range(8)]
        bank_ctr = [0]

        def nb():
            b = bank_ctr[0] % 8
            bank_ctr[0] += 1
            return b

        def nb2():
            if bank_ctr[0] % 2:
                bank_ctr[0] += 1
            b = bank_ctr[0] % 8
            bank_ctr[0] += 2
            return b

        ss_r = Rot(fw, "ss", [128, 8], F32, 4)
        ms_r = Rot(fw, "ms", [128, 8], F32, 4)
        rstd_r = Rot(fw, "rstd", [128, 8], F32, 4)
        xs_r = Rot(fw, "xs", [128, D], BF16, 2)
        silu_r = Rot(fw, "silu", [128, G], F32, 2)
        tmp_r = Rot(fw, "tmpf", [128, D], F32, 1)
        gate_r = Rot(fw, "gatef", [128, D], F32, 1)

        fsets = []
        for i_ in range(2):
            fsets.append({"hq": T(f"hq_sb{i_}", [128, 256], F32), "sgf": T(f"sgf{i_}", [128, 256], F32),
                          "sgate": T(f"sgate{i_}", [128, 256], F32), "v": T(f"v_bf{i_}", [128, 256], BF16),
                          "qk": T(f"qk_sb{i_}", [128, 10, 64], F32), "ge": T(f"ge_x{i_}", [128, 512], F32)})
        ge_x, Rgex = fsets[0]["ge"]
        ge_t, Rget = T("ge_t", [128, 512], F32)
        ge_s, Rges = ge_t, Rget
        ff, Rff = T("ff", [128, 256], F32)
        logf, Rlogf = ff, Rff
        kk, Rkk = T("kk", [128, 256], F32)
        lhi, Rlhi = T("lhi", [128, 256], BF16)
        llo, Rllo = T("llo", [128, 256], BF16)
        eb, Reb = T("eb", [128, 256], F32)
        enb, Renb = T("enb", [128, 256], F32)
        E_sb, RE = T("E_sb", [64, 4, 4], F32)
        qt, Rqt = T("qt", [128, 256], BF16)
        ktb, Rktb = T("ktb", [128, 256], BF16)
        kt0, Rkt0 = T("kt0", [128, 256], BF16)
        kt1, Rkt1 = T("kt1", [128, 256], BF16)
        qkT, RqkT = T("qkT", [64, 8, 128], BF16)
        qT0, RqT0 = T("qT0", [64, 4, 128], BF16)
        qT1, RqT1 = T("qT1", [64, 4, 128], BF16)
        attT, RattT = T("attT", [128, 4, 128], BF16)
        UE, RUE = T("UE", [64, 2, 4, 64], F32)
        Smf, RSmf = T("Smf", [64, 4, 64], F32)
        Smb, _ = T("Smb", [64, 2, 4, 64], BF16)
        RSmb = [Res("Smb0"), Res("Smb1")]
        St1, RSt1 = T("St1", [64, 4, 64], F32)
        o_sb, Ro = T("o_sb", [128, 256], F32)
        o_sq, Rosq = kk, Rkk
        rtmp, Rrtmp = T("rtmp", [128, 10, 64], F32)
        rsw, Rrsw = T("rsw", [128, 10, 64], F32)
        qk_r, Rqkr = T("qk_r", [128, 10, 64], BF16)
        qT_sb, RqTs = T("qT_sb", [64, 8, 128], BF16)
        PTc, _ = T("PTc", [128, 2, 512], BF16)
        PTp, _ = T("PTp", [128, 2, 512], BF16)
        RPTc = [Res("PTc0"), Res("PTc1")]
        RPTp = [Res("PTp0"), Res("PTp1")]
        den, Rden = T("den", [128, 8], F32)
        rden, Rrden = T("rden", [128, 8], F32)
        st6, Rst6 = T("st6", [128, 6], F32)
        mv, Rmv = T("mv", [128, 2], F32)
        vn, Rvn = rtmp[:, 0:4, :].rearrange("p a b -> p (a b)"), Rrtmp
        vnb, Rvnb = T("vnb", [128, 256], BF16)
        mixed_one = (fw.sb("mixed", [128, D], BF16), [Res("mix_a"), Res("mix_b"), Res("mix_c")])
        mixeds = [mixed_one, mixed_one]
        p_bf, Rpbf = T("p_bf", [128, 256], BF16)
        pT, RpT = T("pT", [128, 2, 128], BF16)
        cos2, Rcos2 = T("cos2", [128, NSUB, 64], F32)
        sin2, Rsin2 = T("sin2", [128, NSUB, 2, 32], F32)
        rp_t, Rrpt = rtmp[:, 0:4, :], Rrtmp
        rp_m, Rrpm = rtmp[:, 4:8, :], Rrtmp
        rp_f, Rrpf = rsw[:, 0:4, :], Rrsw
        rp_i, Rrpi = rsw[:, 4:8, :].bitcast(I32), Rrsw

        Sf = [T(f"Sf{l}", [64, 4, 64], F32) for l in range(2)]
        kTb = [fw.sb(f"kTb{l}", [64, 2, 5, 128], BF16) for l in range(2)]
        RkTb = [[Res(f"kTb{l}_{i}") for i in range(5)] for l in range(2)]
        Vau = [fw.sb(f"Vau{l}", [128, 5, 2, 65], BF16) for l in range(2)]
        RVau = [[Res(f"Vau{l}_{i}") for i in range(5)] for l in range(2)]

        dma(sp, cst[:], cst_d, writes=[Rcst], key="ld")
        op(dve, "tensor_copy", identb[:], cst[:, C_IDENT:C_IDENT + 128], reads=[Rcst], writes=[Rident])
        op(dve, "tensor_copy", LTb[:], cst[:, C_LT:C_LT + 128], reads=[Rcst], writes=[RLT])
        op(dve, "tensor_copy", selb[:], cst[:, C_SEL:C_SEL + 4], reads=[Rcst], writes=[Rsel])
        op(dve, "tensor_copy", mcurb[:], cst[:, C_MCUR:C_MCUR + 128], reads=[Rcst], writes=[Rmcur])
        op(dve, "tensor_copy", mprevb[:], cst[:, C_MPREV:C_MPREV + 128], reads=[Rcst], writes=[Rmprev])
        op(pool, "memset", neghalf[:], -0.5, writes=[Rneg])
        for l in range(2):
            for n in range(8):
                dma(sp, gT[:, l, n, :], ng_d[l, n].rearrange("(c p) -> p c", p=128), writes=[RgT], key="ld",
                    allow_slow_non_contiguous=True)
        lgt, Rlgt = gate_r.t[0][:, 0:512].rearrange("p (a b) -> p a b", a=2), gate_r.r[0]
        dma(sp, lgt, lbl_d.partition_broadcast(128), writes=[Rlgt], key="ld")
        dma(sp, hgn[:], hgn_d.partition_broadcast(128), writes=[Rhgn], key="ld")
        dma(sp, esk[:], snk_d.partition_broadcast(128), writes=[Resk], key="ld")
        dma(sp, lng[:], lng_d.partition_broadcast(128), writes=[Rlng], key="ld")
        dma(sp, sgb[:], sgb_d.rearrange("l g t -> t l g"), writes=[Rsgb], key="ld", allow_slow_non_contiguous=True)
        dma(sp, posi[:], pos_d.rearrange("(n p) -> p n", p=128), writes=[Rposi], key="ld", allow_slow_non_contiguous=True)
        op(dve, "tensor_copy", posf[:], posi[:], reads=[Rposi], writes=[Rposf])
        op(act, "activation", esk[:], esk[:], AF.Exp, reads=[Resk], writes=[Resk])
        d01, Rd01 = ge_x[:, 0:256], Rgex
        p0, Rp0 = ge_x[:, 256:512], Rgex
        p1, Rp1 = ge_t[:, 0:256], Rget
        op(dve, "tensor_tensor", d01, lgt[:, 0, :], lgt[:, 1, :], ALU.subtract, reads=[Rlgt], writes=[Rd01])
        op(act, "activation", p0, d01, AF.Sigmoid, reads=[Rd01], writes=[Rp0])
        op(act, "activation", p1, d01, AF.Sigmoid, scale=-1.0, reads=[Rd01], writes=[Rp1])
        op(dve, "tensor_tensor", lbm[:, 0, :], p0, p0, ALU.subtract, reads=[Rp0], writes=[Rlbm])
        op(dve, "tensor_tensor", lbm[:, 1, :], p0, p1, ALU.add, reads=[Rp0, Rp1, Rlbm], writes=[Rlbm])
        op(dve, "tensor_tensor", lbm[:, 1, :], lbm[:, 1, :], p0, ALU.subtract, reads=[Rp0, Rlbm], writes=[Rlbm])
        op(dve, "tensor_scalar", oml[:], lbm[:], -1.0, 1.0, ALU.mult, ALU.add, reads=[Rlbm], writes=[Roml])
        op(dve, "tensor_scalar", lbm[:], lbm[:], 1e-30, None, ALU.max, reads=[Rlbm, Roml], writes=[Rlbm])
        sgw_f, Rsgwf = tmp_r.t[0][:].rearrange("p (a b) -> p a b", a=8), tmp_r.r[0]
        sgw_b, Rsgwb = xs_r.t[0][:].rearrange("p (a b) -> p a b", a=8), xs_r.r[0]
        dma(sp, sgw_f, sgw_d.rearrange("l g t s -> t (l g) s"), writes=[Rsgwf], key="ld")
        op(dve, "tensor_tensor", sgw_b, sgw_f, cst[:, C_TRIL:C_TRIL + 128].unsqueeze(1).to_broadcast([128, 8, 128]),
           ALU.mult, reads=[Rsgwf, Rcst], writes=[Rsgwb])
        b = nb()
        bT = psum[:, b, :].bitcast(BF16)
        for i in range(8):
            op(pe, "transpose", bT[:, i * 128:(i + 1) * 128], sgw_b[:, i, :], identb[:], reads=[Rsgwb, Rident],
               writes=[PB[b]], inc=(i == 7))
        op(dve, "tensor_copy", sgwT[:].rearrange("p a b -> p (a b)"), bT, reads=[PB[b]], writes=[RsgwT])
        for l in range(2):
            op(pool, "memset", Sf[l][0][:], 0.0, writes=[Sf[l][1]])
            op(pool, "memset", Vau[l][:], 1.0, writes=RVau[l])
            op(pool, "memset", kTb[l][:], 0.0, writes=RkTb[l])
        op(pool, "memset", qT0[:], 0.0, writes=[RqT0])
        op(pool, "memset", qT1[:], 0.0, writes=[RqT1])

        def rstd_from(ss_ap, k, n, eps):
            ms_t, Rms = ms_r.next()
            rs_t, Rrs = rstd_r.next()
            return ms_t, Rms, rs_t, Rrs

        def emit_rstd(ss_t, Rss, k, n, eps):
            ms_t, Rms = ms_r.next()
            rs_t, Rrs = rstd_r.next()
            op(pool, "tensor_scalar", ms_t[:, 0:k], ss_t[:, 0:k], 1.0 / n, eps, ALU.mult, ALU.add, reads=[Rss], writes=[Rms])
            op(pool, "tensor_tensor", rs_t[:, 0:k], ms_t[:, 0:k], neghalf[:, 0:k], ALU.pow, reads=[Rms, Rneg], writes=[Rrs])
            pump(1)
            return rs_t, Rrs

        def prenorm_T(l, n):
            for s in range(NSUB):
                ss_t, Rss = ss_r.next()
                op(act, "activation", junk[:], h[:, s, :], AF.Square, accum_out=ss_t[:, 0:1], reads=[Rh[s]], writes=[Rss])
                rs_t, Rrs = emit_rstd(ss_t, Rss, 1, D, EPS)
                xs_t, Rxs = xs_r.next()
                op(dve, "tensor_scalar", xs_t[:], h[:, s, :], rs_t[:, 0:1], None, ALU.mult, reads=[Rh[s], Rrs], writes=[Rxs])
                b = nb()
                bT = psum[:, b, :].bitcast(BF16)
                for kc in range(8):
                    op(pe, "transpose", bT[:, kc * 128:(kc + 1) * 128], xs_t[:, kc * 128:(kc + 1) * 128], identb[:],
                       reads=[Rxs, Rident], writes=[PB[b]], inc=(kc == 7))
                op(dve, "tensor_tensor", xT[:, :, s * 128:(s + 1) * 128], bT.rearrange("p (c t) -> p c t", c=8),
                   gT[:, l, n, :].unsqueeze(2).to_broadcast([128, 8, 128]), ALU.mult,
                   reads=[PB[b], RgT], writes=[RxT[s]])

        def load_gbc(l, n):
            t, r = gbc.next()
            dma(sp, t[:], ng_d[l, n].partition_broadcast(128), writes=[r], key="gbc")
            return t, r

        def post_norm(y_ap, y_reads, g_t, Rg, factor, s):
            ss_t, Rss = ss_r.next()
            op(act, "activation", junk[:].rearrange("p (a b) -> p a b", a=2), y_ap, AF.Square, accum_out=ss_t[:, 0:1],
               reads=y_reads, writes=[Rss])
            rs_t, Rrs = emit_rstd(ss_t, Rss, 1, D, EPS)
            tmp_t, Rtmp = tmp_r.next()
            op(dve, "scalar_tensor_tensor", tmp_t[:].rearrange("p (a b) -> p a b", a=2), y_ap, rs_t[:, 0:1],
               g_t[:].rearrange("p (a b) -> p a b", a=2), ALU.mult, ALU.mult, reads=list(y_reads) + [Rrs, Rg], writes=[Rtmp])
            op(dve, "scalar_tensor_tensor", h[:, s, :], tmp_t[:], float(factor), h[:, s, :], ALU.mult, ALU.add,
               reads=[Rtmp, Rh[s]], writes=[Rh[s]])

        def ffn(l, which):
            n_pre, n_post = (0, 1) if which == 1 else (4, 5)
            sgu, Rsgu = S[f"gu{which}", l], RS[f"gu{which}", l]
            sdn, Rsdn = S[f"d{which}", l], RS[f"d{which}", l]
            g_t, Rg = load_gbc(l, n_post)
            prenorm_T(l, n_pre)
            if which == 1:
                ensure_cast("win", l)
                dma(sp, win_t[:], S["win", l].rearrange("(c p) n -> p c n", p=128), reads=[RS["win", l]], writes=[Rwin], key="wres")

            ensure_cast(f"gu{which}", l)
            ensure_cast(f"d{which}", l)

            def load_gu(g):
                t, r = wgu.next()
                c0 = g * 256
                dma(sp, t[:, 0, :, :], sgu[:, c0:c0 + 256].rearrange("(c p) n -> p c n", p=128), reads=[Rsgu], writes=[r], key="wgu")
                dma(sp, t[:, 1, :, :], sgu[:, DFF + c0:DFF + c0 + 256].rearrange("(c p) n -> p c n", p=128), reads=[Rsgu], writes=[r], key="wgu")
                return t, r

            nxt = load_gu(0)
            for g in range(11):
                w_t, Rw = nxt
                if g + 1 < 11:
                    nxt = load_gu(g + 1)
                for jj in range(2):
                    j = 2 * g + jj
                    bA = nb()
                    bB = nb()
                    for (gu, bk) in ((0, bA), (1, bB)):
                        for kc in range(8):
                            op(pe, "matmul", psum[:, bk, :], w_t[:, gu, kc, jj * 128:(jj + 1) * 128], xT[:, kc, :],
                               start=(kc == 0), stop=(kc == 7), reads=[Rw] + RxT, writes=[PB[bk]], inc=(kc == 7))
                    sg_t, Rsg = silu_r.next()
                    op(act, "activation", sg_t[:], psum[:, bA, :], AF.Silu, reads=[PB[bA]], writes=[Rsg])
                    op(dve, "tensor_tensor", aT[:, j, :], sg_t[:], psum[:, bB, :], ALU.mult, reads=[Rsg, PB[bB]], writes=[RaT[j]])

            def load_d(jg):
                t, r = wd.next()
                dma(sp, t[:], sdn[jg * 256:(jg + 1) * 256, :].rearrange("(j p) n -> p j n", p=128), reads=[Rsdn], writes=[r], key="wd")
                return t, r

            nxt = load_d(0)
            for jg in range(11):
                w_t, Rw = nxt
                if jg + 1 < 11:
                    nxt = load_d(jg + 1)
                for jj in range(2):
                    j = 2 * jg + jj
                    for s in range(NSUB):
                        for hh in range(2):
                            last = (jj == 1 and s == NSUB - 1 and hh == 1)
                            op(pe, "matmul", psum[:, 2 * s + hh, :], aT[:, j, s * 128:(s + 1) * 128], w_t[:, jj, hh * 512:(hh + 1) * 512],
                               start=(j == 0), stop=(j == NF - 1), reads=[RaT[j], Rw], writes=[PB[2 * s + hh]], inc=last)
            bank_ctr[0] = 0
            for s in range(NSUB):
                post_norm(psum[:, 2 * s:2 * s + 2, :], [PB[2 * s], PB[2 * s + 1]], g_t, Rg, 0.5, s)

        def rope_tables(st):
            for s in range(NSUB):
                n = st * NSUB + s
                op(dve, "tensor_scalar", rp_t[:, s, 0:32], cst[:, C_INV:C_INV + 32], posf[:, n:n + 1], 1.0 / (2 * np.pi),
                   ALU.mult, ALU.mult, reads=[Rcst, Rposf], writes=[Rrpt])
            op(dve, "tensor_scalar", rp_t[:, :, 32:64], rp_t[:, :, 0:32], 0.25, None, ALU.add, reads=[Rrpt], writes=[Rrpt])
            op(dve, "tensor_copy", rp_i, rp_t, reads=[Rrpt], writes=[Rrpi])
            op(dve, "tensor_copy", rp_f, rp_i, reads=[Rrpi], writes=[Rrpf])
            op(dve, "tensor_tensor", rp_t, rp_t, rp_f, ALU.subtract, reads=[Rrpt, Rrpf], writes=[Rrpt])
            op(dve, "tensor_single_scalar", rp_m, rp_t, 0.5, ALU.is_gt, reads=[Rrpt], writes=[Rrpm])
            op(dve, "tensor_tensor", rp_t, rp_t, rp_m, ALU.subtract, reads=[Rrpt, Rrpm], writes=[Rrpt])
            op(dve, "tensor_single_scalar", rp_m, rp_t, -0.5, ALU.is_lt, reads=[Rrpt], writes=[Rrpm])
            op(dve, "tensor_tensor", rp_t, rp_t, rp_m, ALU.add, reads=[Rrpt, Rrpm], writes=[Rrpt])
            op(act, "activation", rp_f, rp_t, AF.Sin, scale=6.28318, reads=[Rrpt], writes=[Rrpf])
            op(dve, "tensor_copy", cos2[:, :, 0:32], rp_f[:, :, 32:64], reads=[Rrpf], writes=[Rcos2])
            op(dve, "tensor_copy", cos2[:, :, 32:64], rp_f[:, :, 32:64], reads=[Rrpf, Rcos2], writes=[Rcos2])
            op(dve, "tensor_scalar", sin2[:, :, 0, :], rp_f[:, :, 0:32], -1.0, None, ALU.mult, reads=[Rrpf], writes=[Rsin2])
            op(dve, "tensor_copy", sin2[:, :, 1, :], rp_f[:, :, 0:32], reads=[Rrpf, Rsin2], writes=[Rsin2])

        def run_gens(gens):
            gens = list(gens)
            while gens:
                for g in list(gens):
                    try:
                        next(g)
                    except StopIteration:
                        gens.remove(g)

        def mixer(l, st):
            g_t, Rg = load_gbc(l, 3)
            prenorm_T(l, 2)
            Sf_t, RSf = Sf[l]

            def gen_Z(s):
                F = fsets[s % 2]
                sl_cur = s + 1
                cbs = [(0, 512), (512, 512), (1024, 512), (1536, 256), (1792, 512)]
                for ci, (c0, w) in enumerate(cbs):
                    b = ci % 2
                    for kc in range(8):
                        op(pe, "matmul", psum[:, b, 0:w], xT[:, kc, s * 128:(s + 1) * 128], win_t[:, kc, c0:c0 + w],
                           start=(kc == 0), stop=(kc == 7), reads=[RxT[s], Rwin], writes=[PB[b]], inc=(kc == 7))
                    if ci == 0:
                        op(act, "activation", F["hq"][0][:], psum[:, b, 0:256], AF.Copy, reads=[PB[b]], writes=[F["hq"][1]])
                        op(act, "activation", F["sgf"][0][:], psum[:, b, 256:512], AF.Sigmoid, reads=[PB[b]], writes=[F["sgf"][1]])
                    elif ci == 1:
                        op(act, "activation", F["v"][0][:], psum[:, b, 0:256], AF.Copy, reads=[PB[b]], writes=[F["v"][1]])
                        op(act, "activation", F["sgate"][0][:], psum[:, b, 256:512], AF.Sigmoid, reads=[PB[b]], writes=[F["sgate"][1]])
                    elif ci == 2:
                        op(act, "activation", F["qk"][0][:, 0:8, :].rearrange("p a b -> p (a b)"), psum[:, b, :], AF.Copy,
                           reads=[PB[b]], writes=[F["qk"][1]])
                    elif ci == 3:
                        op(act, "activation", F["qk"][0][:, 8:10, :].rearrange("p a b -> p (a b)"), psum[:, b, 0:128], AF.Copy,
                           reads=[PB[b], F["qk"][1]], writes=[F["qk"][1]])
                        op(act, "activation", Vau[l][:, sl_cur, :, 0:64], psum[:, b, 128:256].rearrange("p (k d) -> p k d", k=2),
                           AF.Copy, reads=[PB[b]], writes=[RVau[l][sl_cur]])
                    else:
                        op(act, "activation", F["ge"][0][:], psum[:, b, :], AF.Copy, reads=[PB[b]], writes=[F["ge"][1]])
                    yield
                gx, Rgx = F["ge"]
                op(dve, "tensor_tensor", ge_t[:], gx[:], gx[:], ALU.mult, reads=[Rgx], writes=[Rget])
                op(dve, "tensor_scalar", ge_t[:], ge_t[:], 0.044715, 1.0, ALU.mult, ALU.add, reads=[Rget], writes=[Rget])
                op(dve, "tensor_tensor", ge_t[:], ge_t[:], gx[:], ALU.mult, reads=[Rget, Rgx], writes=[Rget])
                yield
                op(act, "activation", ge_t[:], ge_t[:], AF.Sigmoid, scale=GELU_C, reads=[Rget], writes=[Rget])
                yield
                op(dve, "tensor_tensor", gx[:], gx[:], ge_t[:], ALU.mult, reads=[Rgx, Rget], writes=[Rgx])
                op(pool, "tensor_tensor", F["sgate"][0][:].rearrange("p (a b) -> p a b", a=4), F["sgate"][0][:].rearrange("p (a b) -> p a b", a=4),
                   hgn[:, l, :].unsqueeze(1).to_broadcast([128, 4, 64]), ALU.mult, reads=[F["sgate"][1], Rhgn], writes=[F["sgate"][1]])
                yield

            def gen_A(s):
                F = fsets[s % 2]
                hq_sb, Rhq = F["hq"]
                sgf, Rsgf = F["sgf"]
                sgate, Rsgate = F["sgate"]
                v_bf, Rv = F["v"]
                mixed, Rmix = mixeds[s % 2]
                op(dve, "tensor_tensor", ff[:], sgf[:], oml[:, l, :], ALU.mult, reads=[Rsgf, Roml], writes=[Rff])
                op(dve, "tensor_tensor", ff[:], ff[:], lbm[:, l, :], ALU.add, reads=[Rff, Rlbm], writes=[Rff])
                yield
                op(act, "activation", ff[:], ff[:], AF.Ln, reads=[Rff], writes=[Rff])
                op(dve, "tensor_scalar", kk[:], sgf[:], -1.0, 1.0, ALU.mult, ALU.add, reads=[Rsgf], writes=[Rkk])
                op(dve, "tensor_tensor", kk[:], kk[:], oml[:, l, :], ALU.mult, reads=[Rkk, Roml], writes=[Rkk])
                yield
                op(dve, "tensor_copy", lhi[:], logf[:], reads=[Rlogf], writes=[Rlhi])
                op(dve, "tensor_tensor", llo[:], logf[:], lhi[:], ALU.subtract, reads=[Rlogf, Rlhi], writes=[Rllo])
                yield
                bb = 2
                op(pe, "matmul", psum[:, bb, 0:256], LTb[:], lhi[:], start=True, stop=False, reads=[RLT, Rlhi], writes=[PB[bb]], inc=False)
                op(pe, "matmul", psum[:, bb, 0:256], LTb[:], llo[:], start=False, stop=True, reads=[RLT, Rllo], writes=[PB[bb]], inc=True)
                be = 3
                for hh in range(4):
                    op(pe, "matmul", psum[0:64, be, hh * 4:(hh + 1) * 4], lhi[:, hh * 64:(hh + 1) * 64], selb[:], start=True, stop=False,
                       reads=[Rlhi, Rsel], writes=[PB[be]], inc=False)
                    op(pe, "matmul", psum[0:64, be, hh * 4:(hh + 1) * 4], llo[:, hh * 64:(hh + 1) * 64], selb[:], start=False, stop=True,
                       reads=[Rllo, Rsel], writes=[PB[be]], inc=(hh == 3))
                yield
                op(act, "activation", eb[:], psum[:, bb, 0:256], AF.Exp, reads=[PB[bb]], writes=[Reb])
                op(act, "activation", enb[:], psum[:, bb, 0:256], AF.Exp, scale=-1.0, reads=[PB[bb]], writes=[Renb])
                op(act, "activation", E_sb[:].rearrange("p a b -> p (a b)"), psum[0:64, be, 0:16], AF.Exp, reads=[PB[be]], writes=[RE])
                yield
                op(dve, "tensor_tensor", qt[:], hq_sb[:], eb[:], ALU.mult, reads=[Rhq, Reb], writes=[Rqt])
                op(dve, "tensor_tensor", ktb[:], kk[:], enb[:], ALU.mult, reads=[Rkk, Renb], writes=[Rktb])
                yield
                op(dve, "tensor_scalar", kt0[:], ktb[:], cst[:, C_RMASK:C_RMASK + 1], None, ALU.mult, reads=[Rktb, Rcst], writes=[Rkt0])
                op(dve, "tensor_scalar", kt1[:], ktb[:], cst[:, C_RMASK + 1:C_RMASK + 2], None, ALU.mult, reads=[Rktb, Rcst], writes=[Rkt1])
                bt = 2
                bT = psum[:, bt, :].bitcast(BF16)
                for hh in range(4):
                    op(pe, "transpose", bT[0:64, hh * 128:(hh + 1) * 128], qt[:, hh * 64:(hh + 1) * 64], identb[:],
                       reads=[Rqt, Rident], writes=[PB[bt]], inc=False)
                for hh in range(4):
                    op(pe, "transpose", bT[0:64, (4 + hh) * 128:(5 + hh) * 128], ktb[:, hh * 64:(hh + 1) * 64], identb[:],
                       reads=[Rktb, Rident], writes=[PB[bt]], inc=(hh == 3))
                yield
                op(act, "activation", qkT[:].rearrange("p a b -> p (a b)"), bT[0:64, :], AF.Copy, reads=[PB[bt]], writes=[RqkT])
                yield
                op(pool, "tensor_copy", qT0[:, :, 0:64], qkT[:, 0:4, 0:64], reads=[RqkT], writes=[RqT0])
                op(pool, "tensor_copy", qT1[:, :, 64:128], qkT[:, 0:4, 64:128], reads=[RqkT], writes=[RqT1])
                ba = 3
                for hh in range(4):
                    op(pe, "matmul", psum[:, ba, hh * 128:(hh + 1) * 128], qkT[:, 4 + hh, :], qkT[:, hh, :], start=True, stop=True,
                       reads=[RqkT], writes=[PB[ba]], inc=(hh == 3))
                bu = 2
                for c in range(2):
                    ktc, Rktc = (kt0, Rkt0) if c == 0 else (kt1, Rkt1)
                    for hh in range(4):
                        o0 = (c * 4 + hh) * 64
                        op(pe, "matmul", psum[0:64, bu, o0:o0 + 64], ktc[:, hh * 64:(hh + 1) * 64], v_bf[:, hh * 64:(hh + 1) * 64],
                           start=True, stop=True, reads=[Rktc, Rv], writes=[PB[bu]], inc=(c == 1 and hh == 3))
                yield
                op(dve, "tensor_tensor", attT[:], psum[:, ba, :].rearrange("p (a b) -> p a b", a=4),
                   cst[:, C_MHG:C_MHG + 128].unsqueeze(1).to_broadcast([128, 4, 128]), ALU.mult,
                   reads=[PB[ba], Rcst], writes=[RattT])
                for c in range(2):
                    op(dve, "tensor_tensor", UE[:, c, :, :], psum[0:64, bu, c * 256:(c + 1) * 256].rearrange("p (a b) -> p a b", a=4),
                       E_sb[:, :, 2 * c + 1:2 * c + 2].to_broadcast([64, 4, 64]), ALU.mult, reads=[PB[bu], RE], writes=[RUE])
                yield
                bo = 3
                for hh in range(4):
                    op(pe, "matmul", psum[:, bo, hh * 64:(hh + 1) * 64], attT[:, hh, :], v_bf[:, hh * 64:(hh + 1) * 64], start=True, stop=True,
                       reads=[RattT, Rv], writes=[PB[bo]], inc=(hh == 3))
                for c in range(2):
                    op(dve, "tensor_tensor", Smf[:], Sf_t[:], E_sb[:, :, 2 * c:2 * c + 1].to_broadcast([64, 4, 64]), ALU.mult,
                       reads=[RSf, RE], writes=[RSmf])
                    op(act, "activation", Smb[:, c, :, :], Smf[:], AF.Copy, reads=[RSmf], writes=[RSmb[c]])
                    op(dve, "tensor_tensor", St1[:], Smf[:], E_sb[:, :, 2 * c + 1:2 * c + 2].to_broadcast([64, 4, 64]), ALU.mult,
                       reads=[RSmf, RE], writes=[RSt1])
                    op(dve, "tensor_tensor", Sf_t[:], St1[:], UE[:, c, :, :], ALU.add, reads=[RSt1, RUE], writes=[RSf])
                    yield
                bi = 2
                for hh in range(4):
                    for c in range(2):
                        qTc, RqTc = (qT0, RqT0) if c == 0 else (qT1, RqT1)
                        op(pe, "matmul", psum[:, bi, hh * 64:(hh + 1) * 64], qTc[:, hh, :], Smb[:, c, hh, :], start=(c == 0), stop=(c == 1),
                           reads=[RqTc, RSmb[c]], writes=[PB[bi]], inc=(hh == 3 and c == 1))
                op(act, "activation", o_sb[:], psum[:, bo, 0:256], AF.Copy, reads=[PB[bo]], writes=[Ro])
                yield
                op(dve, "tensor_tensor", o_sb[:], o_sb[:], psum[:, bi, 0:256], ALU.add, reads=[Ro, PB[bi]], writes=[Ro])
                op(dve, "tensor_tensor", o_sq[:], o_sb[:], o_sb[:], ALU.mult, reads=[Ro], writes=[Rosq])
                ss_t, Rss = ss_r.next()
                op(dve, "tensor_reduce", ss_t[:, 0:4], o_sq[:].rearrange("p (a b) -> p a b", a=4), AX.X, ALU.add, reads=[Rosq], writes=[Rss])
                rs_t, Rrs = emit_rstd(ss_t, Rss, 4, 64, EPS)
                yield
                op(dve, "tensor_tensor", o_sq[:].rearrange("p (a b) -> p a b", a=4), o_sb[:].rearrange("p (a b) -> p a b", a=4),
                   rs_t[:, 0:4].unsqueeze(2).to_broadcast([128, 4, 64]), ALU.mult, reads=[Ro, Rrs, Rosq], writes=[Rosq])
                op(dve, "tensor_tensor", mixed[:, 0:256], o_sq[:], sgate[:], ALU.mult, reads=[Rosq, Rsgate], writes=[Rmix[0]])
                yield

            def gen_B(s):
                F = fsets[s % 2]
                qk_sb, Rqk = F["qk"]
                mixed, Rmix = mixeds[s % 2]
                nblk = st * NSUB + s
                sl_cur, sl_prev = s + 1, s
                qk3 = qk_sb[:].rearrange("p h (t d) -> p h t d", t=2)
                op(dve, "tensor_tensor", rsw[:].rearrange("p h (t d) -> p h t d", t=2)[:, :, 0, :], qk3[:, :, 1, :],
                   sin2[:, s, 0, :].unsqueeze(1).to_broadcast([128, 10, 32]), ALU.mult, reads=[Rqk, Rsin2], writes=[Rrsw])
                op(dve, "tensor_tensor", rsw[:].rearrange("p h (t d) -> p h t d", t=2)[:, :, 1, :], qk3[:, :, 0, :],
                   sin2[:, s, 1, :].unsqueeze(1).to_broadcast([128, 10, 32]), ALU.mult, reads=[Rqk, Rsin2, Rrsw], writes=[Rrsw])
                yield
                op(dve, "tensor_tensor", rtmp[:], qk_sb[:], cos2[:, s, :].unsqueeze(1).to_broadcast([128, 10, 64]), ALU.mult,
                   reads=[Rqk, Rcos2], writes=[Rrtmp])
                op(dve, "tensor_tensor", qk_r[:], rtmp[:], rsw[:], ALU.add, reads=[Rrtmp, Rrsw], writes=[Rqkr])
                yield
                bq = 4
                bqT = psum[:, bq, :].bitcast(BF16)
                for hh in range(8):
                    op(pe, "transpose", bqT[0:64, hh * 128:(hh + 1) * 128], qk_r[:, hh, :], identb[:], reads=[Rqkr, Rident],
                       writes=[PB[bq]], inc=(hh == 7))
                bk = 5
                bkT = psum[:, bk, :].bitcast(BF16)
                for kv in range(2):
                    op(pe, "transpose", bkT[0:64, kv * 128:(kv + 1) * 128], qk_r[:, 8 + kv, :], identb[:], reads=[Rqkr, Rident],
                       writes=[PB[bk]], inc=(kv == 1))
                yield
                op(act, "activation", qT_sb[:].rearrange("p a b -> p (a b)"), bqT[0:64, :], AF.Copy, reads=[PB[bq]], writes=[RqTs])
                op(act, "activation", kTb[l][:, :, sl_cur, :], bkT[0:64, 0:256].rearrange("p (a b) -> p a b", a=2), AF.Copy,
                   reads=[PB[bk]], writes=[RkTb[l][sl_cur]])
                yield
                slots = [(sl_cur, PTc, RPTc, mcurb, Rmcur)]
                if nblk > 0:
                    slots.append((sl_prev, PTp, RPTp, mprevb, Rmprev))
                for kv in range(2):
                    bss = []
                    for si_, (sl, PT, RPT, mk, Rmk) in enumerate(slots):
                        bs = 4 + si_
                        bss.append(bs)
                        op(pe, "matmul", psum[:, bs, :], kTb[l][:, kv, sl, :], qT_sb[:, 4 * kv:4 * kv + 4, :].rearrange("p a b -> p (a b)"),
                           start=True, stop=True, reads=[RkTb[l][sl], RqTs], writes=[PB[bs]], inc=True)
                    yield
                    for bs, (sl, PT, RPT, mk, Rmk) in zip(bss, slots):
                        op(act, "activation", PT[:, kv, :], psum[:, bs, :], AF.Exp, scale=0.125, reads=[PB[bs]], writes=[RPT[kv]])
                    yield
                    for bs, (sl, PT, RPT, mk, Rmk) in zip(bss, slots):
                        op(dve, "tensor_tensor", PT[:, kv, :].rearrange("p (a b) -> p a b", a=4), PT[:, kv, :].rearrange("p (a b) -> p a b", a=4),
                           mk[:].unsqueeze(1).to_broadcast([128, 4, 128]), ALU.mult, reads=[RPT[kv], Rmk], writes=[RPT[kv]])
                    yield
                for kv in range(2):
                    bp = 4 + kv
                    for g in range(4):
                        for si, (sl, PT, RPT, mk, Rmk) in enumerate(slots):
                            op(pe, "matmul", psum[:, bp, g * 65:(g + 1) * 65], PT[:, kv, g * 128:(g + 1) * 128], Vau[l][:, sl, kv, :],
                               start=(si == 0), stop=(si == len(slots) - 1), reads=[RPT[kv], RVau[l][sl]], writes=[PB[bp]],
                               inc=(g == 3 and si == len(slots) - 1))
                    yield
                    pv3 = psum[:, bp, 0:260].rearrange("p (a b) -> p a b", a=4)
                    op(dve, "tensor_tensor", den[:, 4 * kv:4 * kv + 4], pv3[:, :, 64], esk[:, l, 4 * kv:4 * kv + 4], ALU.add,
                       reads=[PB[bp], Resk], writes=[Rden])
                    op(dve, "reciprocal", rden[:, 4 * kv:4 * kv + 4], den[:, 4 * kv:4 * kv + 4], reads=[Rden], writes=[Rrden])
                    op(dve, "tensor_tensor", mixed[:, 256 + 256 * kv:512 + 256 * kv].rearrange("p (a b) -> p a b", a=4), pv3[:, :, 0:64],
                       rden[:, 4 * kv:4 * kv + 4].unsqueeze(2).to_broadcast([128, 4, 64]), ALU.mult,
                       reads=[PB[bp], Rrden], writes=[Rmix[1]])
                    yield
                F = fsets[s % 2]
                ge_x, Rgex = F["ge"]
                mixed, Rmix = mixeds[s % 2]
                op(dve, "bn_stats", st6[:], ge_x[:, 256:512], reads=[Rgex], writes=[Rst6])
                op(dve, "bn_aggr", mv[:], st6[:], reads=[Rst6], writes=[Rmv])
                ms_t, Rms = ms_r.next()
                rs2_t, Rrs2 = rstd_r.next()
                op(pool, "tensor_scalar", ms_t[:, 0:1], mv[:, 1:2], 1.0, EPS, ALU.mult, ALU.add, reads=[Rmv], writes=[Rms])
                op(pool, "tensor_tensor", rs2_t[:, 0:1], ms_t[:, 0:1], neghalf[:, 0:1], ALU.pow, reads=[Rms, Rneg], writes=[Rrs2])
                yield
                op(dve, "tensor_scalar", vn[:], ge_x[:, 256:512], mv[:, 0:1], rs2_t[:, 0:1], ALU.subtract, ALU.mult,
                   reads=[Rgex, Rmv, Rrs2], writes=[Rvn])
                op(dve, "tensor_tensor", vnb[:], vn[:], lng[:, l, :], ALU.mult, reads=[Rvn, Rlng], writes=[Rvnb])
                yield
                bm = 4
                for g in range(4):
                    op(pe, "matmul", psum[:, bm, g * 64:(g + 1) * 64], sgwT[:, l * 4 + g, :], vnb[:, g * 64:(g + 1) * 64], start=True, stop=True,
                       reads=[RsgwT, Rvnb], writes=[PB[bm]], inc=(g == 3))
                yield
                op(dve, "tensor_tensor", vn[:].rearrange("p (a b) -> p a b", a=4), psum[:, bm, 0:256].rearrange("p (a b) -> p a b", a=4),
                   sgb[:, l, :].unsqueeze(2).to_broadcast([128, 4, 64]), ALU.add, reads=[PB[bm], Rsgb, Rvn], writes=[Rvn])
                op(dve, "tensor_tensor", mixed[:, 768:1024], vn[:], ge_x[:, 0:256], ALU.mult, reads=[Rvn, Rgex], writes=[Rmix[2]])
                yield

            def gen_tail(s):
                mixed, Rmix = mixeds[s % 2]
                bx = 6
                bxT = psum[:, bx, :].bitcast(BF16)
                for kc in range(8):
                    op(pe, "transpose", bxT[:, kc * 128:(kc + 1) * 128], mixed[:, kc * 128:(kc + 1) * 128], identb[:],
                       reads=Rmix + [Rident], writes=[PB[bx]], inc=(kc == 7))
                yield
                op(act, "activation", mT[:, :, s * 128:(s + 1) * 128], bxT.rearrange("p (c t) -> p c t", c=8), AF.Copy,
                   reads=[PB[bx]], writes=[RmT[s]])
                yield
                b2 = 6
                for hh in range(2):
                    for kc in range(8):
                        op(pe, "matmul", psum[:, b2 + hh, :], mT[:, kc, s * 128:(s + 1) * 128], wout_v[:, kc, hh * 512:(hh + 1) * 512],
                           start=(kc == 0), stop=(kc == 7), reads=[RmT[s]] + RaT, writes=[PB[b2 + hh]], inc=(kc == 7))
                    yield
                post_norm(psum[:, b2:b2 + 2, :], [PB[b2], PB[b2 + 1]], g_t, Rg, 1.0, s)
                yield

            run_gens([gen_Z(0)])
            for i in range(NSUB):
                gens = []
                if i + 1 < NSUB:
                    gens.append(gen_Z(i + 1))
                gens += [gen_A(i), gen_B(i)]
                if i >= 1:
                    gens.append(gen_tail(i - 1))
                run_gens(gens)
            run_gens([gen_tail(NSUB - 1)])
            op(pool, "tensor_copy", kTb[l][:, :, 0, :], kTb[l][:, :, 4, :], reads=[RkTb[l][4]], writes=[RkTb[l][0]])
            op(pool, "tensor_copy", Vau[l][:, 0, :, :], Vau[l][:, 4, :, :], reads=[RVau[l][4]], writes=[RVau[l][0]])

        def ple(l, st):
            g_t, Rg = load_gbc(l, 7)
            prenorm_T(l, 6)
            for s in range(NSUB):
                b2 = nb2()
                for hh in range(2):
                    for kc in range(8):
                        op(pe, "matmul", psum[:, b2 + hh, :], xT[:, kc, s * 128:(s + 1) * 128], wgate_v[:, kc, hh * 512:(hh + 1) * 512],
                           start=(kc == 0), stop=(kc == 7), reads=[RxT[s], Rwin], writes=[PB[b2 + hh]], inc=(kc == 7))
                ga_t, Rga = gate_r.next()
                op(act, "activation", ga_t[:].rearrange("p (a b) -> p a b", a=2), psum[:, b2:b2 + 2, :], AF.Sigmoid,
                   reads=[PB[b2], PB[b2 + 1]], writes=[Rga])
                ps_t, Rps = psb.next()
                dma(sp, ps_t[:], p_d[l, st * G + s * 128:st * G + (s + 1) * 128, :], writes=[Rps], key="ldx")
                op(dve, "tensor_copy", p_bf[:], ps_t[:], reads=[Rps], writes=[Rpbf])
                bt = nb()
                bT = psum[:, bt, :].bitcast(BF16)
                for kc in range(2):
                    op(pe, "transpose", bT[:, kc * 128:(kc + 1) * 128], p_bf[:, kc * 128:(kc + 1) * 128], identb[:],
                       reads=[Rpbf, Rident], writes=[PB[bt]], inc=(kc == 1))
                op(act, "activation", pT[:].rearrange("p a b -> p (a b)"), bT[:, 0:256], AF.Copy, reads=[PB[bt]], writes=[RpT])
                b3 = nb2()
                for hh in range(2):
                    for kc in range(2):
                        op(pe, "matmul", psum[:, b3 + hh, :], pT[:, kc, :], wple_v[:, kc, hh * 512:(hh + 1) * 512],
                           start=(kc == 0), stop=(kc == 1), reads=[RpT, Rwin], writes=[PB[b3 + hh]], inc=(kc == 1))
                op(dve, "tensor_tensor", ga_t[:].rearrange("p (a b) -> p a b", a=2), psum[:, b3:b3 + 2, :],
                   ga_t[:].rearrange("p (a b) -> p a b", a=2), ALU.mult, reads=[PB[b3], PB[b3 + 1], Rga], writes=[Rga])
                post_norm(ga_t[:].rearrange("p (a b) -> p a b", a=2), [Rga], g_t, Rg, 1.0, s)

        phase_ctr = 0
        for st in range(n_st):
            t0 = st * G
            for s_ in range(NSUB):
                dma(sp, h[:, s_, :], x_d[t0 + s_ * 128:t0 + (s_ + 1) * 128, :], writes=[Rh[s_]], key="ldx")
            rope_tables(st)
            for l in range(n_layers):
                np_ = n_phases if l == n_layers - 1 else 4
                if np_ >= 1:
                    ffn(l, 1)
                if np_ >= 2:
                    ensure_cast("wout", l)
                    dma(sp, wout_v, S["wout", l].rearrange("(c p) n -> p c n", p=128), reads=[RS["wout", l]], writes=RaT, key="wres")
                    mixer(l, st)
                if np_ >= 3:
                    ensure_cast("gate", l)
                    dma(sp, wgate_v, S["gate", l].rearrange("(c p) n -> p c n", p=128), reads=[RS["gate", l]], writes=[Rwin], key="wres")
                    ensure_cast("ple", l)
                    dma(sp, wple_v, S["ple", l].rearrange("(c p) n -> p c n", p=128), reads=[RS["ple", l]], writes=[Rwin], key="wres")
                    ffn(l, 2)
                if np_ >= 4:
                    ple(l, st)
            for s_ in range(NSUB):
                dma(sp, out_d[t0 + s_ * 128:t0 + (s_ + 1) * 128, :], h[:, s_, :], reads=[Rh[s_]], key="st", final=True)

        for ent in fw.st_sems:
            sp.q.append(("w", ent[0], ent[1]))
        with nc.Block() as block:
            fw.replay(block)
    return nc


_CACHE = {}


def kernel(x, p, positions, norm_gains, w_in, w_out, ffn1_gate_up, ffn1_down, ffn2_gate_up, ffn2_down,
           hgrn_lb_logits, hgrn_norm_gain, attn_sinks, sg_ln_gain, sg_spatial_w, sg_spatial_b, ple_proj, ple_gate,
           _n_st=8, _n_layers=2, _n_phases=4, _cores=NCORES):
    f32 = lambda a: np.ascontiguousarray(np.asarray(a), dtype=np.float32)
    x = f32(x)
    p = f32(p)
    positions = np.ascontiguousarray(np.asarray(positions), dtype=np.int32)
    shared = {
        "norm_gains": f32(norm_gains), "w_in": f32(w_in), "w_out": f32(w_out),
        "ffn1_gate_up": f32(ffn1_gate_up), "ffn1_down": f32(ffn1_down),
        "ffn2_gate_up": f32(ffn2_gate_up), "ffn2_down": f32(ffn2_down),
        "hgrn_lb_logits": f32(hgrn_lb_logits), "hgrn_norm_gain": f32(hgrn_norm_gain),
        "attn_sinks": f32(attn_sinks), "sg_ln_gain": f32(sg_ln_gain),
        "sg_spatial_w": f32(sg_spatial_w), "sg_spatial_b": f32(sg_spatial_b),
        "ple_proj": f32(ple_proj), "ple_gate": f32(ple_gate),
        "consts": host_consts(),
    }
    key = (_n_st, _n_layers, _n_phases)
    nc = build(*key)
    in_maps = []
    for c in range(_cores):
        m = dict(shared)
        m["x"] = np.ascontiguousarray(x[c])
        m["p"] = np.ascontiguousarray(p[:, c])
        m["positions"] = np.ascontiguousarray(positions[c])
        in_maps.append(m)
    res = run_bass_kernel_spmd(nc, in_maps, core_ids=list(range(_cores)))
    out = np.stack([np.asarray(r["out"], dtype=np.float32) for r in res.results], axis=0)
    return out
```
